# Optimizing a Trainium2 kernel written in Bass

```python
import math, functools
import jax, jax.numpy as jnp
from jax import lax
import numpy as np

D_MODEL = 1024
BATCH = 8
SEQ = 8192
DEPTH = 1
DEC_BATCH = 128
DEC_SEQ = 8
PAST_LEN = 8192
PAGE_SIZE = 128

HEAD_DIM = 64
MIX_WIDTH = D_MODEL
A_W = MIX_WIDTH // 2
B_W = MIX_WIDTH - A_W
H_ATTN = A_W // HEAD_DIM
H_RWKV = B_W // HEAD_DIM
BRANCHES = ((128, 1), (512, 4), (2048, 16))
MAX_WINDOW = 2048
MAX_DIL = 16
Q_BLOCK = 128
NUM_BUCKETS = 32
MAX_DISTANCE = 2048
LORA_W = 64
LORA_A = 64
LORA_G = 160
N_SHIFT = 3 * B_W + LORA_W + LORA_A + LORA_G
N_IN = 3 * A_W + N_SHIFT
D_FF = 4 * D_MODEL
RMS_EPS = 1e-6
GN_EPS = 64e-5
NEG_INF = -1e30

kernel_name = 'hybrid_dilated_attn_rwkv7_step'

f32 = jnp.float32


def rms_norm(x, g):
    xf = x.astype(f32)
    y = xf * lax.rsqrt(jnp.mean(xf * xf, -1, keepdims=True) + RMS_EPS) * g.astype(f32)
    return y.astype(x.dtype)


def t5_bucket(dist):
    dist = np.maximum(dist, 0)
    max_exact = NUM_BUCKETS // 2
    large = max_exact + (np.log(np.maximum(dist, 1) / max_exact)
                         / math.log(MAX_DISTANCE / max_exact)
                         * (NUM_BUCKETS - max_exact)).astype(np.int32)
    large = np.minimum(large, NUM_BUCKETS - 1)
    return np.where(dist < max_exact, dist, large).astype(np.int32)


def dilated_attention(q, k_ctx, v_ctx, q_start, bias_table, blk):
    B, Tq, H, E = q.shape
    pad = ((0, 0), (MAX_WINDOW, 0), (0, 0), (0, 0))
    kp = jnp.pad(k_ctx, pad)
    vp = jnp.pad(v_ctx, pad)
    scale = HEAD_DIM ** -0.5
    statics = []
    for w, d in BRANCHES:
        M, N = blk // d, (w + blk) // d
        m = np.arange(M)[:, None]
        n = np.arange(N)[None, :]
        band = jnp.asarray((n >= m) & (n <= m + w // d))
        bias = jnp.transpose(bias_table[t5_bucket(w + (m - n) * d)], (2, 0, 1)).astype(f32)
        statics.append((w, d, M, N, band, bias))

    def one_block(c0):
        qb = lax.dynamic_slice_in_dim(q, c0 - q_start, blk, axis=1)
        maxs, dens, outs = [], [], []
        for w, d, M, N, band, bias in statics:
            kr = lax.dynamic_slice_in_dim(kp, c0 + MAX_WINDOW - w, w + blk, axis=1).reshape(B, N, d, H, E)
            vr = lax.dynamic_slice_in_dim(vp, c0 + MAX_WINDOW - w, w + blk, axis=1).reshape(B, N, d, H, E)
            qr = qb.reshape(B, M, d, H, E)
            s = jnp.einsum('bmrhe,bnrhe->bhrmn', qr, kr).astype(f32) * scale + bias[None, :, None]
            valid = (c0 - w + jnp.arange(N)[None, :] * d + jnp.arange(d)[:, None]) >= 0
            s = jnp.where(band[None] & valid[:, None, :], s, NEG_INF)
            mx = jnp.max(s, -1)
            e = jnp.exp(s - mx[..., None])
            dn = jnp.sum(e, -1)
            o = jnp.einsum('bhrmn,bnrhe->bhrme', e, vr.astype(f32))
            maxs.append(mx.transpose(0, 3, 2, 1).reshape(B, blk, H))
            dens.append(dn.transpose(0, 3, 2, 1).reshape(B, blk, H))
            outs.append(o.transpose(0, 3, 2, 1, 4).reshape(B, blk, H, E))
        mall = maxs[0]
        for mx in maxs[1:]:
            mall = jnp.maximum(mall, mx)
        num = 0.0
        den = 0.0
        for mx, dn, o in zip(maxs, dens, outs):
            sc = jnp.exp(mx - mall)
            num = num + sc[..., None] * o
            den = den + sc * dn
        return num / den[..., None]

    c0s = q_start + blk * jnp.arange(Tq // blk, dtype=jnp.int32)
    out = lax.map(one_block, c0s)
    return out.transpose(1, 0, 2, 3, 4).reshape(B, Tq, H, E)


def wkv_scan(S0, r, w, k, v, kk, a):
    def step(S, inp):
        r_t, w_t, k_t, v_t, kk_t, a_t = inp
        sk = jnp.einsum('bhij,bhj->bhi', S, kk_t)
        S = (S * w_t[:, :, None, :] - sk[..., None] * (kk_t * a_t)[:, :, None, :]
             + v_t[..., None] * k_t[:, :, None, :])
        return S, jnp.einsum('bhij,bhj->bhi', S, r_t)
    xs = tuple(jnp.moveaxis(t, 1, 0) for t in (r, w, k, v, kk, a))
    S, y = lax.scan(step, S0, xs)
    return jnp.moveaxis(y, 0, 1), S


def rwkv7_time_mix(p, shift0, wkv0, mu_shift, w0, w_lora2, a0, a_lora2, g_lora2, k_k, k_a, r_k, lnx_g, lnx_b):
    B, T, _ = p.shape
    pf = p.astype(f32)
    prev = jnp.concatenate([shift0.astype(f32)[:, None], pf[:, :-1]], axis=1)
    xs = pf + (prev - pf) * mu_shift.astype(f32)
    r, k, v, xw, xa, xg = jnp.split(
        xs, (B_W, 2 * B_W, 3 * B_W, 3 * B_W + LORA_W, 3 * B_W + LORA_W + LORA_A), axis=-1)
    w_log = -jax.nn.softplus(-(w0.astype(f32) + jnp.tanh(xw) @ w_lora2.astype(f32))) - 0.5
    decay = jnp.exp(-jnp.exp(w_log))
    a = jax.nn.sigmoid(a0.astype(f32) + xa @ a_lora2.astype(f32))
    g = jax.nn.sigmoid(xg) @ g_lora2.astype(f32)
    heads = lambda t: t.reshape(B, T, H_RWKV, HEAD_DIM)
    kk = heads(k * k_k.astype(f32))
    kk = kk / jnp.maximum(jnp.sqrt(jnp.sum(kk * kk, -1, keepdims=True)), 1e-12)
    k = k * (1.0 + (a - 1.0) * k_a.astype(f32))
    r, k, v, decay, a = heads(r), heads(k), heads(v), heads(decay), heads(a)
    y, S = wkv_scan(wkv0.astype(f32), r, decay, k, v, kk, a)
    mean = jnp.mean(y, -1, keepdims=True)
    var = jnp.mean(jnp.square(y - mean), -1, keepdims=True)
    y = ((y - mean) * lax.rsqrt(var + GN_EPS)).reshape(B, T, B_W) * lnx_g.astype(f32) + lnx_b.astype(f32)
    bonus = jnp.sum(r * k * r_k.astype(f32), -1, keepdims=True) * v
    y = (y + bonus.reshape(B, T, B_W)) * g
    return y, S, p[:, -1]


def head_rms(x, g):
    xf = x.astype(f32)
    return (xf * lax.rsqrt(jnp.mean(xf * xf, -1, keepdims=True) + RMS_EPS) * g.astype(f32)).astype(x.dtype)


def decoder_layer(x, k_past, v_past, wkv0, shift0, bias_table, ln1_g, w_in, q_norm_g, k_norm_g,
                  mu_shift, w0, w_lora2, a0, a_lora2, g_lora2, k_k, k_a, r_k, lnx_g, lnx_b,
                  w_out, ln2_g, w_mlp1, w_mlp2):
    B, T, _ = x.shape
    n = rms_norm(x, ln1_g)
    proj = n @ w_in
    q = head_rms(proj[..., :A_W].reshape(B, T, H_ATTN, HEAD_DIM), q_norm_g)
    k = head_rms(proj[..., A_W:2 * A_W].reshape(B, T, H_ATTN, HEAD_DIM), k_norm_g)
    v = proj[..., 2 * A_W:3 * A_W].reshape(B, T, H_ATTN, HEAD_DIM)
    blk = min(Q_BLOCK, -(-T // MAX_DIL) * MAX_DIL)
    t_pad = -(-T // blk) * blk
    tp = ((0, 0), (0, t_pad - T), (0, 0), (0, 0))
    k_ctx = jnp.concatenate([k_past.astype(k.dtype), jnp.pad(k, tp)], axis=1)
    v_ctx = jnp.concatenate([v_past.astype(v.dtype), jnp.pad(v, tp)], axis=1)
    att = dilated_attention(jnp.pad(q, tp), k_ctx, v_ctx, k_past.shape[1], bias_table, blk)[:, :T]
    rw, wkv_T, shift_T = rwkv7_time_mix(proj[..., 3 * A_W:], shift0, wkv0, mu_shift, w0, w_lora2,
                                        a0, a_lora2, g_lora2, k_k, k_a, r_k, lnx_g, lnx_b)
    mixed = jnp.concatenate([att.reshape(B, T, A_W).astype(x.dtype), rw.astype(x.dtype)], axis=-1)
    h = x + mixed @ w_out
    m = rms_norm(h, ln2_g)
    y = h + jnp.square(jax.nn.relu(m @ w_mlp1)) @ w_mlp2
    return y, k, v, wkv_T, shift_T


def setup_inputs(seed: int = 0) -> dict:
    key = jax.random.key(seed)
    ks = jax.random.split(key, 32)
    L = DEPTH
    lw = min(MAX_WINDOW, PAST_LEN)
    nrm = lambda kk, shape, s: jax.random.normal(kk, shape, f32) * s
    return {
        'x_prompt': nrm(ks[0], (BATCH, SEQ, D_MODEL), 1.0),
        'x_sample': nrm(ks[1], (DEC_BATCH, DEC_SEQ, D_MODEL), 1.0),
        'cache_k_win': nrm(ks[2], (L, DEC_BATCH, lw, H_ATTN, HEAD_DIM), 1.0),
        'cache_v_win': nrm(ks[3], (L, DEC_BATCH, lw, H_ATTN, HEAD_DIM), 1.0),
        'state_wkv': nrm(ks[4], (L, DEC_BATCH, H_RWKV, HEAD_DIM, HEAD_DIM), 0.3),
        'state_shift': nrm(ks[5], (L, DEC_BATCH, N_SHIFT), 1.0),
        'bias_table': nrm(ks[6], (NUM_BUCKETS, H_ATTN), 0.5),
        'ln1_g': 1.0 + nrm(ks[7], (L, D_MODEL), 0.02),
        'w_in': nrm(ks[8], (L, D_MODEL, N_IN), D_MODEL ** -0.5),
        'q_norm_g': 1.0 + nrm(ks[9], (L, HEAD_DIM), 0.02),
        'k_norm_g': 1.0 + nrm(ks[10], (L, HEAD_DIM), 0.02),
        'mu_shift': jax.random.uniform(ks[11], (L, N_SHIFT), f32),
        'w0': jax.random.uniform(ks[12], (L, B_W), f32, -6.0, 1.0),
        'w_lora2': nrm(ks[13], (L, LORA_W, B_W), 0.1),
        'a0': nrm(ks[14], (L, B_W), 0.5),
        'a_lora2': nrm(ks[15], (L, LORA_A, B_W), 0.1),
        'g_lora2': nrm(ks[16], (L, LORA_G, B_W), LORA_G ** -0.5),
        'k_k': 0.85 + nrm(ks[17], (L, B_W), 0.05),
        'k_a': 1.0 + nrm(ks[18], (L, B_W), 0.05),
        'r_k': nrm(ks[19], (L, H_RWKV, HEAD_DIM), 0.1),
        'lnx_g': 1.0 + nrm(ks[20], (L, B_W), 0.02),
        'lnx_b': nrm(ks[21], (L, B_W), 0.02),
        'w_out': nrm(ks[22], (L, D_MODEL, D_MODEL), D_MODEL ** -0.5),
        'ln2_g': 1.0 + nrm(ks[23], (L, D_MODEL), 0.02),
        'w_mlp1': nrm(ks[24], (L, D_MODEL, D_FF), D_MODEL ** -0.5),
        'w_mlp2': nrm(ks[25], (L, D_FF, D_MODEL), D_FF ** -0.5),
    }


def reference(x_prompt, x_sample, cache_k_win, cache_v_win, state_wkv, state_shift, bias_table,
              ln1_g, w_in, q_norm_g, k_norm_g, mu_shift, w0, w_lora2, a0, a_lora2, g_lora2,
              k_k, k_a, r_k, lnx_g, lnx_b, w_out, ln2_g, w_mlp1, w_mlp2):
    yp, ys = x_prompt, x_sample
    Bp, Tp, _ = x_prompt.shape
    keep = min(MAX_WINDOW, Tp)
    kp_l, vp_l, sp_l, hp_l = [], [], [], []
    ks_l, vs_l, ss_l, hs_l = [], [], [], []
    for l in range(DEPTH):
        lw = (ln1_g[l], w_in[l], q_norm_g[l], k_norm_g[l], mu_shift[l], w0[l], w_lora2[l], a0[l],
              a_lora2[l], g_lora2[l], k_k[l], k_a[l], r_k[l], lnx_g[l], lnx_b[l], w_out[l], ln2_g[l],
              w_mlp1[l], w_mlp2[l])
        empty = jnp.zeros((Bp, 0, H_ATTN, HEAD_DIM), yp.dtype)
        yp, k_p, v_p, s_p, h_p = decoder_layer(
            yp, empty, empty, jnp.zeros((Bp, H_RWKV, HEAD_DIM, HEAD_DIM), f32),
            jnp.zeros((Bp, N_SHIFT), yp.dtype), bias_table, *lw)
        kp_l.append(k_p[:, Tp - keep:])
        vp_l.append(v_p[:, Tp - keep:])
        sp_l.append(s_p)
        hp_l.append(h_p)
        ys, k_s, v_s, s_s, h_s = decoder_layer(
            ys, cache_k_win[l], cache_v_win[l], state_wkv[l], state_shift[l], bias_table, *lw)
        ks_l.append(k_s)
        vs_l.append(v_s)
        ss_l.append(s_s)
        hs_l.append(h_s)
    return (yp, ys, jnp.stack(kp_l), jnp.stack(vp_l), jnp.stack(sp_l), jnp.stack(hp_l),
            jnp.stack(ks_l), jnp.stack(vs_l), jnp.stack(ss_l), jnp.stack(hs_l))
```

```python
import contextlib
import math
import numpy as np
import ml_dtypes
import concourse.bass as bass
import concourse.mybir as mybir
from concourse.bass_utils import run_bass_kernel_spmd

F32 = mybir.dt.float32
BF16 = mybir.dt.bfloat16
AF = mybir.ActivationFunctionType
ALU = mybir.AluOpType
AX = mybir.AxisListType

D = 1024
NIN = 3360
NSH = 1824
DFF = 4096
RMS_EPS = 1e-6
GN_EPS = 64e-5
BRANCHES = ((128, 1), (512, 4), (2048, 16))


class Buf:
    __slots__ = ("name", "w", "r")

    def __init__(self, name=""):
        self.name = name
        self.w = None
        self.r = {}


class Prog:
    ENGS = ("pe", "act", "dve", "pool", "sp")

    def __init__(self, nc):
        self.nc = nc
        self.ops = {e: [] for e in self.ENGS}
        self.count = {e: 0 for e in ("pe", "act", "dve", "pool")}
        self.known = {e: {} for e in self.ENGS}
        n_dma_sems = {"sp": 10, "pool": 4, "act": 2}
        self.dma_pool = {q: [f"d_{q}{i}" for i in range(n)] for q, n in n_dma_sems.items()}
        self.dma_tot = {k: 0 for q in self.dma_pool for k in self.dma_pool[q]}
        self.dma_rr = {q: 0 for q in self.dma_pool}
        self.semnames = ["c_pe", "c_act", "c_dve", "c_pool"] + [k for q in self.dma_pool for k in self.dma_pool[q]]
        self.out_tokens = []

    def _need(self, eng, tok):
        if tok is None:
            return
        sem, val = tok
        if eng == "pe" and sem == "c_pe":
            return
        if self.known[eng].get(sem, 0) >= val:
            return
        self.known[eng][sem] = val
        self.ops[eng].append(("wait", sem, val))

    def _deps(self, eng, r, w):
        for b in r:
            self._need(eng, b.w)
        for b in w:
            self._need(eng, b.w)
            for sem, val in b.r.items():
                self._need(eng, (sem, val))

    def _commit(self, tok, r, w):
        sem, val = tok
        for b in r:
            if b.r.get(sem, 0) < val:
                b.r[sem] = val
        for b in w:
            b.w = tok
            b.r = {}

    def op(self, eng, fn, r=(), w=()):
        self._deps(eng, r, w)
        self.count[eng] += 1
        tok = ("c_" + eng, self.count[eng])
        self.ops[eng].append(("op", fn))
        self._commit(tok, r, w)
        return tok

    def mm(self, fn, r=(), w=(), last=True):
        eng = "pe"
        self._deps(eng, r, w)
        if last:
            self.count[eng] += 1
            tok = ("c_pe", self.count[eng])
            self.ops[eng].append(("op", fn))
        else:
            tok = ("c_pe", self.count[eng] + 1)
            self.ops[eng].append(("opq", fn))
        self._commit(tok, r, w)
        return tok

    def dma(self, q, fn, r=(), w=(), is_output=False):
        self._deps(q, r, w)
        pool = self.dma_pool[q]
        i = self.dma_rr[q]
        self.dma_rr[q] = (i + 1) % len(pool)
        sem = pool[i]
        if self.dma_tot[sem] > 0:
            self._need(q, (sem, self.dma_tot[sem]))
        self.dma_tot[sem] += 16
        tok = (sem, self.dma_tot[sem])
        self.ops[q].append(("dma", fn, sem))
        self._commit(tok, r, w)
        if is_output:
            self.out_tokens.append(tok)
        return tok

    def open(self, stack):
        self.sems = {n: stack.enter_context(self.nc.semaphore(n)) for n in self.semnames}

    def emit_phase(self):
        nc = self.nc
        sems = self.sems
        ops = self.ops
        self.ops = {e: [] for e in self.ENGS}
        with nc.Block() as block:
            def run(engname, handle):
                for item in ops[engname]:
                    k = item[0]
                    if k == "wait":
                        handle.wait_ge(sems[item[1]], item[2])
                    elif k == "op":
                        item[1](handle).then_inc(sems["c_" + engname], 1)
                    elif k == "opq":
                        item[1](handle)
                    elif k == "dma":
                        item[1](handle).then_inc(sems[item[2]], 16)

            @block.tensor
            def _(e):
                run("pe", e)

            @block.scalar
            def _(e):
                run("act", e)

            @block.vector
            def _(e):
                run("dve", e)

            @block.gpsimd
            def _(e):
                run("pool", e)

            @block.sync
            def _(e):
                run("sp", e)


def t5_bucket(dist):
    dist = np.maximum(dist, 0)
    max_exact = 16
    large = max_exact + (np.log(np.maximum(dist, 1) / max_exact) / math.log(2048 / max_exact) * 16).astype(np.int32)
    large = np.minimum(large, 31)
    return np.where(dist < max_exact, dist, large).astype(np.int32)


C_ID, C_SHIFT, C_ELAST, C_TRI, C_LMID, C_ONES, C_MUS, C_MUI, C_MLS, C_J, C_SHIFTS, C_TRIS, C_ONESS, NCONST = range(14)


def host_consts():
    c = np.zeros((NCONST, 128, 128), np.float32)
    s = np.arange(128)[:, None]
    t = np.arange(128)[None, :]
    c[C_ID] = (s == t)
    c[C_SHIFT] = (s == t - 1)
    c[C_ELAST] = (s == 127) & (t == 0)
    c[C_TRI] = (s <= t)
    c[C_LMID] = (s <= 63) & (t >= 0)
    c[C_ONES] = 1.0
    c[C_MUS] = (s < t)
    c[C_MUI] = (s <= t)
    c[C_MLS] = (s > t)
    c[C_J] = (s == 127 - t)
    c[C_SHIFTS] = (s == t - 1) & (t % 8 != 0)
    c[C_TRIS] = (s <= t) & (s // 8 == t // 8)
    c[C_ONESS] = (s // 8 == t // 8)
    cst = np.ascontiguousarray(c.transpose(1, 0, 2))
    sel = np.zeros((16, 128), np.float32)
    sel[np.arange(16), np.arange(16) * 8] = 1.0
    oh = np.zeros((32, 3 * 129), np.float32)
    for bi, (w, d) in enumerate(BRANCHES):
        jj = np.arange(129)
        oh[t5_bucket(jj * d), bi * 129 + jj] = 1.0
    ohs = np.zeros((32, 2304), np.float32)
    for x in range(2056):
        dist = 2055 - x
        for bi, (w, d) in enumerate(BRANCHES):
            if dist % d == 0 and dist <= w:
                ohs[int(t5_bucket(np.array(dist))), x] += 1.0
    return {"cst": cst, "sel": sel, "oh": oh, "ohs": ohs}


class KB:
    def __init__(self, TP=8192, debug=False, scratch_in=()):
        self.scratch_in = set(scratch_in)
        self.TP = TP
        self.NTP = TP // 128
        self.NT = self.NTP + 1
        self.TA = self.NT * 128
        self.debug = debug
        self.nc = bass.Bass("TRN2", target_bir_lowering=False)
        self.P = Prog(self.nc)
        self.declare()

    def declare(self):
        nc, TP, TA = self.nc, self.TP, self.TA
        di = lambda n, s, dt=F32: nc.dram_tensor(n, list(s), dt, kind="ExternalInput")
        do = lambda n, s, dt=F32: nc.dram_tensor(n, list(s), dt, kind="ExternalOutput")
        ds = lambda n, s, dt=F32: nc.dram_tensor(n, list(s), dt, kind=("ExternalInput" if n in self.scratch_in else
                                                                       "ExternalOutput" if self.debug else "Internal"))
        self.i = dict(
            xp=di("xp", [TP, D]), xs=di("xs", [128, D]),
            ck=di("ck", [16 * 2048, 512]), cv=di("cv", [16 * 2048, 512]),
            swkv=di("swkv", [128, 4096]), sshift=di("sshift", [16, NSH]),
            bias_table=di("bias_table", [32, 8]), ln1_g=di("ln1_g", [1, D]), w_in=di("w_in", [D, NIN]),
            q_norm_g=di("q_norm_g", [1, 64]), k_norm_g=di("k_norm_g", [1, 64]), mu_shift=di("mu_shift", [1, NSH]),
            w0=di("w0", [1, 512]), w_lora2=di("w_lora2", [64, 512]), a0=di("a0", [1, 512]),
            a_lora2=di("a_lora2", [64, 512]), g_lora2=di("g_lora2", [160, 512]), k_k=di("k_k", [1, 512]),
            k_a=di("k_a", [1, 512]), r_k=di("r_k", [1, 512]), lnx_g=di("lnx_g", [1, 512]), lnx_b=di("lnx_b", [1, 512]),
            w_out=di("w_out", [D, D]), ln2_g=di("ln2_g", [1, D]), w_mlp1=di("w_mlp1", [D, DFF]), w_mlp2=di("w_mlp2", [DFF, D]),
            cst=di("cst", [128, NCONST, 128]), sel=di("sel", [16, 128]), oh=di("oh", [32, 387]), ohs=di("ohs", [32, 2304]),
        )
        KW = min(2048, TP)
        self.KW = KW
        self.o = dict(
            yp=do("yp", [TP, D]), ys=do("ys", [128, D]), kwp=do("kwp", [KW, 512]), vwp=do("vwp", [KW, 512]),
            wkvp=do("wkvp", [512, 64]), shp=do("shp", [1, NSH]), kns=do("kns", [128, 512]), vns=do("vns", [128, 512]),
            wkvs=do("wkvs", [128, 4096]), shs=do("shs", [16, NSH]),
        )
        self.s = dict(
            qkv=ds("s_qkv", [TA, 1536], BF16), p=ds("s_p", [TA, NSH]), rw=ds("s_rw", [TA, 512], BF16),
            att=ds("s_att", [3, TP, 528]), atts=ds("s_atts", [128, 512]),
            gd=ds("s_gd", [8, 3, 384]), erev=ds("s_erev", [2304, 8]),
            rs=ds("s_rs", [128, 6 * 512]), ys=ds("s_ys", [128, 512]),
        )
        self.sb = {k: Buf("s_" + k) for k in self.s}
        self.dbg = {}
        if self.debug:
            self.dbg["E"] = do("dbgE", [128, 3 * 8 * 2 * 128])

    def tt(self, eng, out, a, b, op, r, w):
        return self.P.op(eng, lambda e: e.tensor_tensor(out=out, in0=a, in1=b, op=op), r, w)

    def ts(self, eng, out, a, s1, s2, op0, op1, r, w):
        if op1 is None:
            return self.P.op(eng, lambda e: e.tensor_scalar(out=out, in0=a, scalar1=s1, scalar2=None, op0=op0), r, w)
        return self.P.op(eng, lambda e: e.tensor_scalar(out=out, in0=a, scalar1=s1, scalar2=s2, op0=op0, op1=op1), r, w)

    def stt(self, eng, out, a, sc, b, op0, op1, r, w):
        return self.P.op(eng, lambda e: e.scalar_tensor_tensor(out=out, in0=a, scalar=sc, in1=b, op0=op0, op1=op1), r, w)

    def act(self, out, in_, func, r, w, scale=1.0, bias=0.0, accum=None):
        if accum is None:
            return self.P.op("act", lambda e: e.activation(out=out, in_=in_, func=func, bias=bias, scale=scale), r, w)
        return self.P.op("act", lambda e: e.activation(out=out, in_=in_, func=func, bias=bias, scale=scale, accum_out=accum), r, w)

    def cp(self, eng, out, in_, r, w):
        if eng == "act":
            return self.act(out, in_, AF.Copy, r, w)
        return self.P.op(eng, lambda e: e.tensor_copy(out=out, in_=in_), r, w)

    def red(self, eng, out, in_, r, w, op=ALU.add):
        return self.P.op(eng, lambda e: e.tensor_reduce(out=out, in_=in_, axis=AX.X, op=op), r, w)

    def memset(self, eng, ap, val, w):
        return self.P.op(eng, lambda e: e.memset(ap, val), (), w)

    def mm(self, out, lhsT, rhs, start, stop, r, w, last=None):
        if last is None:
            last = stop
        return self.P.mm(lambda e: e.matmul(out, lhsT=lhsT, rhs=rhs, start=start, stop=stop), r, w, last=last)

    def tr(self, out, in_, ident, r, w, last=True):
        return self.P.mm(lambda e: e.transpose(out, in_, ident), r, w, last=last)

    def dma(self, q, out, in_, r, w, is_output=False, slow=False):
        if slow:
            return self.P.dma(q, lambda e: e.dma_start(out=out, in_=in_, allow_slow_non_contiguous=True), r, w)
        return self.P.dma(q, lambda e: e.dma_start(out=out, in_=in_), r, w, is_output=is_output)

    def barrier(self):
        P = self.P
        for e in P.ENGS:
            for n in ("pe", "act", "dve", "pool"):
                if P.count[n] > 0:
                    P._need(e, ("c_" + n, P.count[n]))
            for sem, tot in P.dma_tot.items():
                if tot > 0:
                    P._need(e, (sem, tot))

    def rsqrt(self, out, in_, r, w, scale, eps, tmp):
        self.act(tmp, in_, AF.Sqrt, r, w + [], scale=scale, bias=eps)
        return self.P.op("dve", lambda e: e.reciprocal(out=out, in_=tmp), r, w)

    def bcast_row(self, t, row=0, n=None):
        n = n or t.shape[1]
        return bass.AP(t, row * t.shape[1], [[0, 128], [1, n]])

    def phase1a(self):
        nc, P, NT, NTP = self.nc, self.P, self.NT, self.NTP
        I, O, S = self.i, self.o, self.s
        with contextlib.ExitStack() as st:
            sb = lambda name, shape, dt=F32: st.enter_context(nc.sbuf_tensor(name, list(shape), dt))
            ps = lambda name, shape, dt=F32: st.enter_context(nc.psum_tensor(name, list(shape), dt))
            win = sb("a_win", [128, 8, NIN], BF16)
            stg = [sb(f"a_stg{i}", [128, NIN]) for i in range(2)]
            gcol = sb("a_gcol", [128, 8])
            identb = sb("a_identb", [128, 128], BF16)
            identf = sb("a_identf", [128, 128])
            gqk = sb("a_gqk", [128, 1024])
            xt = [sb(f"a_x{i}", [128, D]) for i in range(2)]
            junk = sb("a_junk", [128, D])
            junkq = sb("a_junkq", [128, D]); b_junkq = Buf()
            junk2x = sb("a_x2", [128, D])
            stA0 = sb("a_stA0", [128, 4]); stA1 = sb("a_stA1", [128, 4]); stA2 = sb("a_stA2", [128, 4])
            nb = [sb(f"a_nb{i}", [128, D], BF16) for i in range(2)]
            nT = [sb(f"a_nT{i}", [128, 8, 128], BF16) for i in range(2)]
            proj = [sb(f"a_proj{i}", [128, NIN]) for i in range(2)]
            qkvb = [sb(f"a_qkvb{i}", [128, 1536], BF16) for i in range(2)]
            st4 = [sb(f"a_st{i}", [128, 40]) for i in range(2)]
            psT = [ps(f"a_psT{i}", [128, 8, 128], BF16) for i in range(2)]
            psG = [ps(f"a_psG{i}", [128, 512]) for i in range(6)]
            b_win = [Buf() for _ in range(8)]
            b_stg = [Buf(), Buf()]
            b_c = Buf()
            b_x, b_nb, b_nT, b_proj, b_qkvb, b_st, b_psT = ([Buf(), Buf()] for _ in range(7))
            b_junk = Buf()
            b_psG = [Buf() for _ in range(6)]

            self.dma("sp", gcol[:], bass.AP(I["ln1_g"], 0, [[1, 128], [128, 8]]), [], [b_c], slow=True)
            self.dma("sp", identf[:], I["cst"].ap()[:, C_ID, :], [], [b_c])
            self.dma("sp", gqk[:, 0:512].rearrange("p (h e) -> p h e", e=64), bass.AP(I["q_norm_g"], 0, [[0, 128], [0, 8], [1, 64]]), [], [b_c])
            self.dma("sp", gqk[:, 512:1024].rearrange("p (h e) -> p h e", e=64), bass.AP(I["k_norm_g"], 0, [[0, 128], [0, 8], [1, 64]]), [], [b_c])
            self.cp("dve", identb[:], identf[:], [b_c], [b_c])
            for c in range(8):
                self.dma("sp", stg[c % 2][:], I["w_in"].ap()[c * 128:(c + 1) * 128, :], [], [b_stg[c % 2]])
                self.ts("dve", win[:, c, :], stg[c % 2][:], gcol[:, c:c + 1], None, ALU.mult, None,
                        [b_stg[c % 2], b_c], [b_win[c]])

            def xsrc(i):
                return I["xp"].ap()[i * 128:(i + 1) * 128, :] if i < NTP else I["xs"].ap()

            NX = 3
            xt3 = xt + [junk2x]
            b_x3 = b_x + [Buf()]
            st3 = [stA0, stA1, stA2]
            b_st3 = [Buf(), Buf(), Buf()]

            def loadx(i):
                if i < NT:
                    self.dma("sp", xt3[i % NX][:], xsrc(i), [], [b_x3[i % NX]])

            def stats(i):
                if i >= NT:
                    return
                k, kx = i % 2, i % NX
                s3 = st3[kx]
                self.memset("pool", s3[:, 0:1], 0.0, [b_st3[kx]])
                self.act(junk[:], xt3[kx][:], AF.Square, [b_x3[kx]], [b_junk, b_st3[kx]], accum=s3[:, 0:1])
                self.act(s3[:, 1:2], s3[:, 0:1], AF.Sqrt, [], [b_st3[kx]], scale=1.0 / D, bias=RMS_EPS)
                self.P.op("dve", lambda e, s3=s3: e.reciprocal(out=s3[:, 2:3], in_=s3[:, 1:2]), [], [b_st3[kx]])
                self.act(nb[k][:], xt3[kx][:], AF.Copy, [b_x3[kx], b_st3[kx]], [b_nb[k]], scale=s3[:, 2:3])

            def transp(i):
                if i >= NT:
                    return
                k = i % 2
                for c in range(8):
                    self.tr(psT[k][:, c, :], nb[k][:, c * 128:(c + 1) * 128], identb[:], [b_nb[k], b_c], [b_psT[k]], last=(c == 7))
                self.cp("dve", nT[k][:], psT[k][:], [], [b_psT[k], b_nT[k]])

            loadx(0); loadx(1); loadx(2)
            stats(0); transp(0); stats(1)
            gi = 0
            for i in range(NT):
                k = i % 2
                s4 = st4[k]
                for g in range(7):
                    if g == 4:
                        transp(i + 1)
                    c0 = g * 512
                    cw = min(512, NIN - c0)
                    pg = gi % 6
                    gi += 1
                    for c in range(8):
                        self.mm(psG[pg][:, 0:cw], nT[k][:, c, :], win[:, c, c0:c0 + cw], c == 0, c == 7,
                                [b_nT[k], b_win[c]], [b_psG[pg]])
                    self.cp("act" if g % 2 == 0 else "dve", proj[k][:, c0:c0 + cw], psG[pg][:, 0:cw], [], [b_psG[pg], b_proj[k]])
                qk3 = proj[k][:, 0:1024].rearrange("p (g e) -> p g e", e=64)
                self.tt("pool", junkq[:], proj[k][:, 0:1024], proj[k][:, 0:1024], ALU.mult, [b_proj[k]], [b_junkq])
                self.red("dve", s4[:, 4:20], junkq[:].rearrange("p (g e) -> p g e", e=64), [b_junkq], [b_st[k]])
                self.act(s4[:, 20:36], s4[:, 4:20], AF.Sqrt, [], [b_st[k]], scale=1.0 / 64, bias=RMS_EPS)
                self.P.op("dve", lambda e, s4=s4: e.reciprocal(out=s4[:, 4:20], in_=s4[:, 20:36]), [], [b_st[k]])
                self.tt("dve", qk3, qk3, s4[:, 4:20].unsqueeze(2).to_broadcast([128, 16, 64]), ALU.mult, [b_st[k]], [b_proj[k]])
                self.tt("pool", proj[k][:, 0:1024], proj[k][:, 0:1024], gqk[:], ALU.mult, [b_c], [b_proj[k]])
                self.cp("pool", qkvb[k][:], proj[k][:, 0:1536], [b_proj[k]], [b_qkvb[k]])
                loadx(i + 3)
                stats(i + 2)
                r0 = i * 128
                self.dma("sp", S["qkv"].ap()[r0:r0 + 128, :], qkvb[k][:], [b_qkvb[k]], [])
                self.dma("sp", S["p"].ap()[r0:r0 + 128, :], proj[k][:, 1536:NIN], [b_proj[k]], [])
                if i < NTP:
                    j = i - (NTP - self.KW // 128)
                    if j >= 0:
                        self.dma("sp", O["kwp"].ap()[j * 128:(j + 1) * 128, :], proj[k][:, 512:1024], [b_proj[k]], [])
                        self.dma("sp", O["vwp"].ap()[j * 128:(j + 1) * 128, :], proj[k][:, 1024:1536], [b_proj[k]], [])
                    if i == NTP - 1:
                        self.dma("sp", O["shp"].ap(), proj[k][127:128, 1536:NIN], [b_proj[k]], [])
                else:
                    self.dma("sp", O["kns"].ap(), proj[k][:, 512:1024], [b_proj[k]], [])
                    self.dma("sp", O["vns"].ap(), proj[k][:, 1024:1536], [b_proj[k]], [])
                    for s in range(16):
                        self.dma("sp", O["shs"].ap()[s:s + 1, :], proj[k][8 * s + 7:8 * s + 8, 1536:NIN], [b_proj[k]], [])
            self.barrier()
            self.P.emit_phase()


_CACHE = {}


def build(TP=8192, debug=False, phases=("1a", "1b", "2", "2s", "3"), scratch_in=()):
    key = (TP, debug, tuple(phases), tuple(scratch_in))
    if key in _CACHE:
        return _CACHE[key]
    kb = KB(TP, debug, scratch_in)
    with contextlib.ExitStack() as st:
        kb.P.open(st)
        for ph in phases:
            getattr(kb, "phase" + ph)()
    _CACHE[key] = kb
    return kb


def make_in_maps(inp, TP, ncores):
    hc = host_consts()
    f = lambda a: np.ascontiguousarray(np.asarray(a, dtype=np.float32))
    maps = []
    for c in range(ncores):
        m = dict(
            xp=f(inp["x_prompt"][c, :TP]), xs=f(inp["x_sample"][16 * c:16 * c + 16]).reshape(128, D),
            ck=f(inp["cache_k_win"][0, 16 * c:16 * c + 16]).reshape(16 * 2048, 512),
            cv=f(inp["cache_v_win"][0, 16 * c:16 * c + 16]).reshape(16 * 2048, 512),
            swkv=f(inp["state_wkv"][0, 16 * c:16 * c + 16]).reshape(128, 4096),
            sshift=f(inp["state_shift"][0, 16 * c:16 * c + 16]),
            bias_table=f(inp["bias_table"]), ln1_g=f(inp["ln1_g"]), w_in=f(inp["w_in"][0]),
            q_norm_g=f(inp["q_norm_g"]), k_norm_g=f(inp["k_norm_g"]), mu_shift=f(inp["mu_shift"]),
            w0=f(inp["w0"]), w_lora2=f(inp["w_lora2"][0]), a0=f(inp["a0"]), a_lora2=f(inp["a_lora2"][0]),
            g_lora2=f(inp["g_lora2"][0]), k_k=f(inp["k_k"]), k_a=f(inp["k_a"]), r_k=f(inp["r_k"]).reshape(1, 512),
            lnx_g=f(inp["lnx_g"]), lnx_b=f(inp["lnx_b"]), w_out=f(inp["w_out"][0]), ln2_g=f(inp["ln2_g"]),
            w_mlp1=f(inp["w_mlp1"][0]), w_mlp2=f(inp["w_mlp2"][0]),
        )
        m.update(hc)
        maps.append(m)
    return maps


def run(inp, TP=8192, ncores=8, debug=False, phases=("1a", "1b", "2", "2s", "3"), extra=None):
    kb = build(TP, debug, phases, tuple(sorted(extra[0].keys())) if extra else ())
    maps = make_in_maps(inp, TP, ncores)
    if extra:
        for m, e in zip(maps, extra):
            m.update(e)
    import os
    if os.environ.get("KTRACE"):
        res = run_bass_kernel_spmd(kb.nc, maps, core_ids=list(range(ncores)), trace=True)
        print("EXEC_TIME_NS", res.exec_time_ns)
    else:
        res = run_bass_kernel_spmd(kb.nc, maps, core_ids=list(range(ncores)))
    return res.results


def kernel(**inp):
    r = run(inp)
    n = 8
    cat = lambda k: np.stack([np.asarray(r[c][k], dtype=np.float32) for c in range(n)])
    yp = cat("yp")
    ys = cat("ys").reshape(128, 8, D)
    kwp = cat("kwp").reshape(1, 8, 2048, 8, 64)
    vwp = cat("vwp").reshape(1, 8, 2048, 8, 64)
    wkvp = cat("wkvp").reshape(1, 8, 8, 64, 64)
    shp = cat("shp").reshape(1, 8, NSH)
    kns = cat("kns").reshape(1, 128, 8, 8, 64)
    vns = cat("vns").reshape(1, 128, 8, 8, 64)
    wkvs = cat("wkvs").reshape(1, 128, 8, 64, 64)
    shs = cat("shs").reshape(1, 128, NSH)
    return (yp, ys, kwp, vwp, wkvp, shp, kns, vns, wkvs, shs)


def _phase3(self):
    nc, P, NT, NTP = self.nc, self.P, self.NT, self.NTP
    I, O, S = self.i, self.o, self.s
    with contextlib.ExitStack() as st:
        sb = lambda name, shape, dt=F32: st.enter_context(nc.sbuf_tensor(name, list(shape), dt))
        ps = lambda name, shape, dt=F32: st.enter_context(nc.psum_tensor(name, list(shape), dt))
        wout = sb("c_wout", [128, 8, D], BF16)
        w1 = sb("c_w1", [128, 8, DFF], BF16)
        w2 = sb("c_w2", [128, 32, D], BF16)
        stg = [sb(f"c_stg{i}", [128, 512]) for i in range(2)]
        gcol = sb("c_gcol", [128, 8])
        identb = sb("c_identb", [128, 128], BF16)
        identf = sb("c_identf", [128, 128])
        xt = [sb(f"c_x{i}", [128, D]) for i in range(2)]
        att3 = sb("c_att3", [128, 3, 528])
        mixed = [sb(f"c_mix{i}", [128, D], BF16) for i in range(2)]
        mixT = [sb(f"c_mixT{i}", [128, 8, 128], BF16) for i in range(2)]
        hbuf = [sb(f"c_h{i}", [128, D]) for i in range(2)]
        junk = sb("c_junk", [128, D], BF16)
        mbf = [sb(f"c_mbf{i}", [128, D], BF16) for i in range(2)]
        mT = sb("c_mT", [128, 8, 256], BF16)
        rl = [sb(f"c_rl{i}", [128, 256]) for i in range(2)]
        uT = sb("c_uT", [128, 32, 256], BF16)
        st4 = [sb(f"c_st{i}", [128, 16]) for i in range(2)]
        psT = [ps(f"c_psT{i}", [128, 8, 128], BF16) for i in range(2)]
        psH = [ps(f"c_psH{i}", [128, 512]) for i in range(2)]
        psU = [ps(f"c_psU{i}", [128, 512]) for i in range(2)]
        b_c, b_wout, b_w2, b_junk, b_att3, b_mbf_unused, b_mT, b_uT = (Buf() for _ in range(8))
        b_mbf = [Buf(), Buf()]
        b_w1 = [Buf() for _ in range(8)]
        b_stg, b_x, b_mix, b_mixT, b_h, b_rl, b_st, b_psT, b_psH, b_psU = ([Buf(), Buf()] for _ in range(10))

        self.dma("sp", gcol[:], bass.AP(I["ln2_g"], 0, [[1, 128], [128, 8]]), [], [b_c], slow=True)
        self.dma("sp", identf[:], I["cst"].ap()[:, C_ID, :], [], [b_c])
        self.cp("dve", identb[:], identf[:], [b_c], [b_c])
        for c in range(8):
            self.dma("pool", wout[:, c, :], I["w_out"].ap()[c * 128:(c + 1) * 128, :], [], [b_wout])
        si = 0
        for c in range(8):
            for q in range(8):
                k = si % 2
                si += 1
                self.dma("sp", stg[k][:], I["w_mlp1"].ap()[c * 128:(c + 1) * 128, q * 512:(q + 1) * 512], [], [b_stg[k]])
                self.ts("dve", w1[:, c, q * 512:(q + 1) * 512], stg[k][:], gcol[:, c:c + 1], None, ALU.mult, None,
                        [b_stg[k], b_c], [b_w1[c]])
        for fc in range(32):
            self.dma("pool", w2[:, fc, :], I["w_mlp2"].ap()[fc * 128:(fc + 1) * 128, :], [], [b_w2])

        groups = [list(range(g, min(g + 2, NTP))) for g in range(0, NTP, 2)] + [[NTP]]
        cnt = {"pi": 0}

        def part0(grp):
            for g, i in enumerate(grp):
                k = g
                r0 = i * 128
                if i < NTP:
                    self.dma("sp", xt[k][:], I["xp"].ap()[r0:r0 + 128, :], [], [b_x[k]])
                    self.dma("sp", att3[:], S["att"].ap()[:, r0:r0 + 128, :].rearrange("b t f -> t b f"), [], [b_att3])
                    self.tt("pool", att3[:, 0, :], att3[:, 0, :], att3[:, 1, :], ALU.add, [], [b_att3])
                    self.tt("pool", att3[:, 0, :], att3[:, 0, :], att3[:, 2, :], ALU.add, [], [b_att3])
                    a3 = att3[:, 0, :].rearrange("p (h e) -> p h e", e=66)
                    self.P.op("dve", lambda e, a3=a3, s=st4[k]: e.reciprocal(out=s[:, 4:12], in_=a3[:, :, 64]), [b_att3], [b_st[k]])
                    self.tt("dve", mixed[k][:, 0:512].rearrange("p (h e) -> p h e", e=64), a3[:, :, 0:64],
                            st4[k][:, 4:12].unsqueeze(2).to_broadcast([128, 8, 64]), ALU.mult, [b_att3, b_st[k]], [b_mix[k]])
                else:
                    self.dma("sp", xt[k][:], I["xs"].ap(), [], [b_x[k]])
                    self.dma("sp", att3[:, 0, 0:512], S["atts"].ap(), [], [b_att3])
                    self.cp("dve", mixed[k][:, 0:512], att3[:, 0, 0:512], [b_att3], [b_mix[k]])
                self.dma("sp", mixed[k][:, 512:1024], S["rw"].ap()[r0:r0 + 128, :], [], [b_mix[k]])
                for c in range(8):
                    self.tr(psT[k][:, c, :], mixed[k][:, c * 128:(c + 1) * 128], identb[:], [b_mix[k], b_c], [b_psT[k]], last=(c == 7))
                self.cp("act", mixT[k][:], psT[k][:], [], [b_psT[k], b_mixT[k]])

        def part1(grp):
            for g, i in enumerate(grp):
                k = g
                for half in range(2):
                    pk = cnt["pi"] % 2
                    cnt["pi"] += 1
                    for c in range(8):
                        self.mm(psH[pk][:], mixT[k][:, c, :], wout[:, c, half * 512:(half + 1) * 512], c == 0, c == 7,
                                [b_mixT[k], b_wout], [b_psH[pk]])
                    self.tt("dve", hbuf[g][:, half * 512:(half + 1) * 512], psH[pk][:], xt[k][:, half * 512:(half + 1) * 512], ALU.add,
                            [b_x[k]], [b_psH[pk], b_h[g]])

        def part2(grp):
            for g, i in enumerate(grp):
                k = g
                s4 = st4[k]
                self.memset("pool", s4[:, 0:1], 0.0, [b_st[k]])
                self.act(junk[:], hbuf[g][:], AF.Square, [b_h[g]], [b_junk, b_st[k]], accum=s4[:, 0:1])
                self.act(s4[:, 1:2], s4[:, 0:1], AF.Sqrt, [], [b_st[k]], scale=1.0 / D, bias=RMS_EPS)
                self.P.op("dve", lambda e, s4=s4: e.reciprocal(out=s4[:, 2:3], in_=s4[:, 1:2]), [], [b_st[k]])
                self.act(mbf[g][:], hbuf[g][:], AF.Copy, [b_h[g], b_st[k]], [b_mbf[g]], scale=s4[:, 2:3])
            for g, i in enumerate(grp):
                k = g
                for c in range(8):
                    self.tr(psT[k][:, c, :], mbf[g][:, c * 128:(c + 1) * 128], identb[:], [b_mbf[g], b_c], [b_psT[k]], last=(c == 7))
                self.cp("act", mT[:, :, g * 128:(g + 1) * 128], psT[k][:], [], [b_psT[k], b_mT])

        part0(groups[0])
        for n, grp in enumerate(groups):
            G = len(grp)
            part1(grp)
            part2(grp)
            W = G * 128
            for fc in range(32):
                if fc == 16 and n + 1 < len(groups):
                    part0(groups[n + 1])
                pk = fc % 2
                for c in range(8):
                    self.mm(psU[pk][:, 0:W], w1[:, c, fc * 128:(fc + 1) * 128], mT[:, c, 0:W], c == 0, c == 7,
                            [b_mT, b_w1[c]], [b_psU[pk]])
                self.act(rl[pk][:, 0:W], psU[pk][:, 0:W], AF.Relu, [], [b_psU[pk], b_rl[pk]])
                self.tt("pool" if fc % 4 != 3 else "dve", uT[:, fc, 0:W], rl[pk][:, 0:W], rl[pk][:, 0:W], ALU.mult, [b_rl[pk]], [b_uT])
            for g, i in enumerate(grp):
                for half in range(2):
                    pk = cnt["pi"] % 2
                    cnt["pi"] += 1
                    for fc in range(32):
                        self.mm(psH[pk][:], uT[:, fc, g * 128:(g + 1) * 128], w2[:, fc, half * 512:(half + 1) * 512], fc == 0, fc == 31,
                                [b_uT, b_w2], [b_psH[pk]])
                    self.tt("dve", hbuf[g][:, half * 512:(half + 1) * 512], psH[pk][:], hbuf[g][:, half * 512:(half + 1) * 512], ALU.add,
                            [], [b_psH[pk], b_h[g]])
                dst = O["yp"].ap()[i * 128:(i + 1) * 128, :] if i < NTP else O["ys"].ap()
                self.dma("sp", dst, hbuf[g][:], [b_h[g]], [])
        self.barrier()
        self.P.emit_phase()


KB.phase3 = _phase3


def _phase2(self):
    nc, P, NT, NTP, TP = self.nc, self.P, self.NT, self.NTP, self.TP
    I, O, S = self.i, self.o, self.s
    with contextlib.ExitStack() as st:
        sb = lambda name, shape, dt=F32: st.enter_context(nc.sbuf_tensor(name, list(shape), dt))
        ps = lambda name, shape, dt=F32: st.enter_context(nc.psum_tensor(name, list(shape), dt))
        identb = sb("b_identb", [128, 128], BF16)
        identf = sb("b_identf", [128, 128])
        Jm = sb("b_J", [128, 128])
        tbl = sb("b_tbl", [32, 8])
        etbl = sb("b_etbl", [32, 8])
        oh = sb("b_oh", [32, 387])
        Gt = sb("b_G", [8, 3, 384])
        Hl = [sb(f"b_Hl{i}", [128, 8, 128]) for i in range(2)]
        E = sb("b_E", [128, 3, 8, 2, 128])
        qkv = [sb(f"b_qkv{i}", [128, 1536], BF16) for i in range(2)]
        QT = [sb(f"b_QT{i}", [128, 4, 128], BF16) for i in range(2)]
        KT = [sb(f"b_KT{i}", [128, 4, 128], BF16) for i in range(2)]
        V1 = [sb(f"b_V1{i}", [128, 8, 66], BF16) for i in range(2)]
        ex = [sb(f"b_ex{i}", [128, 2, 2, 128]) for i in range(4)]
        PT = [sb(f"b_PT{i}", [128, 8, 2, 128], BF16) for i in range(2)]
        osb = [sb(f"b_o{i}", [128, 8, 66]) for i in range(2)]
        psT = [ps(f"b_psT{i}", [128, 2, 4, 128], BF16) for i in range(2)]
        psS = [ps(f"b_psS{i}", [128, 2, 2, 128]) for i in range(4)]
        psOf = [ps(f"b_psO{i}", [128, 512]) for i in range(2)]
        psO = [t[:, 0:264].rearrange("p (h e) -> p h e", e=66) for t in psOf]
        b_c, b_G, b_E = Buf(), Buf(), Buf()
        b_Hl, b_qkv, b_QT, b_KT, b_V1, b_ex, b_PT, b_o, b_psT, b_psS, b_psO = ([Buf(), Buf(), Buf(), Buf()] for _ in range(11))

        self.dma("sp", identf[:], I["cst"].ap()[:, C_ID, :], [], [b_c])
        self.dma("sp", Jm[:], I["cst"].ap()[:, C_J, :], [], [b_c])
        self.dma("sp", tbl[:], I["bias_table"].ap(), [], [b_c])
        self.dma("sp", oh[:], I["oh"].ap(), [], [b_c])
        self.cp("dve", identb[:], identf[:], [b_c], [b_c])
        self.memset("pool", Gt[:], 0.0, [b_G])
        for k in range(2):
            self.memset("pool", V1[k][:], 1.0, [b_V1[k]])
        import os
        STEP = int(os.environ.get("P2_STEP", "99"))
        if STEP >= 2:
            self.mm(psS[0][:].rearrange("p a b c -> p (a b c)")[0:8, 0:387],
                    tbl[:], oh[:], True, True, [b_c], [b_psS[0]])
        for br in range(3 if STEP >= 2 else 0):
            self.act(Gt[:, br, 127:256], psS[0][:].rearrange("p a b c -> p (a b c)")[0:8, br * 129:(br + 1) * 129], AF.Exp, [b_psS[0]], [b_G])
        if STEP >= 3:
            self.dma("sp", S["gd"].ap(), Gt[:], [b_G], [self.sb["gd"]])
        hi = 0
        for br in range(3 if STEP >= 4 else 0):
            for kb in range(2):
                k = hi % 2
                hi += 1
                off = br * 384 + (128 if kb == 0 else 0)
                self.dma("sp", Hl[k][:], bass.AP(S["gd"], off, [[1, 128], [3 * 384, 8], [1, 128]]), [self.sb["gd"]], [b_Hl[k]])
                for hh in range(2 if STEP >= 5 else 0):
                    self.mm(psS[hh][:].rearrange("p a b c -> p (a b c)"), Jm[:], Hl[k][:, hh * 4:(hh + 1) * 4, :].rearrange("p h m -> p (h m)"),
                            True, True, [b_c, b_Hl[k]], [b_psS[hh]])
                    self.cp("dve", E[:, br, hh * 4:(hh + 1) * 4, kb, :], psS[hh][:].rearrange("p a b c -> p (a b c)").rearrange("p (h m) -> p h m", m=128),
                            [b_psS[hh]], [b_E])

        blocks = []
        for br, (w, d) in enumerate(BRANCHES):
            nbk = TP // (128 * d)
            for r in range(d):
                for B in range(nbk):
                    blocks.append((br, d, r, B))

        def load(idx):
            br, d, r, B = blocks[idx]
            k = idx % 2
            row0 = r + d * 128 * B
            self.dma("sp", qkv[k][:], bass.AP(S["qkv"], row0 * 1536, [[d * 1536, 128], [1, 1536]]), [], [b_qkv[k]])

        import os
        if os.environ.get("P2_MAXBLK"):
            blocks = blocks[:int(os.environ["P2_MAXBLK"])]
        if self.debug:
            self.dma("sp", self.dbg["E"].ap(), E[:].rearrange("p a b c d -> p (a b c d)"), [b_E], [])
        if blocks:
            load(0)
        si = 0
        for idx, (br, d, r, B) in enumerate(blocks):
            k = idx % 2
            cur, prv = B % 2, 1 - (B % 2)
            if idx + 1 < len(blocks):
                load(idx + 1)
            BSTEP = int(os.environ.get("P2_BSTEP", "99"))
            if BSTEP < 1:
                continue
            for hp in range(4):
                self.tr(psT[k][:, 0, hp, :], qkv[k][:, hp * 128:(hp + 1) * 128], identb[:], [b_qkv[k], b_c], [b_psT[k]], last=False)
            for hp in range(4):
                self.tr(psT[k][:, 1, hp, :], qkv[k][:, 512 + hp * 128:512 + (hp + 1) * 128], identb[:], [b_qkv[k], b_c], [b_psT[k]], last=(hp == 3))
            if BSTEP == 1 and os.environ.get("P2_SUB") == "a":
                continue
            if os.environ.get("P2_SUB") != "b2":
                self.cp("dve", QT[k][:], psT[k][:, 0, :, :], [b_psT[k]], [b_QT[k]])
            if os.environ.get("P2_SUB") != "b1":
                self.cp("dve", KT[cur][:], psT[k][:, 1, :, :], [b_psT[k]], [b_KT[cur]])
            if os.environ.get("P2_SUB") in ("b1", "b2"):
                continue
            if BSTEP == 1 and os.environ.get("P2_SUB") == "b":
                continue
            self.cp("dve", V1[cur][:, :, 0:64], qkv[k][:, 1024:1536].rearrange("p (h e) -> p h e", e=64), [b_qkv[k]], [b_V1[cur]])
            kbs = (0, 1) if B > 0 else (1,)
            BSTEP = int(os.environ.get("P2_BSTEP", "99"))
            PTv = PT[k][:].rearrange("p (q i t) b m -> p q t i b m", q=2, i=2, t=2)
            Ev = E[:, br, :, :, :].rearrange("p (q i t) b m -> p q t i b m", q=2, i=2, t=2)
            for q in range(2 if BSTEP >= 2 else 0):
                sk = si % 2
                si += 1
                n_mm = 4 * len(kbs)
                j = 0
                for hpi in range(2):
                    hp = 2 * q + hpi
                    for kb in kbs:
                        kt = KT[cur] if kb == 1 else KT[prv]
                        bk = b_KT[cur] if kb == 1 else b_KT[prv]
                        for h2 in range(2):
                            j += 1
                            self.mm(psS[2 * sk + h2][:, hpi, kb, :], kt[64 * h2:64 * h2 + 64, hp, :], QT[k][64 * h2:64 * h2 + 64, hp, :], True, True,
                                    [bk, b_QT[k]], [b_psS[2 * sk], b_psS[2 * sk + 1]], last=(j == n_mm))
                if BSTEP < 3:
                    continue
                for h2 in range(2):
                    bi = 2 * sk + h2
                    eng = "dve"
                    if B > 0:
                        self.act(ex[bi][:], psS[bi][:], AF.Exp, [b_psS[bi]], [b_ex[bi]], scale=0.125)
                        self.tt(eng, PTv[:, q, h2], ex[bi][:], Ev[:, q, h2], ALU.mult, [b_ex[bi], b_E], [b_PT[k]])
                    else:
                        self.act(ex[bi][:, :, 1, :], psS[bi][:, :, 1, :], AF.Exp, [b_psS[bi]], [b_ex[bi]], scale=0.125)
                        self.tt(eng, PTv[:, q, h2, :, 1, :], ex[bi][:, :, 1, :], Ev[:, q, h2, :, 1, :], ALU.mult, [b_ex[bi], b_E], [b_PT[k]])
            for hh in range(2 if BSTEP >= 4 else 0):
                for h4 in range(4):
                    h = hh * 4 + h4
                    for ji, kb in enumerate(kbs):
                        vv = V1[cur] if kb == 1 else V1[prv]
                        bv = b_V1[cur] if kb == 1 else b_V1[prv]
                        self.mm(psO[hh][:, h4, :], PT[k][:, h, kb, :], vv[:, h, :], ji == 0, ji == len(kbs) - 1,
                                [b_PT[k], bv], [b_psO[hh]], last=(h4 == 3 and ji == len(kbs) - 1))
                self.cp("act" if hh == 0 else "dve", osb[k][:, hh * 4:(hh + 1) * 4, :], psO[hh], [b_psO[hh]], [b_o[k]])
            row0 = r + d * 128 * B
            if BSTEP >= 5:
              self.dma("sp", bass.AP(S["att"], (br * TP + row0) * 528, [[d * 528, 128], [1, 528]]), osb[k][:].rearrange("p h e -> p (h e)"), [b_o[k]], [])
        self.barrier()
        self.P.emit_phase()


KB.phase2 = _phase2


def _phase2s(self):
    nc, P, TP = self.nc, self.P, self.TP
    I, O, S = self.i, self.o, self.s
    with contextlib.ExitStack() as st:
        sb = lambda name, shape, dt=F32: st.enter_context(nc.sbuf_tensor(name, list(shape), dt))
        ps = lambda name, shape, dt=F32: st.enter_context(nc.psum_tensor(name, list(shape), dt))
        identb = sb("s_identb", [128, 128], BF16)
        identf = sb("s_identf", [128, 128])
        tbl = sb("s_tbl", [32, 8])
        etbl = sb("s_etbl", [32, 8])
        ohs = sb("s_ohs", [32, 2304])
        erev = sb("s_erevsb", [128, 18, 8])
        Ecomb = sb("s_Ecomb", [128, 8, 17, 8])
        qs = sb("s_qs", [128, 512], BF16)
        QTs = sb("s_QTs", [128, 4, 128], BF16)
        Kb = [sb(f"s_Kb{i}", [128, 16, 512], BF16) for i in range(2)]
        Vb = [sb(f"s_Vb{i}", [128, 16, 512], BF16) for i in range(2)]
        Knew = [sb(f"s_Knew{i}", [8, 1536], BF16) for i in range(2)]
        KTs = sb("s_KTs", [128, 4, 17, 128], BF16)
        V1s = sb("s_V1s", [128, 17, 8, 66], BF16)
        exs = sb("s_exs", [128, 8, 17, 8])
        PTs = sb("s_PTs", [128, 8, 17, 8], BF16)
        osb = sb("s_osb", [8, 8, 66])
        rec = sb("s_rec", [8, 8])
        ao = [sb(f"s_ao{i}", [8, 8, 64]) for i in range(2)]
        psT = [ps(f"s_psT{i}", [128, 8, 128], BF16) for i in range(2)]
        psS = [ps(f"s_psS{i}", [128, 512]) for i in range(4)]
        psOf = [ps(f"s_psO{i}", [128, 512]) for i in range(2)]
        psSv = [t[:, 0:272].rearrange("p (i n t) -> p i n t", i=2, n=17) for t in psS]
        psO = [t[:, 0:264].rearrange("p (h e) -> p h e", e=66) for t in psOf]
        b_c, b_E, b_QTs, b_KTs, b_V1s, b_exs, b_PTs, b_osb, b_qs = (Buf() for _ in range(9))
        b_Kb, b_Vb, b_Knew, b_ao, b_psT, b_psO = ([Buf(), Buf()] for _ in range(6))
        b_psS = [Buf() for _ in range(4)]

        self.dma("sp", identf[:], I["cst"].ap()[:, C_ID, :], [], [b_c])
        self.dma("sp", tbl[:], I["bias_table"].ap(), [], [b_c])
        self.dma("sp", ohs[:], I["ohs"].ap(), [], [b_c])
        self.dma("sp", qs[:], S["qkv"].ap()[TP:TP + 128, 0:512], [], [b_qs])
        self.cp("dve", identb[:], identf[:], [b_c], [b_c])
        self.act(etbl[:], tbl[:], AF.Exp, [b_c], [b_c])
        self.memset("pool", V1s[:], 1.0, [b_V1s])
        ev = psS[0][:, 0:144].rearrange("p (n h) -> p n h", h=8)
        for xb in range(18):
            self.mm(ev[:, xb, :], ohs[:, xb * 128:(xb + 1) * 128], etbl[:], True, True, [b_c], [b_psS[0]], last=(xb == 17))
        self.cp("dve", erev[:], ev, [], [b_psS[0], b_E])
        self.dma("sp", bass.AP(S["erev"], 0, [[8, 128], [1024, 18], [1, 8]]), erev[:], [b_E], [self.sb["erev"]])
        for t in range(8):
            self.dma("sp", Ecomb[:, t, :, :], bass.AP(S["erev"], (7 - t) * 8, [[8, 128], [1024, 17], [1, 8]]), [self.sb["erev"]], [b_E])
        Ev = Ecomb[:].rearrange("p t n h -> p h n t")
        for hp in range(4):
            self.tr(psT[0][:, hp, :], qs[:, hp * 128:(hp + 1) * 128], identb[:], [b_qs, b_c], [b_psT[0]], last=(hp == 3))
        self.cp("dve", QTs[:], psT[0][:, 0:4, :], [], [b_psT[0], b_QTs])

        def load(s):
            k = s % 2
            self.dma("pool", Kb[k][:], bass.AP(I["ck"], s * 2048 * 512, [[512, 128], [128 * 512, 16], [1, 512]]), [], [b_Kb[k]])
            self.dma("pool", Vb[k][:], bass.AP(I["cv"], s * 2048 * 512, [[512, 128], [128 * 512, 16], [1, 512]]), [], [b_Vb[k]])
            self.dma("sp", Knew[k][:], S["qkv"].ap()[TP + 8 * s:TP + 8 * s + 8, :], [], [b_Knew[k]])

        load(0)
        ti = 0
        for s in range(16):
            k = s % 2
            if s + 1 < 16:
                load(s + 1)
            for bt in range(8):
                pk = ti % 2
                ti += 1
                for nbi in range(2):
                    for hp in range(4):
                        self.tr(psT[pk][:, nbi * 4 + hp, :], Kb[k][:, 2 * bt + nbi, hp * 128:(hp + 1) * 128], identb[:], [b_Kb[k], b_c], [b_psT[pk]],
                                last=(nbi == 1 and hp == 3))
                self.cp("dve", KTs[:, :, 2 * bt:2 * bt + 2, :], psT[pk][:].rearrange("p (n h) m -> p h n m", n=2), [], [b_psT[pk], b_KTs])
            pk = ti % 2
            ti += 1
            for hp in range(4):
                self.tr(psT[pk][:, hp, 0:8], Knew[k][0:8, 512 + hp * 128:512 + (hp + 1) * 128], identb[0:8, 0:8], [b_Knew[k], b_c], [b_psT[pk]], last=(hp == 3))
            self.cp("dve", KTs[:, :, 16, 0:8], psT[pk][:, 0:4, 0:8], [], [b_psT[pk], b_KTs])
            self.cp("pool", V1s[:, 0:16, :, 0:64], Vb[k][:].rearrange("p n (h e) -> p n h e", e=64), [b_Vb[k]], [b_V1s])
            self.cp("pool", V1s[0:8, 16, :, 0:64], Knew[k][0:8, 1024:1536].rearrange("p (h e) -> p h e", e=64), [b_Knew[k]], [b_V1s])
            for hq in range(2):
                for i in range(2):
                    hp = 2 * hq + i
                    for nb in range(17):
                        M = 128 if nb < 16 else 8
                        for h2 in range(2):
                            bi = 2 * hq + h2
                            self.mm(psSv[bi][0:M, i, nb, :], KTs[64 * h2:64 * h2 + 64, hp, nb, 0:M], QTs[64 * h2:64 * h2 + 64, hp, 8 * s:8 * s + 8], True, True,
                                    [b_KTs, b_QTs], [b_psS[bi]], last=(i == 1 and nb == 16))
            exv = exs[:].rearrange("p (q i t) n m -> p q t i n m", q=2, i=2, t=2)
            PTv = PTs[:].rearrange("p (q i t) n m -> p q t i n m", q=2, i=2, t=2)
            Evv = Ev.rearrange("p (q i t) n m -> p q t i n m", q=2, i=2, t=2)
            for hq in range(2):
                for h2 in range(2):
                    bi = 2 * hq + h2
                    self.act(exv[:, hq, h2, :, 0:16, :], psSv[bi][:, :, 0:16, :], AF.Exp, [], [b_psS[bi], b_exs], scale=0.125)
                    self.act(exv[0:8, hq, h2, :, 16, :], psSv[bi][0:8, :, 16, :], AF.Exp, [], [b_psS[bi], b_exs], scale=0.125)
                    self.tt("dve", PTv[:, hq, h2, :, 0:16, :], exv[:, hq, h2, :, 0:16, :], Evv[:, hq, h2, :, 0:16, :], ALU.mult, [b_E], [b_exs, b_PTs])
                    self.tt("dve", PTv[0:8, hq, h2, :, 16, :], exv[0:8, hq, h2, :, 16, :], Evv[0:8, hq, h2, :, 16, :], ALU.mult, [b_E], [b_exs, b_PTs])
            for hh in range(2):
                for h4 in range(4):
                    h = hh * 4 + h4
                    for nb in range(17):
                        Kk = 128 if nb < 16 else 8
                        self.mm(psO[hh][0:8, h4, :], PTs[0:Kk, h, nb, :], V1s[0:Kk, nb, h, :], nb == 0, nb == 16,
                                [b_PTs, b_V1s], [b_psO[hh]], last=(h4 == 3 and nb == 16))
                self.cp("dve", osb[:, hh * 4:(hh + 1) * 4, :], psO[hh][0:8], [], [b_psO[hh], b_osb])
            self.P.op("dve", lambda e: e.reciprocal(out=rec[:], in_=osb[:, :, 64]), [], [b_osb])
            self.tt("dve", ao[k][:], osb[:, :, 0:64], rec[:].unsqueeze(2).to_broadcast([8, 8, 64]), ALU.mult, [b_osb], [b_ao[k]])
            self.dma("sp", S["atts"].ap()[8 * s:8 * s + 8, :], ao[k][:].rearrange("p h e -> p (h e)"), [b_ao[k]], [])
        self.barrier()
        self.P.emit_phase()


KB.phase2s = _phase2s


def _phase1b(self):
    nc, P, NT, NTP, TP = self.nc, self.P, self.NT, self.NTP, self.TP
    I, O, S = self.i, self.o, self.s
    import os
    NLV = int(os.environ.get("RW_LEVELS", "6"))
    with contextlib.ExitStack() as st:
        sb = lambda name, shape, dt=F32: st.enter_context(nc.sbuf_tensor(name, list(shape), dt))
        ps = lambda name, shape, dt=F32: st.enter_context(nc.psum_tensor(name, list(shape), dt))
        identf = sb("r_identf", [128, 128]); identb = sb("r_identb", [128, 128], BF16)
        tri = sb("r_tri", [128, 128]); ones = sb("r_ones", [128, 128])
        M3 = sb("r_M3", [128, 3, 128]); M2 = sb("r_M2", [128, 2, 128])
        mu = sb("r_mu", [128, NSH])
        pb = {n: sb("r_pb_" + n, [128, 512]) for n in ("w0", "a0", "k_k", "k_a", "r_k", "lnx_g", "lnx_b")}
        wl2 = sb("r_wl2", [64, 512], BF16); al2 = sb("r_al2", [64, 512], BF16)
        gl2a = sb("r_gl2a", [128, 512], BF16); gl2b = sb("r_gl2b", [32, 512], BF16)
        pt = [sb(f"r_pt{i}", [128, NSH]) for i in range(2)]
        xs = [sb(f"r_xs{i}", [128, NSH]) for i in range(2)]
        lin = sb("r_lin", [128, 288], BF16); linT = sb("r_linT", [128, 4, 128], BF16)
        T = [sb(f"r_t{i}", [128, 512]) for i in range(11)]
        fm = sb("r_fm", [128, 3, 512], BF16)
        GR = sb("r_GR", [128, 8, 64], BF16)
        BK = sb("r_BK", [128, 2, 512], BF16)
        TA = sb("r_TA", [128, 4, 4, 128], BF16)
        HB = sb("r_HB", [128, 8, 5, 128], BF16)
        Pk = [sb(f"r_Pk{i}", [128, 8, 128], BF16) for i in range(2)]
        PkT = [sb(f"r_PkT{i}", [128, 8, 128], BF16) for i in range(2)]
        Tm = sb("r_Tm", [128, 8, 128], BF16)
        NG = sb("r_NG", [128, 8, 192], BF16)
        Dt = sb("r_Dt", [64, 8, 64]); PHI = sb("r_PHI", [64, 8, 64]); OM = sb("r_OM", [64, 8, 128])
        PSL = sb("r_PSL", [128, 8, 192])
        ST = [sb(f"r_ST{i}", [64, 8, 64]) for i in range(2)]
        rwb = [sb(f"r_rwb{i}", [128, 512], BF16) for i in range(2)]
        stt_ = sb("r_st", [128, 64])
        Fb = [ps(f"r_F{i}", [128, 512]) for i in range(6)]
        Hb = [ps(f"r_H{i}", [128, 8, 128], BF16) for i in range(2)]
        bF = [Buf() for _ in range(6)]; bH = [Buf(), Buf()]
        b_c = Buf(); b_pt = [Buf(), Buf()]; b_xs = [Buf(), Buf()]; b_lin = Buf(); b_linT = Buf()
        bT = [Buf() for _ in range(11)]
        b_fm, b_GR, b_BK, b_TA, b_HB, b_Tm, b_NG, b_Dt, b_PHI, b_OM, b_PSL, b_st = (Buf() for _ in range(12))
        b_Pk = [Buf(), Buf()]; b_PkT = [Buf(), Buf()]; b_ST = [Buf(), Buf()]; b_rwb = [Buf(), Buf()]

        cst = I["cst"].ap()
        self.dma("sp", identf[:], cst[:, C_ID, :], [], [b_c])
        self.dma("sp", tri[:], cst[:, C_TRI, :], [], [b_c])
        self.dma("sp", ones[:], cst[:, C_ONES, :], [], [b_c])
        self.dma("sp", M3[:, 0, :], cst[:, C_MUS, :], [], [b_c])
        self.dma("sp", M3[:, 1, :], cst[:, C_MUI, :], [], [b_c])
        self.dma("sp", M3[:, 2, :], cst[:, C_MUI, :], [], [b_c])
        self.dma("sp", M2[:, 0, :], cst[:, C_MLS, :], [], [b_c])
        self.dma("sp", M2[:, 1, :], cst[:, C_MLS, :], [], [b_c])
        self.dma("sp", mu[:], self.bcast_row(I["mu_shift"]), [], [b_c])
        for n in pb:
            self.dma("sp", pb[n][:], self.bcast_row(I[n]), [], [b_c])
        self.dma("pool", wl2[:], I["w_lora2"].ap(), [], [b_c])
        self.dma("pool", al2[:], I["a_lora2"].ap(), [], [b_c])
        self.dma("pool", gl2a[:], I["g_lora2"].ap()[0:128, :], [], [b_c])
        self.dma("pool", gl2b[:], I["g_lora2"].ap()[128:160, :], [], [b_c])
        self.cp("dve", identb[:], identf[:], [b_c], [b_c])
        self.memset("pool", ST[0][:], 0.0, [b_ST[0]])

        h3 = lambda ap: ap.rearrange("p (h e) -> p h e", e=64)
        bc8 = lambda ap: ap.unsqueeze(2).to_broadcast([128, 8, 64])

        def load(i):
            k = i % 2
            r0 = i * 128
            self.dma("sp", pt[k][:], S["p"].ap()[r0:r0 + 128, :], [], [b_pt[k]])
            if i == 0:
                self.memset("pool", xs[k][0:1, :], 0.0, [b_xs[k]])
                self.dma("sp", xs[k][1:128, :], S["p"].ap()[0:127, :], [], [b_xs[k]])
            elif i < NTP:
                self.dma("sp", xs[k][:], S["p"].ap()[r0 - 1:r0 + 127, :], [], [b_xs[k]])
            else:
                self.dma("sp", xs[k][1:128, :], S["p"].ap()[r0:r0 + 127, :], [], [b_xs[k]])
                for s in range(16):
                    self.dma("sp", xs[k][8 * s:8 * s + 1, :], I["sshift"].ap()[s:s + 1, :], [], [b_xs[k]])

        def rstage(i):
            k = i % 2
            X, Pt = xs[k], pt[k]
            self.tt("pool", X[:], X[:], Pt[:], ALU.subtract, [b_pt[k]], [b_xs[k]])
            self.tt("pool", X[:], X[:], mu[:], ALU.mult, [b_c], [b_xs[k]])
            self.tt("dve", X[:], X[:], Pt[:], ALU.add, [b_pt[k]], [b_xs[k]])
            self.act(lin[:, 0:64], X[:, 1536:1600], AF.Tanh, [b_xs[k]], [b_lin])
            self.act(lin[:, 64:128], X[:, 1600:1664], AF.Copy, [b_xs[k]], [b_lin])
            self.act(lin[:, 128:288], X[:, 1664:1824], AF.Sigmoid, [b_xs[k]], [b_lin])
            self.tr(Hb[0][0:64, 0, :], lin[:, 0:64], identb[:], [b_lin, b_c], [bH[0]], last=False)
            self.tr(Hb[0][0:64, 1, :], lin[:, 64:128], identb[:], [b_lin, b_c], [bH[0]], last=False)
            self.tr(Hb[0][:, 2, :], lin[:, 128:256], identb[:], [b_lin, b_c], [bH[0]], last=False)
            self.tr(Hb[0][0:32, 3, :], lin[:, 256:288], identb[:], [b_lin, b_c], [bH[0]], last=True)
            self.cp("dve", linT[0:64, 0:2, :], Hb[0][0:64, 0:2, :], [], [bH[0], b_linT])
            self.cp("dve", linT[:, 2, :], Hb[0][:, 2, :], [], [bH[0], b_linT])
            self.cp("dve", linT[0:32, 3, :], Hb[0][0:32, 3, :], [], [bH[0], b_linT])
            self.mm(Fb[0][:], linT[0:64, 0, :], wl2[:], True, True, [b_linT, b_c], [bF[0]])
            self.mm(Fb[1][:], linT[0:64, 1, :], al2[:], True, True, [b_linT, b_c], [bF[1]])
            self.mm(Fb[2][:], linT[:, 2, :], gl2a[:], True, False, [b_linT, b_c], [bF[2]])
            self.mm(Fb[2][:], linT[0:32, 3, :], gl2b[:], False, True, [b_linT, b_c], [bF[2]])
            r_, k_, v_ = X[:, 0:512], X[:, 512:1024], X[:, 1024:1536]
            self.tt("dve", T[0][:], Fb[0][:], pb["w0"][:], ALU.add, [b_c], [bF[0], bT[0]])
            self.act(T[0][:], T[0][:], AF.Sigmoid, [], [bT[0]])
            self.ts("pool", T[0][:], T[0][:], -math.exp(-0.5), None, ALU.mult, None, [], [bT[0]])
            self.tt("dve", T[1][:], Fb[1][:], pb["a0"][:], ALU.add, [b_c], [bF[1], bT[1]])
            self.act(T[1][:], T[1][:], AF.Sigmoid, [], [bT[1]])
            self.cp("act", T[6][:], Fb[2][:], [], [bF[2], bT[6]])
            self.tt("pool", T[2][:], k_, pb["k_k"][:], ALU.mult, [b_xs[k], b_c], [bT[2]])
            self.tt("dve", T[5][:], T[2][:], T[2][:], ALU.mult, [bT[2]], [bT[5]])
            self.red("dve", stt_[:, 0:8], h3(T[5][:]), [bT[5]], [b_st])
            self.ts("dve", stt_[:, 0:8], stt_[:, 0:8], 1e-24, None, ALU.max, None, [], [b_st])
            self.act(stt_[:, 8:16], stt_[:, 0:8], AF.Sqrt, [], [b_st])
            self.P.op("dve", lambda e: e.reciprocal(out=stt_[:, 0:8], in_=stt_[:, 8:16]), [], [b_st])
            self.tt("dve", h3(T[2][:]), h3(T[2][:]), bc8(stt_[:, 0:8]), ALU.mult, [b_st], [bT[2]])
            self.stt("dve", T[3][:], T[1][:], -1.0, pb["k_a"][:], ALU.add, ALU.mult, [bT[1], b_c], [bT[3]])
            self.stt("dve", T[3][:], T[3][:], 1.0, k_, ALU.add, ALU.mult, [b_xs[k]], [bT[3]])
            self.tt("pool", T[4][:], T[2][:], T[1][:], ALU.mult, [bT[2], bT[1]], [bT[4]])
            self.tt("dve", T[5][:], r_, T[3][:], ALU.mult, [b_xs[k], bT[3]], [bT[5]])
            self.tt("pool", T[5][:], T[5][:], pb["r_k"][:], ALU.mult, [b_c], [bT[5]])
            self.red("dve", stt_[:, 16:24], h3(T[5][:]), [bT[5]], [b_st])

        def post(i, ysrc, ybuf_r, ybuf_w):
            k = i % 2
            X = xs[k]
            v_ = X[:, 1024:1536]
            self.cp("act", T[8][:], ysrc, ybuf_r, ybuf_w + [bT[8]])
            self.red("dve", stt_[:, 24:32], h3(T[8][:]), [bT[8]], [b_st])
            self.ts("dve", stt_[:, 24:32], stt_[:, 24:32], 1.0 / 64, None, ALU.mult, None, [], [b_st])
            self.tt("dve", h3(T[8][:]), h3(T[8][:]), bc8(stt_[:, 24:32]), ALU.subtract, [b_st], [bT[8]])
            self.tt("pool", T[5][:], T[8][:], T[8][:], ALU.mult, [bT[8]], [bT[5]])
            self.red("dve", stt_[:, 32:40], h3(T[5][:]), [bT[5]], [b_st])
            self.act(stt_[:, 40:48], stt_[:, 32:40], AF.Sqrt, [], [b_st], scale=1.0 / 64, bias=GN_EPS)
            self.P.op("dve", lambda e: e.reciprocal(out=stt_[:, 32:40], in_=stt_[:, 40:48]), [], [b_st])
            self.tt("dve", h3(T[8][:]), h3(T[8][:]), bc8(stt_[:, 32:40]), ALU.mult, [b_st], [bT[8]])
            self.tt("pool", T[8][:], T[8][:], pb["lnx_g"][:], ALU.mult, [b_c], [bT[8]])
            self.tt("pool", T[8][:], T[8][:], pb["lnx_b"][:], ALU.add, [b_c], [bT[8]])
            self.tt("dve", h3(T[5][:]), h3(v_), bc8(stt_[:, 16:24]), ALU.mult, [b_xs[k], b_st], [bT[5]])
            self.tt("pool", T[8][:], T[8][:], T[5][:], ALU.add, [bT[5]], [bT[8]])
            self.tt("dve", rwb[k][:], T[8][:], T[6][:], ALU.mult, [bT[8], bT[6]], [b_rwb[k]])
            self.dma("sp", S["rw"].ap()[i * 128:(i + 1) * 128, :], rwb[k][:], [b_rwb[k]], [])

        load(0)
        cur = 0
        for i in range(NTP):
            k = i % 2
            load(i + 1)
            rstage(i)
            X = xs[k]
            r_, v_ = X[:, 0:512], X[:, 1024:1536]
            self.mm(Fb[3][:], tri[:], T[0][:], True, True, [b_c, bT[0]], [bF[3]])
            self.mm(Fb[4][:], ones[:], T[0][:], True, True, [b_c, bT[0]], [bF[4]])
            self.cp("act", T[9][:], Fb[4][:], [], [bF[4], bT[9]])
            self.cp("dve", T[7][:], Fb[3][:], [], [bF[3], bT[7]])
            self.act(T[8][:], T[7][:], AF.Exp, [bT[7]], [bT[8]])
            self.tt("dve", fm[:, 0, :], r_, T[8][:], ALU.mult, [b_xs[k], bT[8]], [b_fm])
            self.act(T[8][:], T[7][:], AF.Exp, [bT[7]], [bT[8]], scale=-1.0)
            self.tt("dve", fm[:, 1, :], T[3][:], T[8][:], ALU.mult, [bT[3], bT[8]], [b_fm])
            self.tt("pool", fm[:, 2, :], T[4][:], T[8][:], ALU.mult, [bT[4], bT[8]], [b_fm])
            self.tt("pool", T[10][:], T[7][:], T[0][:], ALU.subtract, [bT[7], bT[0]], [bT[10]])
            self.act(T[10][:], T[10][:], AF.Exp, [], [bT[10]])
            self.tt("dve", GR[:].rearrange("p h e -> p (h e)"), T[2][:], T[10][:], ALU.mult, [bT[2], bT[10]], [b_GR])
            self.tt("pool", T[10][:], T[9][:], T[7][:], ALU.subtract, [bT[9], bT[7]], [bT[10]])
            self.act(T[10][:], T[10][:], AF.Exp, [], [bT[10]])
            self.tt("dve", BK[:, 0, :], T[4][:], T[10][:], ALU.mult, [bT[4], bT[10]], [b_BK])
            self.tt("pool", BK[:, 1, :], T[3][:], T[10][:], ALU.mult, [bT[3], bT[10]], [b_BK])
            self.act(T[9][0:64, :], T[9][0:64, :], AF.Exp, [], [bT[9]])
            self.tt("dve", Dt[:], identf[0:64, 0:64].unsqueeze(1).to_broadcast([64, 8, 64]), T[9][0:64, :].rearrange("p (h e) -> p h e", e=64),
                    ALU.mult, [b_c, bT[9]], [b_Dt])
            for q in range(2):
                for hpi in range(2):
                    hp = 2 * q + hpi
                    srcs = [GR[:, 2 * hp:2 * hp + 2, :].rearrange("p h e -> p (h e)"), fm[:, 0, hp * 128:(hp + 1) * 128],
                            fm[:, 1, hp * 128:(hp + 1) * 128], fm[:, 2, hp * 128:(hp + 1) * 128]]
                    for w_, src in enumerate(srcs):
                        self.tr(Hb[q][:, hpi * 4 + w_, :], src, identb[:], [b_GR, b_fm, b_c], [bH[q]], last=(hpi == 1 and w_ == 3))
                self.cp("dve" if q == 0 else "act", TA[:, 2 * q:2 * q + 2, :, :].rearrange("p a b m -> p (a b) m"), Hb[q][:], [], [bH[q], b_TA])
            for hp in range(4):
                for h2 in range(2):
                    pr = slice(64 * h2, 64 * h2 + 64)
                    f1, f2 = Fb[2 * h2], Fb[2 * h2 + 1]
                    w1, w2 = bF[2 * h2], bF[2 * h2 + 1]
                    self.mm(f1[:, 0:256], TA[pr, hp, 3, :], TA[pr, hp, 0:2, :].rearrange("p a m -> p (a m)"), True, True, [b_TA], [w1], last=False)
                    self.mm(f1[:, 256:384], TA[pr, hp, 2, :], TA[pr, hp, 1, :], True, True, [b_TA], [w1], last=False)
                    self.mm(f2[:, 0:256], TA[pr, hp, 0, :], TA[pr, hp, 2:4, :].rearrange("p a m -> p (a m)"), True, True, [b_TA], [w2], last=(h2 == 1))
                for h2 in range(2):
                    h = 2 * hp + h2
                    self.tt("dve", HB[:, h, 0:3, :].rearrange("p a m -> p (a m)"), Fb[2 * h2][:, 0:384], M3[:].rearrange("p a m -> p (a m)"), ALU.mult,
                            [b_c], [bF[2 * h2], b_HB])
                    self.tt("dve", HB[:, h, 3:5, :].rearrange("p a m -> p (a m)"), Fb[2 * h2 + 1][:, 0:256], M2[:].rearrange("p a m -> p (a m)"), ALU.mult,
                            [b_c], [bF[2 * h2 + 1], b_HB])
            self.tt("pool", Tm[:], identb[:].unsqueeze(1).to_broadcast([128, 8, 128]), HB[:, :, 0, :], ALU.subtract, [b_c, b_HB], [b_Tm])
            pk_r = lambda hh: HB[:, hh, 0, :]
            pkT_r = lambda hh: HB[:, hh, 4, :]
            bk_r, bkT_r = b_HB, b_HB
            for lv in range(NLV):
                pw = lv % 2
                lastlv = (lv == NLV - 1)
                for g in range(2):
                    fa, fb, fc = Fb[3 * g], Fb[3 * g + 1], Fb[3 * g + 2]
                    wa, wb, wc = bF[3 * g], bF[3 * g + 1], bF[3 * g + 2]
                    for h4 in range(4):
                        hh = 4 * g + h4
                        if not lastlv:
                            self.mm(fa[:, h4 * 128:(h4 + 1) * 128], pkT_r(hh), pk_r(hh), True, True, [bk_r, bkT_r], [wa], last=(h4 == 3))
                    for h4 in range(4):
                        hh = 4 * g + h4
                        self.mm(fb[:, h4 * 128:(h4 + 1) * 128], pk_r(hh), pkT_r(hh), True, True, [bk_r, bkT_r], [wb], last=(h4 == 3))
                    if not lastlv:
                        self.cp("act", Pk[pw][:, 4 * g:4 * g + 4, :].rearrange("p h m -> p (h m)"), fa[:], [], [wa, b_Pk[pw]])
                    self.cp("dve", PkT[pw][:, 4 * g:4 * g + 4, :].rearrange("p h m -> p (h m)"), fb[:], [], [wb, b_PkT[pw]])
                    for h4 in range(4):
                        hh = 4 * g + h4
                        self.mm(fc[:, h4 * 128:(h4 + 1) * 128], PkT[pw][:, hh, :], Tm[:, hh, :], True, True, [b_PkT[pw], b_Tm], [wc], last=(h4 == 3))
                    self.tt("dve", Tm[:, 4 * g:4 * g + 4, :].rearrange("p h m -> p (h m)"), fc[:], Tm[:, 4 * g:4 * g + 4, :].rearrange("p h m -> p (h m)"),
                            ALU.add, [], [wc, b_Tm])
                pk_r = (lambda hh, pw=pw: Pk[pw][:, hh, :])
                pkT_r = (lambda hh, pw=pw: PkT[pw][:, hh, :])
                bk_r, bkT_r = b_Pk[pw], b_PkT[pw]
            for hq in range(4):
                f = Fb[hq % 2]
                w = bF[hq % 2]
                for h2 in range(2):
                    h = 2 * hq + h2
                    self.mm(f[:, h2 * 192:h2 * 192 + 64], Tm[:, h, :], GR[:, h, :], True, True, [b_Tm, b_GR], [w], last=False)
                    self.mm(f[:, h2 * 192 + 64:h2 * 192 + 192], Tm[:, h, :], HB[:, h, 3, :], True, True, [b_Tm, b_HB], [w], last=(h2 == 1))
                self.ts("dve", NG[:, 2 * hq:2 * hq + 2, :].rearrange("p h m -> p (h m)"), f[:, 0:384], -1.0, None, ALU.mult, None, [], [w, b_NG])
            for hq in range(4):
                fo, fl = Fb[2 + (hq % 2) * 2], Fb[3 + (hq % 2) * 2]
                wo, wl = bF[2 + (hq % 2) * 2], bF[3 + (hq % 2) * 2]
                for h2 in range(2):
                    h = 2 * hq + h2
                    pr = slice(64 * h2, 64 * h2 + 64)
                    c0 = h2 * 192
                    self.mm(fo[0:64, c0:c0 + 64], NG[:, h, 0:64], BK[:, 0, h * 64:(h + 1) * 64], True, True, [b_NG, b_BK], [wo], last=False)
                    self.mm(fo[0:64, c0 + 64:c0 + 192], NG[:, h, 0:64], HB[:, h, 1, :], True, False, [b_NG, b_HB], [wo], last=False)
                    self.mm(fo[0:64, c0 + 64:c0 + 192], identb[pr, pr], TA[pr, hq, 1, :], False, True, [b_c, b_TA], [wo], last=(h2 == 1))
                for h2 in range(2):
                    h = 2 * hq + h2
                    c0 = h2 * 192
                    self.mm(fl[:, c0:c0 + 64], NG[:, h, 64:192], BK[:, 0, h * 64:(h + 1) * 64], True, False, [b_NG, b_BK], [wl], last=False)
                    self.mm(fl[:, c0:c0 + 64], identb[:], BK[:, 1, h * 64:(h + 1) * 64], False, True, [b_c, b_BK], [wl], last=False)
                    self.mm(fl[:, c0 + 64:c0 + 192], NG[:, h, 64:192], HB[:, h, 1, :], True, False, [b_NG, b_HB], [wl], last=False)
                    self.mm(fl[:, c0 + 64:c0 + 192], identb[:], HB[:, h, 2, :], False, True, [b_c, b_HB], [wl], last=(h2 == 1))
                fo3 = fo[0:64, 0:384].rearrange("p (h m) -> p h m", m=192)
                self.tt("dve", PHI[:, 2 * hq:2 * hq + 2, :], fo3[:, :, 0:64], Dt[:, 2 * hq:2 * hq + 2, :], ALU.add, [b_Dt], [wo, b_PHI])
                self.cp("dve", OM[:, 2 * hq:2 * hq + 2, :], fo3[:, :, 64:192], [], [wo, b_OM])
                self.cp("act", PSL[:, 2 * hq:2 * hq + 2, :].rearrange("p h m -> p (h m)"), fl[:, 0:384], [], [wl, b_PSL])
            nxt = 1 - cur
            for h in range(8):
                self.mm(Fb[0][:, h * 64:(h + 1) * 64], OM[:, h, :], ST[cur][:, h, :], True, False, [b_OM, b_ST[cur]], [bF[0]], last=False)
                self.mm(Fb[0][:, h * 64:(h + 1) * 64], PSL[:, h, 64:192], v_[:, h * 64:(h + 1) * 64], False, True, [b_PSL, b_xs[k]], [bF[0]], last=(h == 7))
            for h in range(8):
                self.mm(Fb[1][0:64, h * 64:(h + 1) * 64], PHI[:, h, :], ST[cur][:, h, :], True, False, [b_PHI, b_ST[cur]], [bF[1]], last=False)
                self.mm(Fb[1][0:64, h * 64:(h + 1) * 64], PSL[:, h, 0:64], v_[:, h * 64:(h + 1) * 64], False, True, [b_PSL, b_xs[k]], [bF[1]], last=(h == 7))
            self.cp("dve", ST[nxt][:].rearrange("p h m -> p (h m)"), Fb[1][0:64, :], [], [bF[1], b_ST[nxt]])
            cur = nxt
            post(i, Fb[0][:], [], [bF[0]])
        for h in range(8):
            self.mm(Fb[2][0:64, h * 64:(h + 1) * 64], ST[cur][:, h, :], identf[0:64, 0:64], True, True, [b_ST[cur], b_c], [bF[2]], last=(h == 7))
        self.cp("dve", PHI[:].rearrange("p h m -> p (h m)"), Fb[2][0:64, :], [], [bF[2], b_PHI])
        self.dma("sp", bass.AP(O["wkvp"], 0, [[64, 64], [4096, 8], [1, 64]]), PHI[:], [b_PHI], [])
        self.rwkv_sample(st, locals())
        self.barrier()
        self.P.emit_phase()


KB.phase1b = _phase1b


def _rwkv_sample(self, st, L):
    nc, NTP = self.nc, self.NTP
    I, O, S = self.i, self.o, self.s
    sb = lambda name, shape, dt=F32: st.enter_context(nc.sbuf_tensor(name, list(shape), dt))
    T, bT, xs, b_xs, rstage, post = L["T"], L["bT"], L["xs"], L["b_xs"], L["rstage"], L["post"]
    i = NTP
    k = i % 2
    Ssb = sb("r_Ssb", [128, 4096]); tmp = sb("r_tmp", [128, 4096])
    vec = sb("r_vec", [128, 6, 8, 64]); ysb = sb("r_ysb", [128, 8, 64]); ytm = sb("r_ytm", [128, 512]); sk = sb("r_sk", [128, 64])
    b_S, b_tmp, b_vec, b_ys, b_ytm, b_sk, b_rs, b_yscr = (Buf() for _ in range(8))
    self.dma("sp", Ssb[:], I["swkv"].ap(), [], [b_S])
    rstage(i)
    self.act(T[0][:], T[0][:], AF.Exp, [], [bT[0]])
    X = xs[k]
    srcs = [(X[:, 0:512], b_xs[k]), (T[0][:], bT[0]), (T[3][:], bT[3]), (X[:, 1024:1536], b_xs[k]), (T[2][:], bT[2]), (T[4][:], bT[4])]
    for q, (ap, bb) in enumerate(srcs):
        self.dma("sp", S["rs"].ap()[:, q * 512:(q + 1) * 512], ap, [bb], [b_rs])
    for q in range(6):
        for s in range(16):
            self.dma("sp", vec[8 * s:8 * s + 8, q, :, :], bass.AP(S["rs"], s * 8 * 3072 + q * 512, [[64, 8], [3072, 8], [1, 64]]), [b_rs], [b_vec])
    S3 = Ssb[:].rearrange("p (i j) -> p i j", j=64)
    t3 = tmp[:].rearrange("p (i j) -> p i j", j=64)
    bi = lambda ap: ap.unsqueeze(1).to_broadcast([128, 64, 64])
    bj = lambda ap: ap.unsqueeze(2).to_broadcast([128, 64, 64])
    for t in range(8):
        r_t, w_t, k_t, v_t, kap_t, b_t = (vec[:, q, t, :] for q in range(6))
        self.tt("dve", t3, S3, bi(kap_t), ALU.mult, [b_S, b_vec], [b_tmp])
        self.red("dve", sk[:], t3, [b_tmp], [b_sk])
        self.tt("pool", S3, S3, bi(w_t), ALU.mult, [b_vec], [b_S])
        self.tt("dve", t3, bj(sk[:]), bi(b_t), ALU.mult, [b_sk, b_vec], [b_tmp])
        self.tt("pool", S3, S3, t3, ALU.subtract, [b_tmp], [b_S])
        self.tt("dve", t3, bj(v_t), bi(k_t), ALU.mult, [b_vec], [b_tmp])
        self.tt("pool", S3, S3, t3, ALU.add, [b_tmp], [b_S])
        self.tt("dve", t3, S3, bi(r_t), ALU.mult, [b_S, b_vec], [b_tmp])
        self.red("dve", ysb[:, t, :], t3, [b_tmp], [b_ys])
    self.dma("sp", O["wkvs"].ap(), Ssb[:], [b_S], [])
    for s in range(16):
        self.dma("sp", bass.AP(S["ys"], s * 8 * 512, [[64, 8], [512, 8], [1, 64]]), ysb[8 * s:8 * s + 8, :, :], [b_ys], [b_yscr])
    self.dma("sp", ytm[:], S["ys"].ap(), [b_yscr], [b_ytm])
    post(i, ytm[:], [b_ytm], [])


KB.rwkv_sample = _rwkv_sample


class _NS:
    pass


def _phase1b_v2(self):
    nc, P, NT, NTP, TP = self.nc, self.P, self.NT, self.NTP, self.TP
    I, O, S = self.i, self.o, self.s
    import os
    NLV = int(os.environ.get("RW_LEVELS", "6"))
    h3 = lambda ap: ap.rearrange("p (h e) -> p h e", e=64)
    bc8 = lambda ap: ap.unsqueeze(2).to_broadcast([128, 8, 64])
    with contextlib.ExitStack() as st0:
        sb0 = lambda name, shape, dt=F32: st0.enter_context(nc.sbuf_tensor(name, list(shape), dt))
        ps = lambda name, shape, dt=F32: st0.enter_context(nc.psum_tensor(name, list(shape), dt))
        identf = sb0("r_identf", [128, 128]); identb = sb0("r_identb", [128, 128], BF16)
        tri = sb0("r_tri", [128, 128]); ones = sb0("r_ones", [128, 128])
        M3 = sb0("r_M3", [128, 3, 128]); M2 = sb0("r_M2", [128, 2, 128])
        mu = sb0("r_mu", [128, NSH])
        pb = {n: sb0("r_pb_" + n, [128, 512]) for n in ("w0", "a0", "k_k", "k_a", "r_k", "lnx_g", "lnx_b")}
        wl2 = sb0("r_wl2", [64, 512], BF16); al2 = sb0("r_al2", [64, 512], BF16)
        gl2a = sb0("r_gl2a", [128, 512], BF16); gl2b = sb0("r_gl2b", [32, 512], BF16)
        ST = [sb0(f"r_ST{i}", [64, 8, 64]) for i in range(2)]
        Fb = [ps(f"r_F{i}", [128, 512]) for i in range(6)]
        Hb = [ps(f"r_H{i}", [128, 8, 128], BF16) for i in range(2)]
        bF = [Buf() for _ in range(6)]; bH = [Buf(), Buf()]
        b_c = Buf(); b_ST = [Buf(), Buf()]
        cst = I["cst"].ap()
        self.dma("sp", identf[:], cst[:, C_ID, :], [], [b_c])
        self.dma("sp", tri[:], cst[:, C_TRI, :], [], [b_c])
        self.dma("sp", ones[:], cst[:, C_ONES, :], [], [b_c])
        for j_, cc in enumerate((C_MUS, C_MUI, C_MUI)):
            self.dma("sp", M3[:, j_, :], cst[:, cc, :], [], [b_c])
        for j_ in range(2):
            self.dma("sp", M2[:, j_, :], cst[:, C_MLS, :], [], [b_c])
        self.dma("sp", mu[:], self.bcast_row(I["mu_shift"]), [], [b_c])
        for n in pb:
            self.dma("sp", pb[n][:], self.bcast_row(I[n]), [], [b_c])
        self.dma("pool", wl2[:], I["w_lora2"].ap(), [], [b_c])
        self.dma("pool", al2[:], I["a_lora2"].ap(), [], [b_c])
        self.dma("pool", gl2a[:], I["g_lora2"].ap()[0:128, :], [], [b_c])
        self.dma("pool", gl2b[:], I["g_lora2"].ap()[128:160, :], [], [b_c])
        self.cp("dve", identb[:], identf[:], [b_c], [b_c])
        self.memset("pool", ST[0][:], 0.0, [b_ST[0]])

        def mkset(stk, tag, par=0):
            sb = lambda name, shape, dt=F32: stk.enter_context(nc.sbuf_tensor(f"r{tag}_{name}", list(shape), dt))
            z = _NS()
            z.F = [Fb[3 * par + j_] for j_ in range(3)]; z.bF = [bF[3 * par + j_] for j_ in range(3)]
            z.H = Hb[par]; z.bH = bH[par]
            z.pt = sb("pt", [128, NSH]); z.xs = sb("xs", [128, NSH])
            z.lin = sb("lin", [128, 288], BF16); z.linT = sb("linT", [128, 4, 128], BF16)
            z.T = [sb(f"t{i}", [128, 512]) for i in range(11)]
            z.fm = sb("fm", [128, 3, 512], BF16); z.GR = sb("GR", [128, 8, 64], BF16); z.BK = sb("BK", [128, 2, 512], BF16)
            z.TA = sb("TA", [128, 4, 4, 128], BF16); z.HB = sb("HB", [128, 8, 5, 128], BF16)
            z.Pk = [sb(f"Pk{i}", [128, 8, 128], BF16) for i in range(2)]
            z.PkT = [sb(f"PkT{i}", [128, 8, 128], BF16) for i in range(2)]
            z.Tm = sb("Tm", [128, 8, 128], BF16); z.NG = sb("NG", [128, 8, 192], BF16)
            z.Dt = sb("Dt", [64, 8, 64]); z.PHI = sb("PHI", [64, 8, 64]); z.OM = sb("OM", [64, 8, 128])
            z.PSL = z.pt[:, 0:1536].rearrange("p (h m) -> p h m", m=192)
            z.rwb = sb("rwb", [128, 512], BF16); z.st = sb("st", [128, 64])
            z.b_pt, z.b_xs, z.b_lin, z.b_linT = Buf(), Buf(), Buf(), Buf()
            z.bT = [Buf() for _ in range(11)]
            (z.b_fm, z.b_GR, z.b_BK, z.b_TA, z.b_HB, z.b_Tm, z.b_NG, z.b_Dt, z.b_PHI, z.b_OM, z.b_PSL, z.b_st, z.b_rwb) = (Buf() for _ in range(13))
            z.b_PSL = z.b_pt
            z.b_Pk = [Buf(), Buf()]; z.b_PkT = [Buf(), Buf()]
            return z

        def load(i, z):
            r0 = i * 128
            self.dma("sp", z.pt[:], S["p"].ap()[r0:r0 + 128, :], [], [z.b_pt])
            if i == 0:
                self.memset("pool", z.xs[0:1, :], 0.0, [z.b_xs])
                self.dma("sp", z.xs[1:128, :], S["p"].ap()[0:127, :], [], [z.b_xs])
            elif i < NTP:
                self.dma("sp", z.xs[:], S["p"].ap()[r0 - 1:r0 + 127, :], [], [z.b_xs])
            else:
                self.dma("sp", z.xs[1:128, :], S["p"].ap()[r0:r0 + 127, :], [], [z.b_xs])
                z.b_xrows = [Buf() for _ in range(16)]
                for s in range(16):
                    self.dma("sp", z.xs[8 * s:8 * s + 1, :], I["sshift"].ap()[s:s + 1, :], [z.b_xs], [z.b_xrows[s]])

        def rstage(i, z):
            X, Pt, T, bT, stt_ = z.xs, z.pt, z.T, z.bT, z.st
            self.tt("dve", X[:], X[:], Pt[:], ALU.subtract, [z.b_pt] + getattr(z, "b_xrows", []), [z.b_xs])
            yield
            self.tt("dve", X[:], X[:], mu[:], ALU.mult, [b_c], [z.b_xs])
            yield
            self.tt("dve", X[:], X[:], Pt[:], ALU.add, [z.b_pt], [z.b_xs])
            self.act(z.lin[:, 0:64], X[:, 1536:1600], AF.Tanh, [z.b_xs], [z.b_lin])
            self.act(z.lin[:, 64:128], X[:, 1600:1664], AF.Copy, [z.b_xs], [z.b_lin])
            self.act(z.lin[:, 128:288], X[:, 1664:1824], AF.Sigmoid, [z.b_xs], [z.b_lin])
            yield
            self.tr(z.H[0:64, 0, :], z.lin[:, 0:64], identb[:], [z.b_lin, b_c], [z.bH], last=False)
            self.tr(z.H[0:64, 1, :], z.lin[:, 64:128], identb[:], [z.b_lin, b_c], [z.bH], last=False)
            self.tr(z.H[:, 2, :], z.lin[:, 128:256], identb[:], [z.b_lin, b_c], [z.bH], last=False)
            self.tr(z.H[0:32, 3, :], z.lin[:, 256:288], identb[:], [z.b_lin, b_c], [z.bH], last=True)
            self.cp("dve", z.linT[0:64, 0:2, :], z.H[0:64, 0:2, :], [], [z.bH, z.b_linT])
            self.cp("dve", z.linT[:, 2, :], z.H[:, 2, :], [], [z.bH, z.b_linT])
            self.cp("dve", z.linT[0:32, 3, :], z.H[0:32, 3, :], [], [z.bH, z.b_linT])
            yield
            self.mm(z.F[0][:], z.linT[0:64, 0, :], wl2[:], True, True, [z.b_linT, b_c], [z.bF[0]])
            self.mm(z.F[1][:], z.linT[0:64, 1, :], al2[:], True, True, [z.b_linT, b_c], [z.bF[1]])
            self.mm(z.F[2][:], z.linT[:, 2, :], gl2a[:], True, False, [z.b_linT, b_c], [z.bF[2]])
            self.mm(z.F[2][:], z.linT[0:32, 3, :], gl2b[:], False, True, [z.b_linT, b_c], [z.bF[2]])
            r_, k_, v_ = X[:, 0:512], X[:, 512:1024], X[:, 1024:1536]
            self.tt("dve", T[0][:], z.F[0][:], pb["w0"][:], ALU.add, [b_c], [z.bF[0], bT[0]])
            self.tt("dve", T[1][:], z.F[1][:], pb["a0"][:], ALU.add, [b_c], [z.bF[1], bT[1]])
            self.cp("act", T[6][:], z.F[2][:], [], [z.bF[2], bT[6]])
            yield
            self.act(T[0][:], T[0][:], AF.Sigmoid, [], [bT[0]])
            self.act(T[1][:], T[1][:], AF.Sigmoid, [], [bT[1]])
            self.ts("dve", T[0][:], T[0][:], -math.exp(-0.5), None, ALU.mult, None, [], [bT[0]])
            self.tt("pool", T[2][:], k_, pb["k_k"][:], ALU.mult, [z.b_xs, b_c], [bT[2]])
            yield
            self.tt("dve", T[5][:], T[2][:], T[2][:], ALU.mult, [bT[2]], [bT[5]])
            self.red("dve", stt_[:, 0:8], h3(T[5][:]), [bT[5]], [z.b_st])
            self.ts("dve", stt_[:, 0:8], stt_[:, 0:8], 1e-24, None, ALU.max, None, [], [z.b_st])
            self.act(stt_[:, 8:16], stt_[:, 0:8], AF.Sqrt, [], [z.b_st])
            yield
            self.P.op("dve", lambda e: e.reciprocal(out=stt_[:, 0:8], in_=stt_[:, 8:16]), [], [z.b_st])
            self.tt("dve", h3(T[2][:]), h3(T[2][:]), bc8(stt_[:, 0:8]), ALU.mult, [z.b_st], [bT[2]])
            self.stt("dve", T[3][:], T[1][:], -1.0, pb["k_a"][:], ALU.add, ALU.mult, [bT[1], b_c], [bT[3]])
            yield
            self.stt("dve", T[3][:], T[3][:], 1.0, k_, ALU.add, ALU.mult, [z.b_xs], [bT[3]])
            self.tt("pool", T[4][:], T[2][:], T[1][:], ALU.mult, [bT[2], bT[1]], [bT[4]])
            yield
            self.tt("dve", T[5][:], r_, T[3][:], ALU.mult, [z.b_xs, bT[3]], [bT[5]])
            self.tt("pool", T[5][:], T[5][:], pb["r_k"][:], ALU.mult, [b_c], [bT[5]])
            self.red("dve", stt_[:, 16:24], h3(T[5][:]), [bT[5]], [z.b_st])
            yield

        def post(i, z, ysrc, ybuf_r, ybuf_w):
            T, bT, stt_ = z.T, z.bT, z.st
            v_ = z.xs[:, 1024:1536]
            self.cp("act", T[8][:], ysrc, ybuf_r, ybuf_w + [bT[8]])
            self.red("dve", stt_[:, 24:32], h3(T[8][:]), [bT[8]], [z.b_st])
            self.ts("dve", stt_[:, 24:32], stt_[:, 24:32], 1.0 / 64, None, ALU.mult, None, [], [z.b_st])
            self.tt("dve", h3(T[8][:]), h3(T[8][:]), bc8(stt_[:, 24:32]), ALU.subtract, [z.b_st], [bT[8]])
            self.tt("pool", T[5][:], T[8][:], T[8][:], ALU.mult, [bT[8]], [bT[5]])
            self.red("dve", stt_[:, 32:40], h3(T[5][:]), [bT[5]], [z.b_st])
            self.act(stt_[:, 40:48], stt_[:, 32:40], AF.Sqrt, [], [z.b_st], scale=1.0 / 64, bias=GN_EPS)
            self.P.op("dve", lambda e: e.reciprocal(out=stt_[:, 32:40], in_=stt_[:, 40:48]), [], [z.b_st])
            self.tt("dve", h3(T[8][:]), h3(T[8][:]), bc8(stt_[:, 32:40]), ALU.mult, [z.b_st], [bT[8]])
            self.tt("pool", T[8][:], T[8][:], pb["lnx_g"][:], ALU.mult, [b_c], [bT[8]])
            self.tt("pool", T[8][:], T[8][:], pb["lnx_b"][:], ALU.add, [b_c], [bT[8]])
            self.tt("dve", h3(T[5][:]), h3(v_), bc8(stt_[:, 16:24]), ALU.mult, [z.b_xs, z.b_st], [bT[5]])
            self.tt("pool", T[8][:], T[8][:], T[5][:], ALU.add, [bT[5]], [bT[8]])
            self.tt("dve", z.rwb[:], T[8][:], T[6][:], ALU.mult, [bT[8], bT[6]], [z.b_rwb])
            self.dma("pool", S["rw"].ap()[i * 128:(i + 1) * 128, :], z.rwb[:], [z.b_rwb], [])

        def xstage(i, z):
            load(i, z)
            yield
            yield from rstage(i, z)
            X, T, bT = z.xs, z.T, z.bT
            r_ = X[:, 0:512]
            self.mm(z.F[0][:], tri[:], T[0][:], True, True, [b_c, bT[0]], [z.bF[0]])
            self.mm(z.F[1][:], ones[:], T[0][:], True, True, [b_c, bT[0]], [z.bF[1]])
            self.cp("act", T[9][:], z.F[1][:], [], [z.bF[1], bT[9]])
            self.cp("dve", T[7][:], z.F[0][:], [], [z.bF[0], bT[7]])
            yield
            self.act(T[8][:], T[7][:], AF.Exp, [bT[7]], [bT[8]])
            self.tt("pool", T[10][:], T[7][:], T[0][:], ALU.subtract, [bT[7], bT[0]], [bT[10]])
            yield
            self.tt("dve", z.fm[:, 0, :], r_, T[8][:], ALU.mult, [z.b_xs, bT[8]], [z.b_fm])
            self.act(T[8][:], T[7][:], AF.Exp, [bT[7]], [bT[8]], scale=-1.0)
            self.act(T[10][:], T[10][:], AF.Exp, [], [bT[10]])
            yield
            self.tt("dve", z.fm[:, 1, :], T[3][:], T[8][:], ALU.mult, [bT[3], bT[8]], [z.b_fm])
            self.tt("pool", z.fm[:, 2, :], T[4][:], T[8][:], ALU.mult, [bT[4], bT[8]], [z.b_fm])
            self.tt("dve", z.GR[:].rearrange("p h e -> p (h e)"), T[2][:], T[10][:], ALU.mult, [bT[2], bT[10]], [z.b_GR])
            yield
            self.tt("pool", T[10][:], T[9][:], T[7][:], ALU.subtract, [bT[9], bT[7]], [bT[10]])
            self.act(T[10][:], T[10][:], AF.Exp, [], [bT[10]])
            self.act(T[9][0:64, :], T[9][0:64, :], AF.Exp, [], [bT[9]])
            yield
            self.tt("dve", z.BK[:, 0, :], T[4][:], T[10][:], ALU.mult, [bT[4], bT[10]], [z.b_BK])
            self.tt("pool", z.BK[:, 1, :], T[3][:], T[10][:], ALU.mult, [bT[3], bT[10]], [z.b_BK])
            self.tt("dve", z.Dt[:], identf[0:64, 0:64].unsqueeze(1).to_broadcast([64, 8, 64]), T[9][0:64, :].rearrange("p (h e) -> p h e", e=64),
                    ALU.mult, [b_c, bT[9]], [z.b_Dt])
            yield
            for q in range(2):
                for hpi in range(2):
                    hp = 2 * q + hpi
                    srcs = [z.GR[:, 2 * hp:2 * hp + 2, :].rearrange("p h e -> p (h e)"), z.fm[:, 0, hp * 128:(hp + 1) * 128],
                            z.fm[:, 1, hp * 128:(hp + 1) * 128], z.fm[:, 2, hp * 128:(hp + 1) * 128]]
                    for w_, src in enumerate(srcs):
                        self.tr(z.H[:, hpi * 4 + w_, :], src, identb[:], [z.b_GR, z.b_fm, b_c], [z.bH], last=(hpi == 1 and w_ == 3))
                self.cp("dve" if q == 0 else "act", z.TA[:, 2 * q:2 * q + 2, :, :].rearrange("p a b m -> p (a b) m"), z.H[:], [], [z.bH, z.b_TA])
                yield
            TA, HB = z.TA, z.HB
            FX, FY, FZ = z.F
            wX, wY, wZ = z.bF
            MLS_ = M2[:, 0, :]
            for hp in range(4):
                p0, p1 = slice(0, 64), slice(64, 128)
                self.mm(FX[:, 0:256], TA[p0, hp, 3, :], TA[p0, hp, 0:2, :].rearrange("p a m -> p (a m)"), True, True, [z.b_TA], [wX], last=False)
                self.mm(FY[:, 0:256], TA[p1, hp, 3, :], TA[p1, hp, 0:2, :].rearrange("p a m -> p (a m)"), True, True, [z.b_TA], [wY], last=False)
                self.mm(FX[:, 256:384], TA[p0, hp, 2, :], TA[p0, hp, 1, :], True, True, [z.b_TA], [wX], last=False)
                self.mm(FY[:, 256:384], TA[p1, hp, 2, :], TA[p1, hp, 1, :], True, True, [z.b_TA], [wY], last=False)
                self.mm(FZ[:, 0:256], TA[p0, hp, 0, :], TA[p0, hp, 2:4, :].rearrange("p a m -> p (a m)"), True, True, [z.b_TA], [wZ], last=False)
                self.mm(FY[:, 384:512], TA[p1, hp, 0, :], TA[p1, hp, 3, :], True, True, [z.b_TA], [wY], last=False)
                self.mm(FX[:, 384:512], TA[p1, hp, 0, :], TA[p1, hp, 2, :], True, True, [z.b_TA], [wX], last=True)
                h0_, h1_ = 2 * hp, 2 * hp + 1
                m3 = M3[:].rearrange("p a m -> p (a m)")
                self.tt("dve", HB[:, h0_, 0:3, :].rearrange("p a m -> p (a m)"), FX[:, 0:384], m3, ALU.mult, [b_c], [wX, z.b_HB])
                self.tt("dve", HB[:, h0_, 3:5, :].rearrange("p a m -> p (a m)"), FZ[:, 0:256], M2[:].rearrange("p a m -> p (a m)"), ALU.mult, [b_c], [wZ, z.b_HB])
                self.tt("dve", HB[:, h1_, 0:3, :].rearrange("p a m -> p (a m)"), FY[:, 0:384], m3, ALU.mult, [b_c], [wY, z.b_HB])
                self.tt("dve", HB[:, h1_, 3, :], FX[:, 384:512], MLS_, ALU.mult, [b_c], [wX, z.b_HB])
                self.tt("dve", HB[:, h1_, 4, :], FY[:, 384:512], MLS_, ALU.mult, [b_c], [wY, z.b_HB])
                yield
            Tm, Pk, PkT = z.Tm, z.Pk, z.PkT
            self.tt("pool", Tm[:], identb[:].unsqueeze(1).to_broadcast([128, 8, 128]), HB[:, :, 0, :], ALU.subtract, [b_c, z.b_HB], [z.b_Tm])
            pk_r = lambda hh: HB[:, hh, 0, :]
            pkT_r = lambda hh: HB[:, hh, 4, :]
            bk_r, bkT_r = z.b_HB, z.b_HB
            for lv in range(NLV):
                pw = lv % 2
                lastlv = (lv == NLV - 1)
                for g in range(2):
                    fa, fb, fc = z.F
                    wa, wb, wc = z.bF
                    if not lastlv:
                        for h4 in range(4):
                            hh = 4 * g + h4
                            self.mm(fa[:, h4 * 128:(h4 + 1) * 128], pkT_r(hh), pk_r(hh), True, True, [bk_r, bkT_r], [wa], last=(h4 == 3))
                    for h4 in range(4):
                        hh = 4 * g + h4
                        self.mm(fb[:, h4 * 128:(h4 + 1) * 128], pk_r(hh), pkT_r(hh), True, True, [bk_r, bkT_r], [wb], last=(h4 == 3))
                    if not lastlv:
                        self.cp("act", Pk[pw][:, 4 * g:4 * g + 4, :].rearrange("p h m -> p (h m)"), fa[:], [], [wa, z.b_Pk[pw]])
                    self.cp("dve" if (g == 0 or not lastlv) and g == 0 else "act", PkT[pw][:, 4 * g:4 * g + 4, :].rearrange("p h m -> p (h m)"), fb[:], [], [wb, z.b_PkT[pw]])
                    yield
                    for h4 in range(4):
                        hh = 4 * g + h4
                        self.mm(fc[:, h4 * 128:(h4 + 1) * 128], PkT[pw][:, hh, :], Tm[:, hh, :], True, True, [z.b_PkT[pw], z.b_Tm], [wc], last=(h4 == 3))
                    self.tt("dve", Tm[:, 4 * g:4 * g + 4, :].rearrange("p h m -> p (h m)"), fc[:], Tm[:, 4 * g:4 * g + 4, :].rearrange("p h m -> p (h m)"),
                            ALU.add, [], [wc, z.b_Tm])
                    yield
                pk_r = (lambda hh, pw=pw: Pk[pw][:, hh, :])
                pkT_r = (lambda hh, pw=pw: PkT[pw][:, hh, :])
                bk_r, bkT_r = z.b_Pk[pw], z.b_PkT[pw]
            NG, GR, BK = z.NG, z.GR, z.BK
            for hq in range(4):
                f = z.F[hq % 2]
                w = z.bF[hq % 2]
                for h2 in range(2):
                    h = 2 * hq + h2
                    self.mm(f[:, h2 * 192:h2 * 192 + 64], Tm[:, h, :], GR[:, h, :], True, True, [z.b_Tm, z.b_GR], [w], last=False)
                    self.mm(f[:, h2 * 192 + 64:h2 * 192 + 192], Tm[:, h, :], HB[:, h, 3, :], True, True, [z.b_Tm, z.b_HB], [w], last=(h2 == 1))
                self.ts("dve", NG[:, 2 * hq:2 * hq + 2, :].rearrange("p h m -> p (h m)"), f[:, 0:384], -1.0, None, ALU.mult, None, [], [w, z.b_NG])
                yield
            for hq in range(4):
                fo, fl = z.F[(2 * hq + 2) % 3], z.F[(2 * hq + 3) % 3]
                wo, wl = z.bF[(2 * hq + 2) % 3], z.bF[(2 * hq + 3) % 3]
                for h2 in range(2):
                    h = 2 * hq + h2
                    pr = slice(64 * h2, 64 * h2 + 64)
                    c0 = h2 * 192
                    self.mm(fo[0:64, c0:c0 + 64], NG[:, h, 0:64], BK[:, 0, h * 64:(h + 1) * 64], True, True, [z.b_NG, z.b_BK], [wo], last=False)
                    self.mm(fo[0:64, c0 + 64:c0 + 192], NG[:, h, 0:64], HB[:, h, 1, :], True, False, [z.b_NG, z.b_HB], [wo], last=False)
                    self.mm(fo[0:64, c0 + 64:c0 + 192], identb[pr, pr], TA[pr, hq, 1, :], False, True, [b_c, z.b_TA], [wo], last=(h2 == 1))
                for h2 in range(2):
                    h = 2 * hq + h2
                    c0 = h2 * 192
                    self.mm(fl[:, c0:c0 + 64], NG[:, h, 64:192], BK[:, 0, h * 64:(h + 1) * 64], True, False, [z.b_NG, z.b_BK], [wl], last=False)
                    self.mm(fl[:, c0:c0 + 64], identb[:], BK[:, 1, h * 64:(h + 1) * 64], False, True, [b_c, z.b_BK], [wl], last=False)
                    self.mm(fl[:, c0 + 64:c0 + 192], NG[:, h, 64:192], HB[:, h, 1, :], True, False, [z.b_NG, z.b_HB], [wl], last=False)
                    self.mm(fl[:, c0 + 64:c0 + 192], identb[:], HB[:, h, 2, :], False, True, [b_c, z.b_HB], [wl], last=(h2 == 1))
                fo3 = fo[0:64, 0:384].rearrange("p (h m) -> p h m", m=192)
                self.tt("dve", z.PHI[:, 2 * hq:2 * hq + 2, :], fo3[:, :, 0:64], z.Dt[:, 2 * hq:2 * hq + 2, :], ALU.add, [z.b_Dt], [wo, z.b_PHI])
                self.cp("act", z.OM[:, 2 * hq:2 * hq + 2, :], fo3[:, :, 64:192], [], [wo, z.b_OM])
                self.cp("act", z.PSL[:, 2 * hq:2 * hq + 2, :].rearrange("p h m -> p (h m)"), fl[:, 0:384], [], [wl, z.b_PSL])
                yield

        state = {"cur": 0}

        def ystage(i, z):
            cur = state["cur"]
            nxt = 1 - cur
            v_ = z.xs[:, 1024:1536]
            for h in range(8):
                self.mm(z.F[0][:, h * 64:(h + 1) * 64], z.OM[:, h, :], ST[cur][:, h, :], True, False, [z.b_OM, b_ST[cur]], [z.bF[0]], last=False)
                self.mm(z.F[0][:, h * 64:(h + 1) * 64], z.PSL[:, h, 64:192], v_[:, h * 64:(h + 1) * 64], False, True, [z.b_PSL, z.b_xs], [z.bF[0]], last=(h == 7))
            for h in range(8):
                self.mm(z.F[1][0:64, h * 64:(h + 1) * 64], z.PHI[:, h, :], ST[cur][:, h, :], True, False, [z.b_PHI, b_ST[cur]], [z.bF[1]], last=False)
                self.mm(z.F[1][0:64, h * 64:(h + 1) * 64], z.PSL[:, h, 0:64], v_[:, h * 64:(h + 1) * 64], False, True, [z.b_PSL, z.b_xs], [z.bF[1]], last=(h == 7))
            self.cp("dve", ST[nxt][:].rearrange("p h m -> p (h m)"), z.F[1][0:64, :], [], [z.bF[1], b_ST[nxt]])
            state["cur"] = nxt
            post(i, z, z.F[0][:], [], [z.bF[0]])

        with contextlib.ExitStack() as st1:
            sets = [mkset(st1, "A", 0), mkset(st1, "B", 1)]
            active = []
            nxt_i = 0
            free_sets = [sets[0], sets[1]]
            while nxt_i < NTP or active:
                while nxt_i < NTP and free_sets:
                    z = free_sets.pop(0)
                    active.append([nxt_i, xstage(nxt_i, z), z])
                    nxt_i += 1
                done_any = False
                for ent in list(active):
                    try:
                        next(ent[1])
                    except StopIteration:
                        assert ent is active[0]
                        ystage(ent[0], ent[2])
                        active.pop(0)
                        free_sets.append(ent[2])
                        done_any = True
                        break
            cur = state["cur"]
            for h in range(8):
                self.mm(sets[0].F[2][0:64, h * 64:(h + 1) * 64], ST[cur][:, h, :], identf[0:64, 0:64], True, True, [b_ST[cur], b_c], [sets[0].bF[2]], last=(h == 7))
            zz = sets[0]
            self.cp("dve", zz.PHI[:].rearrange("p h m -> p (h m)"), zz.F[2][0:64, :], [], [zz.bF[2], zz.b_PHI])
            self.dma("sp", bass.AP(O["wkvp"], 0, [[64, 64], [4096, 8], [1, 64]]), zz.PHI[:], [zz.b_PHI], [])
            self.barrier()
            self.P.emit_phase()

        with contextlib.ExitStack() as st2:
            z = mkset(st2, "S")
            sb = lambda name, shape, dt=F32: st2.enter_context(nc.sbuf_tensor(name, list(shape), dt))
            i = NTP
            Ssb = sb("r_Ssb", [128, 4096]); tmp = sb("r_tmp", [128, 4096])
            vec = sb("r_vec", [128, 6, 8, 64]); ysb = sb("r_ysb", [128, 8, 64]); ytm = sb("r_ytm", [128, 512]); sk = sb("r_sk", [128, 64])
            b_S, b_tmp, b_vec, b_ys, b_ytm, b_sk, b_rs, b_yscr = (Buf() for _ in range(8))
            self.dma("sp", Ssb[:], I["swkv"].ap(), [], [b_S])
            load(i, z)
            for _ in rstage(i, z):
                pass
            T, bT = z.T, z.bT
            self.act(T[0][:], T[0][:], AF.Exp, [], [bT[0]])
            X = z.xs
            srcs = [(X[:, 0:512], z.b_xs), (T[0][:], bT[0]), (T[3][:], bT[3]), (X[:, 1024:1536], z.b_xs), (T[2][:], bT[2]), (T[4][:], bT[4])]
            b_rsq = [Buf() for _ in range(6)]
            for q, (ap, bb) in enumerate(srcs):
                self.dma("sp", S["rs"].ap()[:, q * 512:(q + 1) * 512], ap, [bb], [b_rsq[q]])
            b_vecs = [Buf() for _ in range(96)]
            for q in range(6):
                for s in range(16):
                    self.dma("sp", vec[8 * s:8 * s + 8, q, :, :], bass.AP(S["rs"], s * 8 * 3072 + q * 512, [[64, 8], [3072, 8], [1, 64]]), [b_rsq[q]], [b_vecs[q * 16 + s]])
            self.P.op("dve", lambda e: e.memset(sk[:], 0.0), b_vecs, [b_vec, b_sk])
            S3 = Ssb[:].rearrange("p (i j) -> p i j", j=64)
            t3 = tmp[:].rearrange("p (i j) -> p i j", j=64)
            bi = lambda ap: ap.unsqueeze(1).to_broadcast([128, 64, 64])
            bj = lambda ap: ap.unsqueeze(2).to_broadcast([128, 64, 64])
            for t in range(8):
                r_t, w_t, k_t, v_t, kap_t, b_t = (vec[:, q, t, :] for q in range(6))
                self.tt("dve", t3, S3, bi(kap_t), ALU.mult, [b_S, b_vec], [b_tmp])
                self.red("dve", sk[:], t3, [b_tmp], [b_sk])
                self.tt("dve", S3, S3, bi(w_t), ALU.mult, [b_vec], [b_S])
                self.tt("dve", t3, bj(sk[:]), bi(b_t), ALU.mult, [b_sk, b_vec], [b_tmp])
                self.tt("dve", S3, S3, t3, ALU.subtract, [b_tmp], [b_S])
                self.tt("dve", t3, bj(v_t), bi(k_t), ALU.mult, [b_vec], [b_tmp])
                self.tt("dve", S3, S3, t3, ALU.add, [b_tmp], [b_S])
                self.tt("dve", t3, S3, bi(r_t), ALU.mult, [b_S, b_vec], [b_tmp])
                self.red("dve", ysb[:, t, :], t3, [b_tmp], [b_ys])
            self.dma("sp", O["wkvs"].ap(), Ssb[:], [b_S], [])
            b_yss = [Buf() for _ in range(16)]
            for s in range(16):
                self.dma("sp", bass.AP(S["ys"], s * 8 * 512, [[64, 8], [512, 8], [1, 64]]), ysb[8 * s:8 * s + 8, :, :], [b_ys], [b_yss[s]])
            self.dma("sp", ytm[:], S["ys"].ap(), b_yss, [b_ytm])
            post(i, z, ytm[:], [b_ytm], [])
            self.barrier()
            self.P.emit_phase()


KB.phase1b = _phase1b_v2
```

```python
import contextlib
import math
import numpy as np
import ml_dtypes
import concourse.bass as bass
import concourse.mybir as mybir
from concourse.bass_utils import run_bass_kernel_spmd

F32 = mybir.dt.float32
BF16 = mybir.dt.bfloat16
AF = mybir.ActivationFunctionType
ALU = mybir.AluOpType
AX = mybir.AxisListType

D = 1024
NIN = 3360
NSH = 1824
DFF = 4096
RMS_EPS = 1e-6
GN_EPS = 64e-5
BRANCHES = ((128, 1), (512, 4), (2048, 16))


class Buf:
    __slots__ = ("name", "w", "r")

    def __init__(self, name=""):
        self.name = name
        self.w = None
        self.r = {}


class Prog:
    ENGS = ("pe", "act", "dve", "pool", "sp")

    def __init__(self, nc):
        self.nc = nc
        self.ops = {e: [] for e in self.ENGS}
        self.count = {e: 0 for e in ("pe", "act", "dve", "pool")}
        self.known = {e: {} for e in self.ENGS}
        n_dma_sems = {"sp": 10, "pool": 4, "act": 2}
        self.dma_pool = {q: [f"d_{q}{i}" for i in range(n)] for q, n in n_dma_sems.items()}
        self.dma_tot = {k: 0 for q in self.dma_pool for k in self.dma_pool[q]}
        self.dma_rr = {q: 0 for q in self.dma_pool}
        self.semnames = ["c_pe", "c_act", "c_dve", "c_pool"] + [k for q in self.dma_pool for k in self.dma_pool[q]]
        self.out_tokens = []

    def _need(self, eng, tok):
        if tok is None:
            return
        sem, val = tok
        if eng == "pe" and sem == "c_pe":
            return
        if self.known[eng].get(sem, 0) >= val:
            return
        self.known[eng][sem] = val
        self.ops[eng].append(("wait", sem, val))

    def _deps(self, eng, r, w):
        for b in r:
            self._need(eng, b.w)
        for b in w:
            self._need(eng, b.w)
            for sem, val in b.r.items():
                self._need(eng, (sem, val))

    def _commit(self, tok, r, w):
        sem, val = tok
        for b in r:
            if b.r.get(sem, 0) < val:
                b.r[sem] = val
        for b in w:
            b.w = tok
            b.r = {}

    def op(self, eng, fn, r=(), w=()):
        self._deps(eng, r, w)
        self.count[eng] += 1
        tok = ("c_" + eng, self.count[eng])
        self.ops[eng].append(("op", fn))
        self._commit(tok, r, w)
        return tok

    def mm(self, fn, r=(), w=(), last=True):
        eng = "pe"
        self._deps(eng, r, w)
        if last:
            self.count[eng] += 1
            tok = ("c_pe", self.count[eng])
            self.ops[eng].append(("op", fn))
        else:
            tok = ("c_pe", self.count[eng] + 1)
            self.ops[eng].append(("opq", fn))
        self._commit(tok, r, w)
        return tok

    def dma(self, q, fn, r=(), w=(), is_output=False):
        self._deps(q, r, w)
        pool = self.dma_pool[q]
        i = self.dma_rr[q]
        self.dma_rr[q] = (i + 1) % len(pool)
        sem = pool[i]
        if self.dma_tot[sem] > 0:
            self._need(q, (sem, self.dma_tot[sem]))
        self.dma_tot[sem] += 16
        tok = (sem, self.dma_tot[sem])
        self.ops[q].append(("dma", fn, sem))
        self._commit(tok, r, w)
        if is_output:
            self.out_tokens.append(tok)
        return tok

    def open(self, stack):
        self.sems = {n: stack.enter_context(self.nc.semaphore(n)) for n in self.semnames}

    def emit_phase(self):
        nc = self.nc
        sems = self.sems
        ops = self.ops
        self.ops = {e: [] for e in self.ENGS}
        with nc.Block() as block:
            def run(engname, handle):
                for item in ops[engname]:
                    k = item[0]
                    if k == "wait":
                        handle.wait_ge(sems[item[1]], item[2])
                    elif k == "op":
                        item[1](handle).then_inc(sems["c_" + engname], 1)
                    elif k == "opq":
                        item[1](handle)
                    elif k == "dma":
                        item[1](handle).then_inc(sems[item[2]], 16)

            @block.tensor
            def _(e):
                run("pe", e)

            @block.scalar
            def _(e):
                run("act", e)

            @block.vector
            def _(e):
                run("dve", e)

            @block.gpsimd
            def _(e):
                run("pool", e)

            @block.sync
            def _(e):
                run("sp", e)


def t5_bucket(dist):
    dist = np.maximum(dist, 0)
    max_exact = 16
    large = max_exact + (np.log(np.maximum(dist, 1) / max_exact) / math.log(2048 / max_exact) * 16).astype(np.int32)
    large = np.minimum(large, 31)
    return np.where(dist < max_exact, dist, large).astype(np.int32)


C_ID, C_SHIFT, C_ELAST, C_TRI, C_LMID, C_ONES, C_MUS, C_MUI, C_MLS, C_J, C_SHIFTS, C_TRIS, C_ONESS, NCONST = range(14)


def host_consts():
    c = np.zeros((NCONST, 128, 128), np.float32)
    s = np.arange(128)[:, None]
    t = np.arange(128)[None, :]
    c[C_ID] = (s == t)
    c[C_SHIFT] = (s == t - 1)
    c[C_ELAST] = (s == 127) & (t == 0)
    c[C_TRI] = (s <= t)
    c[C_LMID] = (s <= 63) & (t >= 0)
    c[C_ONES] = 1.0
    c[C_MUS] = (s < t)
    c[C_MUI] = (s <= t)
    c[C_MLS] = (s > t)
    c[C_J] = (s == 127 - t)
    c[C_SHIFTS] = (s == t - 1) & (t % 8 != 0)
    c[C_TRIS] = (s <= t) & (s // 8 == t // 8)
    c[C_ONESS] = (s // 8 == t // 8)
    cst = np.ascontiguousarray(c.transpose(1, 0, 2))
    sel = np.zeros((16, 128), np.float32)
    sel[np.arange(16), np.arange(16) * 8] = 1.0
    oh = np.zeros((32, 3 * 129), np.float32)
    for bi, (w, d) in enumerate(BRANCHES):
        jj = np.arange(129)
        oh[t5_bucket(jj * d), bi * 129 + jj] = 1.0
    ohs = np.zeros((32, 2304), np.float32)
    for x in range(2056):
        dist = 2055 - x
        for bi, (w, d) in enumerate(BRANCHES):
            if dist % d == 0 and dist <= w:
                ohs[int(t5_bucket(np.array(dist))), x] += 1.0
    return {"cst": cst, "sel": sel, "oh": oh, "ohs": ohs}


class KB:
    def __init__(self, TP=8192, debug=False, scratch_in=()):
        self.scratch_in = set(scratch_in)
        self.TP = TP
        self.NTP = TP // 128
        self.NT = self.NTP + 1
        self.TA = self.NT * 128
        self.debug = debug
        self.nc = bass.Bass("TRN2", target_bir_lowering=False)
        self.P = Prog(self.nc)
        self.declare()

    def declare(self):
        nc, TP, TA = self.nc, self.TP, self.TA
        di = lambda n, s, dt=F32: nc.dram_tensor(n, list(s), dt, kind="ExternalInput")
        do = lambda n, s, dt=F32: nc.dram_tensor(n, list(s), dt, kind="ExternalOutput")
        ds = lambda n, s, dt=F32: nc.dram_tensor(n, list(s), dt, kind=("ExternalInput" if n in self.scratch_in else
                                                                       "ExternalOutput" if self.debug else "Internal"))
        self.i = dict(
            xp=di("xp", [TP, D]), xs=di("xs", [128, D]),
            ck=di("ck", [16 * 2048, 512]), cv=di("cv", [16 * 2048, 512]),
            swkv=di("swkv", [128, 4096]), sshift=di("sshift", [16, NSH]),
            bias_table=di("bias_table", [32, 8]), ln1_g=di("ln1_g", [1, D]), w_in=di("w_in", [D, NIN]),
            q_norm_g=di("q_norm_g", [1, 64]), k_norm_g=di("k_norm_g", [1, 64]), mu_shift=di("mu_shift", [1, NSH]),
            w0=di("w0", [1, 512]), w_lora2=di("w_lora2", [64, 512]), a0=di("a0", [1, 512]),
            a_lora2=di("a_lora2", [64, 512]), g_lora2=di("g_lora2", [160, 512]), k_k=di("k_k", [1, 512]),
            k_a=di("k_a", [1, 512]), r_k=di("r_k", [1, 512]), lnx_g=di("lnx_g", [1, 512]), lnx_b=di("lnx_b", [1, 512]),
            w_out=di("w_out", [D, D]), ln2_g=di("ln2_g", [1, D]), w_mlp1=di("w_mlp1", [D, DFF]), w_mlp2=di("w_mlp2", [DFF, D]),
            cst=di("cst", [128, NCONST, 128]), sel=di("sel", [16, 128]), oh=di("oh", [32, 387]), ohs=di("ohs", [32, 2304]),
        )
        KW = min(2048, TP)
        self.KW = KW
        self.o = dict(
            yp=do("yp", [TP, D]), ys=do("ys", [128, D]), kwp=do("kwp", [KW, 512]), vwp=do("vwp", [KW, 512]),
            wkvp=do("wkvp", [512, 64]), shp=do("shp", [1, NSH]), kns=do("kns", [128, 512]), vns=do("vns", [128, 512]),
            wkvs=do("wkvs", [128, 4096]), shs=do("shs", [16, NSH]),
        )
        self.s = dict(
            qkv=ds("s_qkv", [TA, 1536], BF16), p=ds("s_p", [TA, NSH]), rw=ds("s_rw", [TA, 512], BF16),
            att=ds("s_att", [3, TP, 528]), atts=ds("s_atts", [128, 512]),
            gd=ds("s_gd", [8, 3, 384]), erev=ds("s_erev", [2304, 8]),
            rs=ds("s_rs", [128, 6 * 512]), ys=ds("s_ys", [128, 512]),
        )
        self.sb = {k: Buf("s_" + k) for k in self.s}
        self.dbg = {}
        if self.debug:
            self.dbg["E"] = do("dbgE", [128, 3 * 8 * 2 * 128])

    def tt(self, eng, out, a, b, op, r, w):
        return self.P.op(eng, lambda e: e.tensor_tensor(out=out, in0=a, in1=b, op=op), r, w)

    def ts(self, eng, out, a, s1, s2, op0, op1, r, w):
        if op1 is None:
            return self.P.op(eng, lambda e: e.tensor_scalar(out=out, in0=a, scalar1=s1, scalar2=None, op0=op0), r, w)
        return self.P.op(eng, lambda e: e.tensor_scalar(out=out, in0=a, scalar1=s1, scalar2=s2, op0=op0, op1=op1), r, w)

    def stt(self, eng, out, a, sc, b, op0, op1, r, w):
        return self.P.op(eng, lambda e: e.scalar_tensor_tensor(out=out, in0=a, scalar=sc, in1=b, op0=op0, op1=op1), r, w)

    def act(self, out, in_, func, r, w, scale=1.0, bias=0.0, accum=None):
        if accum is None:
            return self.P.op("act", lambda e: e.activation(out=out, in_=in_, func=func, bias=bias, scale=scale), r, w)
        return self.P.op("act", lambda e: e.activation(out=out, in_=in_, func=func, bias=bias, scale=scale, accum_out=accum), r, w)

    def cp(self, eng, out, in_, r, w):
        if eng == "act":
            return self.act(out, in_, AF.Copy, r, w)
        return self.P.op(eng, lambda e: e.tensor_copy(out=out, in_=in_), r, w)

    def red(self, eng, out, in_, r, w, op=ALU.add):
        return self.P.op(eng, lambda e: e.tensor_reduce(out=out, in_=in_, axis=AX.X, op=op), r, w)

    def memset(self, eng, ap, val, w):
        return self.P.op(eng, lambda e: e.memset(ap, val), (), w)

    def mm(self, out, lhsT, rhs, start, stop, r, w, last=None):
        if last is None:
            last = stop
        return self.P.mm(lambda e: e.matmul(out, lhsT=lhsT, rhs=rhs, start=start, stop=stop), r, w, last=last)

    def tr(self, out, in_, ident, r, w, last=True):
        return self.P.mm(lambda e: e.transpose(out, in_, ident), r, w, last=last)

    def dma(self, q, out, in_, r, w, is_output=False, slow=False):
        if slow:
            return self.P.dma(q, lambda e: e.dma_start(out=out, in_=in_, allow_slow_non_contiguous=True), r, w)
        return self.P.dma(q, lambda e: e.dma_start(out=out, in_=in_), r, w, is_output=is_output)

    def barrier(self):
        P = self.P
        for e in P.ENGS:
            for n in ("pe", "act", "dve", "pool"):
                if P.count[n] > 0:
                    P._need(e, ("c_" + n, P.count[n]))
            for sem, tot in P.dma_tot.items():
                if tot > 0:
                    P._need(e, (sem, tot))

    def rsqrt(self, out, in_, r, w, scale, eps, tmp):
        self.act(tmp, in_, AF.Sqrt, r, w + [], scale=scale, bias=eps)
        return self.P.op("dve", lambda e: e.reciprocal(out=out, in_=tmp), r, w)

    def bcast_row(self, t, row=0, n=None):
        n = n or t.shape[1]
        return bass.AP(t, row * t.shape[1], [[0, 128], [1, n]])

    def phase1a(self):
        nc, P, NT, NTP = self.nc, self.P, self.NT, self.NTP
        I, O, S = self.i, self.o, self.s
        with contextlib.ExitStack() as st:
            sb = lambda name, shape, dt=F32: st.enter_context(nc.sbuf_tensor(name, list(shape), dt))
            ps = lambda name, shape, dt=F32: st.enter_context(nc.psum_tensor(name, list(shape), dt))
            win = sb("a_win", [128, 8, NIN], BF16)
            stg = [sb(f"a_stg{i}", [128, NIN]) for i in range(2)]
            gcol = sb("a_gcol", [128, 8])
            identb = sb("a_identb", [128, 128], BF16)
            identf = sb("a_identf", [128, 128])
            gqk = sb("a_gqk", [128, 1024])
            xt = [sb(f"a_x{i}", [128, D]) for i in range(2)]
            junk = sb("a_junk", [128, D])
            junkq = sb("a_junkq", [128, D]); b_junkq = Buf()
            junk2x = sb("a_x2", [128, D])
            stA0 = sb("a_stA0", [128, 4]); stA1 = sb("a_stA1", [128, 4]); stA2 = sb("a_stA2", [128, 4])
            nb = [sb(f"a_nb{i}", [128, D], BF16) for i in range(2)]
            nT = [sb(f"a_nT{i}", [128, 8, 128], BF16) for i in range(2)]
            proj = [sb(f"a_proj{i}", [128, NIN]) for i in range(2)]
            qkvb = [sb(f"a_qkvb{i}", [128, 1536], BF16) for i in range(2)]
            st4 = [sb(f"a_st{i}", [128, 40]) for i in range(2)]
            psT = [ps(f"a_psT{i}", [128, 8, 128], BF16) for i in range(2)]
            psG = [ps(f"a_psG{i}", [128, 512]) for i in range(6)]
            b_win = [Buf() for _ in range(8)]
            b_stg = [Buf(), Buf()]
            b_c = Buf()
            b_x, b_nb, b_nT, b_proj, b_qkvb, b_st, b_psT = ([Buf(), Buf()] for _ in range(7))
            b_junk = Buf()
            b_psG = [Buf() for _ in range(6)]

            self.dma("sp", gcol[:], bass.AP(I["ln1_g"], 0, [[1, 128], [128, 8]]), [], [b_c], slow=True)
            self.dma("sp", identf[:], I["cst"].ap()[:, C_ID, :], [], [b_c])
            self.dma("sp", gqk[:, 0:512].rearrange("p (h e) -> p h e", e=64), bass.AP(I["q_norm_g"], 0, [[0, 128], [0, 8], [1, 64]]), [], [b_c])
            self.dma("sp", gqk[:, 512:1024].rearrange("p (h e) -> p h e", e=64), bass.AP(I["k_norm_g"], 0, [[0, 128], [0, 8], [1, 64]]), [], [b_c])
            self.cp("dve", identb[:], identf[:], [b_c], [b_c])
            for c in range(8):
                self.dma("sp", stg[c % 2][:], I["w_in"].ap()[c * 128:(c + 1) * 128, :], [], [b_stg[c % 2]])
                self.ts("dve", win[:, c, :], stg[c % 2][:], gcol[:, c:c + 1], None, ALU.mult, None,
                        [b_stg[c % 2], b_c], [b_win[c]])

            def xsrc(i):
                return I["xp"].ap()[i * 128:(i + 1) * 128, :] if i < NTP else I["xs"].ap()

            NX = 3
            xt3 = xt + [junk2x]
            b_x3 = b_x + [Buf()]
            st3 = [stA0, stA1, stA2]
            b_st3 = [Buf(), Buf(), Buf()]

            def loadx(i):
                if i < NT:
                    self.dma("sp", xt3[i % NX][:], xsrc(i), [], [b_x3[i % NX]])

            def stats(i):
                if i >= NT:
                    return
                k, kx = i % 2, i % NX
                s3 = st3[kx]
                self.memset("pool", s3[:, 0:1], 0.0, [b_st3[kx]])
                self.act(junk[:], xt3[kx][:], AF.Square, [b_x3[kx]], [b_junk, b_st3[kx]], accum=s3[:, 0:1])
                self.act(s3[:, 1:2], s3[:, 0:1], AF.Sqrt, [], [b_st3[kx]], scale=1.0 / D, bias=RMS_EPS)
                self.P.op("dve", lambda e, s3=s3: e.reciprocal(out=s3[:, 2:3], in_=s3[:, 1:2]), [], [b_st3[kx]])
                self.act(nb[k][:], xt3[kx][:], AF.Copy, [b_x3[kx], b_st3[kx]], [b_nb[k]], scale=s3[:, 2:3])

            def transp(i):
                if i >= NT:
                    return
                k = i % 2
                for c in range(8):
                    self.tr(psT[k][:, c, :], nb[k][:, c * 128:(c + 1) * 128], identb[:], [b_nb[k], b_c], [b_psT[k]], last=(c == 7))
                self.cp("dve", nT[k][:], psT[k][:], [], [b_psT[k], b_nT[k]])

            loadx(0); loadx(1); loadx(2)
            stats(0); transp(0); stats(1)
            gi = 0
            for i in range(NT):
                k = i % 2
                s4 = st4[k]
                for g in range(7):
                    if g == 4:
                        transp(i + 1)
                    c0 = g * 512
                    cw = min(512, NIN - c0)
                    pg = gi % 6
                    gi += 1
                    for c in range(8):
                        self.mm(psG[pg][:, 0:cw], nT[k][:, c, :], win[:, c, c0:c0 + cw], c == 0, c == 7,
                                [b_nT[k], b_win[c]], [b_psG[pg]])
                    self.cp("act" if g % 2 == 0 else "dve", proj[k][:, c0:c0 + cw], psG[pg][:, 0:cw], [], [b_psG[pg], b_proj[k]])
                qk3 = proj[k][:, 0:1024].rearrange("p (g e) -> p g e", e=64)
                self.tt("pool", junkq[:], proj[k][:, 0:1024], proj[k][:, 0:1024], ALU.mult, [b_proj[k]], [b_junkq])
                self.red("dve", s4[:, 4:20], junkq[:].rearrange("p (g e) -> p g e", e=64), [b_junkq], [b_st[k]])
                self.act(s4[:, 20:36], s4[:, 4:20], AF.Sqrt, [], [b_st[k]], scale=1.0 / 64, bias=RMS_EPS)
                self.P.op("dve", lambda e, s4=s4: e.reciprocal(out=s4[:, 4:20], in_=s4[:, 20:36]), [], [b_st[k]])
                self.tt("dve", qk3, qk3, s4[:, 4:20].unsqueeze(2).to_broadcast([128, 16, 64]), ALU.mult, [b_st[k]], [b_proj[k]])
                self.tt("pool", proj[k][:, 0:1024], proj[k][:, 0:1024], gqk[:], ALU.mult, [b_c], [b_proj[k]])
                self.cp("pool", qkvb[k][:], proj[k][:, 0:1536], [b_proj[k]], [b_qkvb[k]])
                loadx(i + 3)
                stats(i + 2)
                r0 = i * 128
                self.dma("sp", S["qkv"].ap()[r0:r0 + 128, :], qkvb[k][:], [b_qkvb[k]], [])
                self.dma("sp", S["p"].ap()[r0:r0 + 128, :], proj[k][:, 1536:NIN], [b_proj[k]], [])
                if i < NTP:
                    j = i - (NTP - self.KW // 128)
                    if j >= 0:
                        self.dma("sp", O["kwp"].ap()[j * 128:(j + 1) * 128, :], proj[k][:, 512:1024], [b_proj[k]], [])
                        self.dma("sp", O["vwp"].ap()[j * 128:(j + 1) * 128, :], proj[k][:, 1024:1536], [b_proj[k]], [])
                    if i == NTP - 1:
                        self.dma("sp", O["shp"].ap(), proj[k][127:128, 1536:NIN], [b_proj[k]], [])
                else:
                    self.dma("sp", O["kns"].ap(), proj[k][:, 512:1024], [b_proj[k]], [])
                    self.dma("sp", O["vns"].ap(), proj[k][:, 1024:1536], [b_proj[k]], [])
                    for s in range(16):
                        self.dma("sp", O["shs"].ap()[s:s + 1, :], proj[k][8 * s + 7:8 * s + 8, 1536:NIN], [b_proj[k]], [])
            self.barrier()
            self.P.emit_phase()


_CACHE = {}


def build(TP=8192, debug=False, phases=("1a", "1b", "2", "2s", "3"), scratch_in=()):
    key = (TP, debug, tuple(phases), tuple(scratch_in))
    if key in _CACHE:
        return _CACHE[key]
    kb = KB(TP, debug, scratch_in)
    with contextlib.ExitStack() as st:
        kb.P.open(st)
        for ph in phases:
            getattr(kb, "phase" + ph)()
    _CACHE[key] = kb
    return kb


def make_in_maps(inp, TP, ncores):
    hc = host_consts()
    f = lambda a: np.ascontiguousarray(np.asarray(a, dtype=np.float32))
    maps = []
    for c in range(ncores):
        m = dict(
            xp=f(inp["x_prompt"][c, :TP]), xs=f(inp["x_sample"][16 * c:16 * c + 16]).reshape(128, D),
            ck=f(inp["cache_k_win"][0, 16 * c:16 * c + 16]).reshape(16 * 2048, 512),
            cv=f(inp["cache_v_win"][0, 16 * c:16 * c + 16]).reshape(16 * 2048, 512),
            swkv=f(inp["state_wkv"][0, 16 * c:16 * c + 16]).reshape(128, 4096),
            sshift=f(inp["state_shift"][0, 16 * c:16 * c + 16]),
            bias_table=f(inp["bias_table"]), ln1_g=f(inp["ln1_g"]), w_in=f(inp["w_in"][0]),
            q_norm_g=f(inp["q_norm_g"]), k_norm_g=f(inp["k_norm_g"]), mu_shift=f(inp["mu_shift"]),
            w0=f(inp["w0"]), w_lora2=f(inp["w_lora2"][0]), a0=f(inp["a0"]), a_lora2=f(inp["a_lora2"][0]),
            g_lora2=f(inp["g_lora2"][0]), k_k=f(inp["k_k"]), k_a=f(inp["k_a"]), r_k=f(inp["r_k"]).reshape(1, 512),
            lnx_g=f(inp["lnx_g"]), lnx_b=f(inp["lnx_b"]), w_out=f(inp["w_out"][0]), ln2_g=f(inp["ln2_g"]),
            w_mlp1=f(inp["w_mlp1"][0]), w_mlp2=f(inp["w_mlp2"][0]),
        )
        m.update(hc)
        maps.append(m)
    return maps


def run(inp, TP=8192, ncores=8, debug=False, phases=("1a", "1b", "2", "2s", "3"), extra=None):
    kb = build(TP, debug, phases, tuple(sorted(extra[0].keys())) if extra else ())
    maps = make_in_maps(inp, TP, ncores)
    if extra:
        for m, e in zip(maps, extra):
            m.update(e)
    import os
    if os.environ.get("KTRACE"):
        res = run_bass_kernel_spmd(kb.nc, maps, core_ids=list(range(ncores)), trace=True)
        print("EXEC_TIME_NS", res.exec_time_ns)
    else:
        res = run_bass_kernel_spmd(kb.nc, maps, core_ids=list(range(ncores)))
    return res.results


def kernel(**inp):
    r = run(inp)
    n = 8
    cat = lambda k: np.stack([np.asarray(r[c][k], dtype=np.float32) for c in range(n)])
    yp = cat("yp")
    ys = cat("ys").reshape(128, 8, D)
    kwp = cat("kwp").reshape(1, 8, 2048, 8, 64)
    vwp = cat("vwp").reshape(1, 8, 2048, 8, 64)
    wkvp = cat("wkvp").reshape(1, 8, 8, 64, 64)
    shp = cat("shp").reshape(1, 8, NSH)
    kns = cat("kns").reshape(1, 128, 8, 8, 64)
    vns = cat("vns").reshape(1, 128, 8, 8, 64)
    wkvs = cat("wkvs").reshape(1, 128, 8, 64, 64)
    shs = cat("shs").reshape(1, 128, NSH)
    return (yp, ys, kwp, vwp, wkvp, shp, kns, vns, wkvs, shs)


def _phase3(self):
    nc, P, NT, NTP = self.nc, self.P, self.NT, self.NTP
    I, O, S = self.i, self.o, self.s
    with contextlib.ExitStack() as st:
        sb = lambda name, shape, dt=F32: st.enter_context(nc.sbuf_tensor(name, list(shape), dt))
        ps = lambda name, shape, dt=F32: st.enter_context(nc.psum_tensor(name, list(shape), dt))
        wout = sb("c_wout", [128, 8, D], BF16)
        w1 = sb("c_w1", [128, 8, DFF], BF16)
        w2 = sb("c_w2", [128, 32, D], BF16)
        stg = [sb(f"c_stg{i}", [128, 512]) for i in range(2)]
        gcol = sb("c_gcol", [128, 8])
        identb = sb("c_identb", [128, 128], BF16)
        identf = sb("c_identf", [128, 128])
        xt = [sb(f"c_x{i}", [128, D]) for i in range(2)]
        att3 = sb("c_att3", [128, 3, 528])
        mixed = [sb(f"c_mix{i}", [128, D], BF16) for i in range(2)]
        mixT = [sb(f"c_mixT{i}", [128, 8, 128], BF16) for i in range(2)]
        hbuf = [sb(f"c_h{i}", [128, D]) for i in range(2)]
        junk = sb("c_junk", [128, D], BF16)
        mbf = [sb(f"c_mbf{i}", [128, D], BF16) for i in range(2)]
        mT = sb("c_mT", [128, 8, 256], BF16)
        rl = [sb(f"c_rl{i}", [128, 256]) for i in range(2)]
        uT = sb("c_uT", [128, 32, 256], BF16)
        st4 = [sb(f"c_st{i}", [128, 16]) for i in range(2)]
        psT = [ps(f"c_psT{i}", [128, 8, 128], BF16) for i in range(2)]
        psH = [ps(f"c_psH{i}", [128, 512]) for i in range(2)]
        psU = [ps(f"c_psU{i}", [128, 512]) for i in range(2)]
        b_c, b_wout, b_w2, b_junk, b_att3, b_mbf_unused, b_mT, b_uT = (Buf() for _ in range(8))
        b_mbf = [Buf(), Buf()]
        b_w1 = [Buf() for _ in range(8)]
        b_stg, b_x, b_mix, b_mixT, b_h, b_rl, b_st, b_psT, b_psH, b_psU = ([Buf(), Buf()] for _ in range(10))

        self.dma("sp", gcol[:], bass.AP(I["ln2_g"], 0, [[1, 128], [128, 8]]), [], [b_c], slow=True)
        self.dma("sp", identf[:], I["cst"].ap()[:, C_ID, :], [], [b_c])
        self.cp("dve", identb[:], identf[:], [b_c], [b_c])
        for c in range(8):
            self.dma("pool", wout[:, c, :], I["w_out"].ap()[c * 128:(c + 1) * 128, :], [], [b_wout])
        si = 0
        for c in range(8):
            for q in range(8):
                k = si % 2
                si += 1
                self.dma("sp", stg[k][:], I["w_mlp1"].ap()[c * 128:(c + 1) * 128, q * 512:(q + 1) * 512], [], [b_stg[k]])
                self.ts("dve", w1[:, c, q * 512:(q + 1) * 512], stg[k][:], gcol[:, c:c + 1], None, ALU.mult, None,
                        [b_stg[k], b_c], [b_w1[c]])
        for fc in range(32):
            self.dma("pool", w2[:, fc, :], I["w_mlp2"].ap()[fc * 128:(fc + 1) * 128, :], [], [b_w2])

        groups = [list(range(g, min(g + 2, NTP))) for g in range(0, NTP, 2)] + [[NTP]]
        cnt = {"pi": 0}

        def part0(grp):
            for g, i in enumerate(grp):
                k = g
                r0 = i * 128
                if i < NTP:
                    self.dma("sp", xt[k][:], I["xp"].ap()[r0:r0 + 128, :], [], [b_x[k]])
                    self.dma("sp", att3[:], S["att"].ap()[:, r0:r0 + 128, :].rearrange("b t f -> t b f"), [], [b_att3])
                    self.tt("pool", att3[:, 0, :], att3[:, 0, :], att3[:, 1, :], ALU.add, [], [b_att3])
                    self.tt("pool", att3[:, 0, :], att3[:, 0, :], att3[:, 2, :], ALU.add, [], [b_att3])
                    a3 = att3[:, 0, :].rearrange("p (h e) -> p h e", e=66)
                    self.P.op("dve", lambda e, a3=a3, s=st4[k]: e.reciprocal(out=s[:, 4:12], in_=a3[:, :, 64]), [b_att3], [b_st[k]])
                    self.tt("dve", mixed[k][:, 0:512].rearrange("p (h e) -> p h e", e=64), a3[:, :, 0:64],
                            st4[k][:, 4:12].unsqueeze(2).to_broadcast([128, 8, 64]), ALU.mult, [b_att3, b_st[k]], [b_mix[k]])
                else:
                    self.dma("sp", xt[k][:], I["xs"].ap(), [], [b_x[k]])
                    self.dma("sp", att3[:, 0, 0:512], S["atts"].ap(), [], [b_att3])
                    self.cp("dve", mixed[k][:, 0:512], att3[:, 0, 0:512], [b_att3], [b_mix[k]])
                self.dma("sp", mixed[k][:, 512:1024], S["rw"].ap()[r0:r0 + 128, :], [], [b_mix[k]])
                for c in range(8):
                    self.tr(psT[k][:, c, :], mixed[k][:, c * 128:(c + 1) * 128], identb[:], [b_mix[k], b_c], [b_psT[k]], last=(c == 7))
                self.cp("act", mixT[k][:], psT[k][:], [], [b_psT[k], b_mixT[k]])

        def part1(grp):
            for g, i in enumerate(grp):
                k = g
                for half in range(2):
                    pk = cnt["pi"] % 2
                    cnt["pi"] += 1
                    for c in range(8):
                        self.mm(psH[pk][:], mixT[k][:, c, :], wout[:, c, half * 512:(half + 1) * 512], c == 0, c == 7,
                                [b_mixT[k], b_wout], [b_psH[pk]])
                    self.tt("dve", hbuf[g][:, half * 512:(half + 1) * 512], psH[pk][:], xt[k][:, half * 512:(half + 1) * 512], ALU.add,
                            [b_x[k]], [b_psH[pk], b_h[g]])

        def part2(grp):
            for g, i in enumerate(grp):
                k = g
                s4 = st4[k]
                self.memset("pool", s4[:, 0:1], 0.0, [b_st[k]])
                self.act(junk[:], hbuf[g][:], AF.Square, [b_h[g]], [b_junk, b_st[k]], accum=s4[:, 0:1])
                self.act(s4[:, 1:2], s4[:, 0:1], AF.Sqrt, [], [b_st[k]], scale=1.0 / D, bias=RMS_EPS)
                self.P.op("dve", lambda e, s4=s4: e.reciprocal(out=s4[:, 2:3], in_=s4[:, 1:2]), [], [b_st[k]])
                self.act(mbf[g][:], hbuf[g][:], AF.Copy, [b_h[g], b_st[k]], [b_mbf[g]], scale=s4[:, 2:3])
            for g, i in enumerate(grp):
                k = g
                for c in range(8):
                    self.tr(psT[k][:, c, :], mbf[g][:, c * 128:(c + 1) * 128], identb[:], [b_mbf[g], b_c], [b_psT[k]], last=(c == 7))
                self.cp("act", mT[:, :, g * 128:(g + 1) * 128], psT[k][:], [], [b_psT[k], b_mT])

        part0(groups[0])
        for n, grp in enumerate(groups):
            G = len(grp)
            part1(grp)
            part2(grp)
            W = G * 128
            for fc in range(32):
                if fc == 16 and n + 1 < len(groups):
                    part0(groups[n + 1])
                pk = fc % 2
                for c in range(8):
                    self.mm(psU[pk][:, 0:W], w1[:, c, fc * 128:(fc + 1) * 128], mT[:, c, 0:W], c == 0, c == 7,
                            [b_mT, b_w1[c]], [b_psU[pk]])
                self.act(rl[pk][:, 0:W], psU[pk][:, 0:W], AF.Relu, [], [b_psU[pk], b_rl[pk]])
                self.tt("pool" if fc % 4 != 3 else "dve", uT[:, fc, 0:W], rl[pk][:, 0:W], rl[pk][:, 0:W], ALU.mult, [b_rl[pk]], [b_uT])
            for g, i in enumerate(grp):
                for half in range(2):
                    pk = cnt["pi"] % 2
                    cnt["pi"] += 1
                    for fc in range(32):
                        self.mm(psH[pk][:], uT[:, fc, g * 128:(g + 1) * 128], w2[:, fc, half * 512:(half + 1) * 512], fc == 0, fc == 31,
                                [b_uT, b_w2], [b_psH[pk]])
                    self.tt("dve", hbuf[g][:, half * 512:(half + 1) * 512], psH[pk][:], hbuf[g][:, half * 512:(half + 1) * 512], ALU.add,
                            [], [b_psH[pk], b_h[g]])
                dst = O["yp"].ap()[i * 128:(i + 1) * 128, :] if i < NTP else O["ys"].ap()
                self.dma("sp", dst, hbuf[g][:], [b_h[g]], [])
        self.barrier()
        self.P.emit_phase()


KB.phase3 = _phase3


def _phase2(self):
    nc, P, NT, NTP, TP = self.nc, self.P, self.NT, self.NTP, self.TP
    I, O, S = self.i, self.o, self.s
    with contextlib.ExitStack() as st:
        sb = lambda name, shape, dt=F32: st.enter_context(nc.sbuf_tensor(name, list(shape), dt))
        ps = lambda name, shape, dt=F32: st.enter_context(nc.psum_tensor(name, list(shape), dt))
        identb = sb("b_identb", [128, 128], BF16)
        identf = sb("b_identf", [128, 128])
        Jm = sb("b_J", [128, 128])
        tbl = sb("b_tbl", [32, 8])
        etbl = sb("b_etbl", [32, 8])
        oh = sb("b_oh", [32, 387])
        Gt = sb("b_G", [8, 3, 384])
        Hl = [sb(f"b_Hl{i}", [128, 8, 128]) for i in range(2)]
        E = sb("b_E", [128, 3, 8, 2, 128])
        qkv = [sb(f"b_qkv{i}", [128, 1536], BF16) for i in range(2)]
        QT = [sb(f"b_QT{i}", [128, 4, 128], BF16) for i in range(2)]
        KT = [sb(f"b_KT{i}", [128, 4, 128], BF16) for i in range(3)]
        V1 = [sb(f"b_V1{i}", [128, 8, 66], BF16) for i in range(3)]
        ex = [sb(f"b_ex{i}", [128, 2, 2, 128]) for i in range(4)]
        PT = [sb(f"b_PT{i}", [128, 8, 2, 128], BF16) for i in range(2)]
        osb = [sb(f"b_o{i}", [128, 8, 66]) for i in range(2)]
        psT = [ps(f"b_psT{i}", [128, 2, 4, 128], BF16) for i in range(2)]
        psS = [ps(f"b_psS{i}", [128, 2, 2, 128]) for i in range(4)]
        psOf = [ps(f"b_psO{i}", [128, 512]) for i in range(2)]
        psO = [t[:, 0:264].rearrange("p (h e) -> p h e", e=66) for t in psOf]
        b_c, b_G, b_E = Buf(), Buf(), Buf()
        b_Hl, b_qkv, b_QT, b_KT, b_V1, b_ex, b_PT, b_o, b_psT, b_psS, b_psO = ([Buf(), Buf(), Buf(), Buf()] for _ in range(11))

        self.dma("sp", identf[:], I["cst"].ap()[:, C_ID, :], [], [b_c])
        self.dma("sp", Jm[:], I["cst"].ap()[:, C_J, :], [], [b_c])
        self.dma("sp", tbl[:], I["bias_table"].ap(), [], [b_c])
        self.dma("sp", oh[:], I["oh"].ap(), [], [b_c])
        self.cp("dve", identb[:], identf[:], [b_c], [b_c])
        self.memset("pool", Gt[:], 0.0, [b_G])
        for k in range(3):
            self.memset("pool", V1[k][:], 1.0, [b_V1[k]])
        import os
        STEP = int(os.environ.get("P2_STEP", "99"))
        if STEP >= 2:
            self.mm(psS[0][:].rearrange("p a b c -> p (a b c)")[0:8, 0:387],
                    tbl[:], oh[:], True, True, [b_c], [b_psS[0]])
        for br in range(3 if STEP >= 2 else 0):
            self.act(Gt[:, br, 127:256], psS[0][:].rearrange("p a b c -> p (a b c)")[0:8, br * 129:(br + 1) * 129], AF.Exp, [b_psS[0]], [b_G])
        if STEP >= 3:
            self.dma("sp", S["gd"].ap(), Gt[:], [b_G], [self.sb["gd"]])
        hi = 0
        for br in range(3 if STEP >= 4 else 0):
            for kb in range(2):
                k = hi % 2
                hi += 1
                off = br * 384 + (128 if kb == 0 else 0)
                self.dma("sp", Hl[k][:], bass.AP(S["gd"], off, [[1, 128], [3 * 384, 8], [1, 128]]), [self.sb["gd"]], [b_Hl[k]])
                for hh in range(2 if STEP >= 5 else 0):
                    self.mm(psS[hh][:].rearrange("p a b c -> p (a b c)"), Jm[:], Hl[k][:, hh * 4:(hh + 1) * 4, :].rearrange("p h m -> p (h m)"),
                            True, True, [b_c, b_Hl[k]], [b_psS[hh]])
                    self.cp("dve", E[:, br, hh * 4:(hh + 1) * 4, kb, :], psS[hh][:].rearrange("p a b c -> p (a b c)").rearrange("p (h m) -> p h m", m=128),
                            [b_psS[hh]], [b_E])

        blocks = []
        for br, (w, d) in enumerate(BRANCHES):
            nbk = TP // (128 * d)
            for r in range(d):
                for B in range(nbk):
                    blocks.append((br, d, r, B))

        def load(idx):
            br, d, r, B = blocks[idx]
            k = idx % 2
            row0 = r + d * 128 * B
            self.dma("sp", qkv[k][:], bass.AP(S["qkv"], row0 * 1536, [[d * 1536, 128], [1, 1536]]), [], [b_qkv[k]])

        import os
        if os.environ.get("P2_MAXBLK"):
            blocks = blocks[:int(os.environ["P2_MAXBLK"])]
        if self.debug:
            self.dma("sp", self.dbg["E"].ap(), E[:].rearrange("p a b c d -> p (a b c d)"), [b_E], [])
        BSTEP = 99
        st_ = {"si": 0}

        def stageA(idx):
            br, d, r, B = blocks[idx]
            k = idx % 2
            cur = idx % 3
            for hp in range(4):
                self.tr(psT[k][:, 0, hp, :], qkv[k][:, hp * 128:(hp + 1) * 128], identb[:], [b_qkv[k], b_c], [b_psT[k]], last=False)
            for hp in range(4):
                self.tr(psT[k][:, 1, hp, :], qkv[k][:, 512 + hp * 128:512 + (hp + 1) * 128], identb[:], [b_qkv[k], b_c], [b_psT[k]], last=(hp == 3))
            self.cp("dve", QT[k][:], psT[k][:, 0, :, :], [], [b_psT[k], b_QT[k]])
            self.cp("dve", KT[cur][:], psT[k][:, 1, :, :], [], [b_psT[k], b_KT[cur]])
            self.cp("dve", V1[cur][:, :, 0:64], qkv[k][:, 1024:1536].rearrange("p (h e) -> p h e", e=64), [b_qkv[k]], [b_V1[cur]])

        def stageB(idx):
            br, d, r, B = blocks[idx]
            k = idx % 2
            cur, prv = idx % 3, (idx - 1) % 3
            kbs = (0, 1) if B > 0 else (1,)
            PTv = PT[k][:].rearrange("p (q i t) b m -> p q t i b m", q=2, i=2, t=2)
            Ev = E[:, br, :, :, :].rearrange("p (q i t) b m -> p q t i b m", q=2, i=2, t=2)
            for q in range(2):
                sk = st_["si"] % 2
                st_["si"] += 1
                n_mm = 4 * len(kbs)
                j = 0
                for hpi in range(2):
                    hp = 2 * q + hpi
                    for kb in kbs:
                        kt = KT[cur] if kb == 1 else KT[prv]
                        bk = b_KT[cur] if kb == 1 else b_KT[prv]
                        for h2 in range(2):
                            j += 1
                            self.mm(psS[2 * sk + h2][:, hpi, kb, :], kt[64 * h2:64 * h2 + 64, hp, :], QT[k][64 * h2:64 * h2 + 64, hp, :], True, True,
                                    [bk, b_QT[k]], [b_psS[2 * sk], b_psS[2 * sk + 1]], last=(j == n_mm))
                for h2 in range(2):
                    bi = 2 * sk + h2
                    if B > 0:
                        self.act(ex[bi][:], psS[bi][:], AF.Exp, [], [b_psS[bi], b_ex[bi]], scale=0.125)
                        self.tt("dve", PTv[:, q, h2], ex[bi][:], Ev[:, q, h2], ALU.mult, [b_ex[bi], b_E], [b_PT[k]])
                    else:
                        self.act(ex[bi][:, :, 1, :], psS[bi][:, :, 1, :], AF.Exp, [], [b_psS[bi], b_ex[bi]], scale=0.125)
                        self.tt("dve", PTv[:, q, h2, :, 1, :], ex[bi][:, :, 1, :], Ev[:, q, h2, :, 1, :], ALU.mult, [b_ex[bi], b_E], [b_PT[k]])
            for hh in range(2):
                for h4 in range(4):
                    h = hh * 4 + h4
                    for ji, kb in enumerate(kbs):
                        vv = V1[cur] if kb == 1 else V1[prv]
                        bv = b_V1[cur] if kb == 1 else b_V1[prv]
                        self.mm(psO[hh][:, h4, :], PT[k][:, h, kb, :], vv[:, h, :], ji == 0, ji == len(kbs) - 1,
                                [b_PT[k], bv], [b_psO[hh]], last=(h4 == 3 and ji == len(kbs) - 1))
                self.cp("act" if hh == 0 else "dve", osb[k][:, hh * 4:(hh + 1) * 4, :], psO[hh], [], [b_psO[hh], b_o[k]])
            row0 = r + d * 128 * B
            self.dma("sp", bass.AP(S["att"], (br * TP + row0) * 528, [[d * 528, 128], [1, 528]]), osb[k][:].rearrange("p h e -> p (h e)"), [b_o[k]], [])

        nb_ = len(blocks)
        if nb_:
            load(0)
            if nb_ > 1:
                load(1)
            stageA(0)
        for idx in range(nb_):
            if idx + 1 < nb_:
                stageA(idx + 1)
            if idx + 2 < nb_:
                load(idx + 2)
            stageB(idx)
        self.barrier()
        self.P.emit_phase()


KB.phase2 = _phase2


def _phase2s(self):
    nc, P, TP = self.nc, self.P, self.TP
    I, O, S = self.i, self.o, self.s
    with contextlib.ExitStack() as st:
        sb = lambda name, shape, dt=F32: st.enter_context(nc.sbuf_tensor(name, list(shape), dt))
        ps = lambda name, shape, dt=F32: st.enter_context(nc.psum_tensor(name, list(shape), dt))
        identb = sb("s_identb", [128, 128], BF16)
        identf = sb("s_identf", [128, 128])
        tbl = sb("s_tbl", [32, 8])
        etbl = sb("s_etbl", [32, 8])
        ohs = sb("s_ohs", [32, 2304])
        erev = sb("s_erevsb", [128, 18, 8])
        Ecomb = sb("s_Ecomb", [128, 8, 17, 8])
        qs = sb("s_qs", [128, 512], BF16)
        QTs = sb("s_QTs", [128, 4, 128], BF16)
        Kb = [sb(f"s_Kb{i}", [128, 16, 512], BF16) for i in range(2)]
        Vb = [sb(f"s_Vb{i}", [128, 16, 512], BF16) for i in range(2)]
        Knew = [sb(f"s_Knew{i}", [8, 1536], BF16) for i in range(2)]
        KTs = sb("s_KTs", [128, 4, 17, 128], BF16)
        V1s = sb("s_V1s", [128, 17, 8, 66], BF16)
        exs = sb("s_exs", [128, 8, 17, 8])
        PTs = sb("s_PTs", [128, 8, 17, 8], BF16)
        osb = sb("s_osb", [8, 8, 66])
        rec = sb("s_rec", [8, 8])
        ao = [sb(f"s_ao{i}", [8, 8, 64]) for i in range(2)]
        psT = [ps(f"s_psT{i}", [128, 8, 128], BF16) for i in range(2)]
        psS = [ps(f"s_psS{i}", [128, 512]) for i in range(4)]
        psOf = [ps(f"s_psO{i}", [128, 512]) for i in range(2)]
        psSv = [t[:, 0:272].rearrange("p (i n t) -> p i n t", i=2, n=17) for t in psS]
        psO = [t[:, 0:264].rearrange("p (h e) -> p h e", e=66) for t in psOf]
        b_c, b_E, b_QTs, b_KTs, b_V1s, b_exs, b_PTs, b_osb, b_qs = (Buf() for _ in range(9))
        b_Kb, b_Vb, b_Knew, b_ao, b_psT, b_psO = ([Buf(), Buf()] for _ in range(6))
        b_psS = [Buf() for _ in range(4)]

        self.dma("sp", identf[:], I["cst"].ap()[:, C_ID, :], [], [b_c])
        self.dma("sp", tbl[:], I["bias_table"].ap(), [], [b_c])
        self.dma("sp", ohs[:], I["ohs"].ap(), [], [b_c])
        self.dma("sp", qs[:], S["qkv"].ap()[TP:TP + 128, 0:512], [], [b_qs])
        self.cp("dve", identb[:], identf[:], [b_c], [b_c])
        self.act(etbl[:], tbl[:], AF.Exp, [b_c], [b_c])
        self.memset("pool", V1s[:], 1.0, [b_V1s])
        ev = psS[0][:, 0:144].rearrange("p (n h) -> p n h", h=8)
        for xb in range(18):
            self.mm(ev[:, xb, :], ohs[:, xb * 128:(xb + 1) * 128], etbl[:], True, True, [b_c], [b_psS[0]], last=(xb == 17))
        self.cp("dve", erev[:], ev, [], [b_psS[0], b_E])
        self.dma("sp", bass.AP(S["erev"], 0, [[8, 128], [1024, 18], [1, 8]]), erev[:], [b_E], [self.sb["erev"]])
        for t in range(8):
            self.dma("sp", Ecomb[:, t, :, :], bass.AP(S["erev"], (7 - t) * 8, [[8, 128], [1024, 17], [1, 8]]), [self.sb["erev"]], [b_E])
        Ev = Ecomb[:].rearrange("p t n h -> p h n t")
        for hp in range(4):
            self.tr(psT[0][:, hp, :], qs[:, hp * 128:(hp + 1) * 128], identb[:], [b_qs, b_c], [b_psT[0]], last=(hp == 3))
        self.cp("dve", QTs[:], psT[0][:, 0:4, :], [], [b_psT[0], b_QTs])

        def load(s):
            k = s % 2
            self.dma("pool", Kb[k][:], bass.AP(I["ck"], s * 2048 * 512, [[512, 128], [128 * 512, 16], [1, 512]]), [], [b_Kb[k]])
            self.dma("pool", Vb[k][:], bass.AP(I["cv"], s * 2048 * 512, [[512, 128], [128 * 512, 16], [1, 512]]), [], [b_Vb[k]])
            self.dma("sp", Knew[k][:], S["qkv"].ap()[TP + 8 * s:TP + 8 * s + 8, :], [], [b_Knew[k]])

        load(0)
        ti = 0
        for s in range(16):
            k = s % 2
            if s + 1 < 16:
                load(s + 1)
            for bt in range(8):
                pk = ti % 2
                ti += 1
                for nbi in range(2):
                    for hp in range(4):
                        self.tr(psT[pk][:, nbi * 4 + hp, :], Kb[k][:, 2 * bt + nbi, hp * 128:(hp + 1) * 128], identb[:], [b_Kb[k], b_c], [b_psT[pk]],
                                last=(nbi == 1 and hp == 3))
                self.cp("dve", KTs[:, :, 2 * bt:2 * bt + 2, :], psT[pk][:].rearrange("p (n h) m -> p h n m", n=2), [], [b_psT[pk], b_KTs])
            pk = ti % 2
            ti += 1
            for hp in range(4):
                self.tr(psT[pk][:, hp, 0:8], Knew[k][0:8, 512 + hp * 128:512 + (hp + 1) * 128], identb[0:8, 0:8], [b_Knew[k], b_c], [b_psT[pk]], last=(hp == 3))
            self.cp("dve", KTs[:, :, 16, 0:8], psT[pk][:, 0:4, 0:8], [], [b_psT[pk], b_KTs])
            self.cp("pool", V1s[:, 0:16, :, 0:64], Vb[k][:].rearrange("p n (h e) -> p n h e", e=64), [b_Vb[k]], [b_V1s])
            self.cp("pool", V1s[0:8, 16, :, 0:64], Knew[k][0:8, 1024:1536].rearrange("p (h e) -> p h e", e=64), [b_Knew[k]], [b_V1s])
            for hq in range(2):
                for i in range(2):
                    hp = 2 * hq + i
                    for nb in range(17):
                        M = 128 if nb < 16 else 8
                        for h2 in range(2):
                            bi = 2 * hq + h2
                            self.mm(psSv[bi][0:M, i, nb, :], KTs[64 * h2:64 * h2 + 64, hp, nb, 0:M], QTs[64 * h2:64 * h2 + 64, hp, 8 * s:8 * s + 8], True, True,
                                    [b_KTs, b_QTs], [b_psS[bi]], last=(i == 1 and nb == 16))
            exv = exs[:].rearrange("p (q i t) n m -> p q t i n m", q=2, i=2, t=2)
            PTv = PTs[:].rearrange("p (q i t) n m -> p q t i n m", q=2, i=2, t=2)
            Evv = Ev.rearrange("p (q i t) n m -> p q t i n m", q=2, i=2, t=2)
            for hq in range(2):
                for h2 in range(2):
                    bi = 2 * hq + h2
                    self.act(exv[:, hq, h2, :, 0:16, :], psSv[bi][:, :, 0:16, :], AF.Exp, [], [b_psS[bi], b_exs], scale=0.125)
                    self.act(exv[0:8, hq, h2, :, 16, :], psSv[bi][0:8, :, 16, :], AF.Exp, [], [b_psS[bi], b_exs], scale=0.125)
                    self.tt("dve", PTv[:, hq, h2, :, 0:16, :], exv[:, hq, h2, :, 0:16, :], Evv[:, hq, h2, :, 0:16, :], ALU.mult, [b_E], [b_exs, b_PTs])
                    self.tt("dve", PTv[0:8, hq, h2, :, 16, :], exv[0:8, hq, h2, :, 16, :], Evv[0:8, hq, h2, :, 16, :], ALU.mult, [b_E], [b_exs, b_PTs])
            for hh in range(2):
                for h4 in range(4):
                    h = hh * 4 + h4
                    for nb in range(17):
                        Kk = 128 if nb < 16 else 8
                        self.mm(psO[hh][0:8, h4, :], PTs[0:Kk, h, nb, :], V1s[0:Kk, nb, h, :], nb == 0, nb == 16,
                                [b_PTs, b_V1s], [b_psO[hh]], last=(h4 == 3 and nb == 16))
                self.cp("dve", osb[:, hh * 4:(hh + 1) * 4, :], psO[hh][0:8], [], [b_psO[hh], b_osb])
            self.P.op("dve", lambda e: e.reciprocal(out=rec[:], in_=osb[:, :, 64]), [], [b_osb])
            self.tt("dve", ao[k][:], osb[:, :, 0:64], rec[:].unsqueeze(2).to_broadcast([8, 8, 64]), ALU.mult, [b_osb], [b_ao[k]])
            self.dma("sp", S["atts"].ap()[8 * s:8 * s + 8, :], ao[k][:].rearrange("p h e -> p (h e)"), [b_ao[k]], [])
        self.barrier()
        self.P.emit_phase()


KB.phase2s = _phase2s


def _phase1b(self):
    nc, P, NT, NTP, TP = self.nc, self.P, self.NT, self.NTP, self.TP
    I, O, S = self.i, self.o, self.s
    import os
    NLV = int(os.environ.get("RW_LEVELS", "6"))
    with contextlib.ExitStack() as st:
        sb = lambda name, shape, dt=F32: st.enter_context(nc.sbuf_tensor(name, list(shape), dt))
        ps = lambda name, shape, dt=F32: st.enter_context(nc.psum_tensor(name, list(shape), dt))
        identf = sb("r_identf", [128, 128]); identb = sb("r_identb", [128, 128], BF16)
        tri = sb("r_tri", [128, 128]); ones = sb("r_ones", [128, 128])
        M3 = sb("r_M3", [128, 3, 128]); M2 = sb("r_M2", [128, 2, 128])
        mu = sb("r_mu", [128, NSH])
        pb = {n: sb("r_pb_" + n, [128, 512]) for n in ("w0", "a0", "k_k", "k_a", "r_k", "lnx_g", "lnx_b")}
        wl2 = sb("r_wl2", [64, 512], BF16); al2 = sb("r_al2", [64, 512], BF16)
        gl2a = sb("r_gl2a", [128, 512], BF16); gl2b = sb("r_gl2b", [32, 512], BF16)
        pt = [sb(f"r_pt{i}", [128, NSH]) for i in range(2)]
        xs = [sb(f"r_xs{i}", [128, NSH]) for i in range(2)]
        lin = sb("r_lin", [128, 288], BF16); linT = sb("r_linT", [128, 4, 128], BF16)
        T = [sb(f"r_t{i}", [128, 512]) for i in range(11)]
        fm = sb("r_fm", [128, 3, 512], BF16)
        GR = sb("r_GR", [128, 8, 64], BF16)
        BK = sb("r_BK", [128, 2, 512], BF16)
        TA = sb("r_TA", [128, 4, 4, 128], BF16)
        HB = sb("r_HB", [128, 8, 5, 128], BF16)
        Pk = [sb(f"r_Pk{i}", [128, 8, 128], BF16) for i in range(2)]
        PkT = [sb(f"r_PkT{i}", [128, 8, 128], BF16) for i in range(2)]
        Tm = sb("r_Tm", [128, 8, 128], BF16)
        NG = sb("r_NG", [128, 8, 192], BF16)
        Dt = sb("r_Dt", [64, 8, 64]); PHI = sb("r_PHI", [64, 8, 64]); OM = sb("r_OM", [64, 8, 128])
        PSL = sb("r_PSL", [128, 8, 192])
        ST = [sb(f"r_ST{i}", [64, 8, 64]) for i in range(2)]
        rwb = [sb(f"r_rwb{i}", [128, 512], BF16) for i in range(2)]
        stt_ = sb("r_st", [128, 64])
        Fb = [ps(f"r_F{i}", [128, 512]) for i in range(6)]
        Hb = [ps(f"r_H{i}", [128, 8, 128], BF16) for i in range(2)]
        bF = [Buf() for _ in range(6)]; bH = [Buf(), Buf()]
        b_c = Buf(); b_pt = [Buf(), Buf()]; b_xs = [Buf(), Buf()]; b_lin = Buf(); b_linT = Buf()
        bT = [Buf() for _ in range(11)]
        b_fm, b_GR, b_BK, b_TA, b_HB, b_Tm, b_NG, b_Dt, b_PHI, b_OM, b_PSL, b_st = (Buf() for _ in range(12))
        b_Pk = [Buf(), Buf()]; b_PkT = [Buf(), Buf()]; b_ST = [Buf(), Buf()]; b_rwb = [Buf(), Buf()]

        cst = I["cst"].ap()
        self.dma("sp", identf[:], cst[:, C_ID, :], [], [b_c])
        self.dma("sp", tri[:], cst[:, C_TRI, :], [], [b_c])
        self.dma("sp", ones[:], cst[:, C_ONES, :], [], [b_c])
        self.dma("sp", M3[:, 0, :], cst[:, C_MUS, :], [], [b_c])
        self.dma("sp", M3[:, 1, :], cst[:, C_MUI, :], [], [b_c])
        self.dma("sp", M3[:, 2, :], cst[:, C_MUI, :], [], [b_c])
        self.dma("sp", M2[:, 0, :], cst[:, C_MLS, :], [], [b_c])
        self.dma("sp", M2[:, 1, :], cst[:, C_MLS, :], [], [b_c])
        self.dma("sp", mu[:], self.bcast_row(I["mu_shift"]), [], [b_c])
        for n in pb:
            self.dma("sp", pb[n][:], self.bcast_row(I[n]), [], [b_c])
        self.dma("pool", wl2[:], I["w_lora2"].ap(), [], [b_c])
        self.dma("pool", al2[:], I["a_lora2"].ap(), [], [b_c])
        self.dma("pool", gl2a[:], I["g_lora2"].ap()[0:128, :], [], [b_c])
        self.dma("pool", gl2b[:], I["g_lora2"].ap()[128:160, :], [], [b_c])
        self.cp("dve", identb[:], identf[:], [b_c], [b_c])
        self.memset("pool", ST[0][:], 0.0, [b_ST[0]])

        h3 = lambda ap: ap.rearrange("p (h e) -> p h e", e=64)
        bc8 = lambda ap: ap.unsqueeze(2).to_broadcast([128, 8, 64])

        def load(i):
            k = i % 2
            r0 = i * 128
            self.dma("sp", pt[k][:], S["p"].ap()[r0:r0 + 128, :], [], [b_pt[k]])
            if i == 0:
                self.memset("pool", xs[k][0:1, :], 0.0, [b_xs[k]])
                self.dma("sp", xs[k][1:128, :], S["p"].ap()[0:127, :], [], [b_xs[k]])
            elif i < NTP:
                self.dma("sp", xs[k][:], S["p"].ap()[r0 - 1:r0 + 127, :], [], [b_xs[k]])
            else:
                self.dma("sp", xs[k][1:128, :], S["p"].ap()[r0:r0 + 127, :], [], [b_xs[k]])
                for s in range(16):
                    self.dma("sp", xs[k][8 * s:8 * s + 1, :], I["sshift"].ap()[s:s + 1, :], [], [b_xs[k]])

        def rstage(i):
            k = i % 2
            X, Pt = xs[k], pt[k]
            self.tt("pool", X[:], X[:], Pt[:], ALU.subtract, [b_pt[k]], [b_xs[k]])
            self.tt("pool", X[:], X[:], mu[:], ALU.mult, [b_c], [b_xs[k]])
            self.tt("dve", X[:], X[:], Pt[:], ALU.add, [b_pt[k]], [b_xs[k]])
            self.act(lin[:, 0:64], X[:, 1536:1600], AF.Tanh, [b_xs[k]], [b_lin])
            self.act(lin[:, 64:128], X[:, 1600:1664], AF.Copy, [b_xs[k]], [b_lin])
            self.act(lin[:, 128:288], X[:, 1664:1824], AF.Sigmoid, [b_xs[k]], [b_lin])
            self.tr(Hb[0][0:64, 0, :], lin[:, 0:64], identb[:], [b_lin, b_c], [bH[0]], last=False)
            self.tr(Hb[0][0:64, 1, :], lin[:, 64:128], identb[:], [b_lin, b_c], [bH[0]], last=False)
            self.tr(Hb[0][:, 2, :], lin[:, 128:256], identb[:], [b_lin, b_c], [bH[0]], last=False)
            self.tr(Hb[0][0:32, 3, :], lin[:, 256:288], identb[:], [b_lin, b_c], [bH[0]], last=True)
            self.cp("dve", linT[0:64, 0:2, :], Hb[0][0:64, 0:2, :], [], [bH[0], b_linT])
            self.cp("dve", linT[:, 2, :], Hb[0][:, 2, :], [], [bH[0], b_linT])
            self.cp("dve", linT[0:32, 3, :], Hb[0][0:32, 3, :], [], [bH[0], b_linT])
            self.mm(Fb[0][:], linT[0:64, 0, :], wl2[:], True, True, [b_linT, b_c], [bF[0]])
            self.mm(Fb[1][:], linT[0:64, 1, :], al2[:], True, True, [b_linT, b_c], [bF[1]])
            self.mm(Fb[2][:], linT[:, 2, :], gl2a[:], True, False, [b_linT, b_c], [bF[2]])
            self.mm(Fb[2][:], linT[0:32, 3, :], gl2b[:], False, True, [b_linT, b_c], [bF[2]])
            r_, k_, v_ = X[:, 0:512], X[:, 512:1024], X[:, 1024:1536]
            self.tt("dve", T[0][:], Fb[0][:], pb["w0"][:], ALU.add, [b_c], [bF[0], bT[0]])
            self.act(T[0][:], T[0][:], AF.Sigmoid, [], [bT[0]])
            self.ts("pool", T[0][:], T[0][:], -math.exp(-0.5), None, ALU.mult, None, [], [bT[0]])
            self.tt("dve", T[1][:], Fb[1][:], pb["a0"][:], ALU.add, [b_c], [bF[1], bT[1]])
            self.act(T[1][:], T[1][:], AF.Sigmoid, [], [bT[1]])
            self.cp("act", T[6][:], Fb[2][:], [], [bF[2], bT[6]])
            self.tt("pool", T[2][:], k_, pb["k_k"][:], ALU.mult, [b_xs[k], b_c], [bT[2]])
            self.tt("dve", T[5][:], T[2][:], T[2][:], ALU.mult, [bT[2]], [bT[5]])
            self.red("dve", stt_[:, 0:8], h3(T[5][:]), [bT[5]], [b_st])
            self.ts("dve", stt_[:, 0:8], stt_[:, 0:8], 1e-24, None, ALU.max, None, [], [b_st])
            self.act(stt_[:, 8:16], stt_[:, 0:8], AF.Sqrt, [], [b_st])
            self.P.op("dve", lambda e: e.reciprocal(out=stt_[:, 0:8], in_=stt_[:, 8:16]), [], [b_st])
            self.tt("dve", h3(T[2][:]), h3(T[2][:]), bc8(stt_[:, 0:8]), ALU.mult, [b_st], [bT[2]])
            self.stt("dve", T[3][:], T[1][:], -1.0, pb["k_a"][:], ALU.add, ALU.mult, [bT[1], b_c], [bT[3]])
            self.stt("dve", T[3][:], T[3][:], 1.0, k_, ALU.add, ALU.mult, [b_xs[k]], [bT[3]])
            self.tt("pool", T[4][:], T[2][:], T[1][:], ALU.mult, [bT[2], bT[1]], [bT[4]])
            self.tt("dve", T[5][:], r_, T[3][:], ALU.mult, [b_xs[k], bT[3]], [bT[5]])
            self.tt("pool", T[5][:], T[5][:], pb["r_k"][:], ALU.mult, [b_c], [bT[5]])
            self.red("dve", stt_[:, 16:24], h3(T[5][:]), [bT[5]], [b_st])

        def post(i, ysrc, ybuf_r, ybuf_w):
            k = i % 2
            X = xs[k]
            v_ = X[:, 1024:1536]
            self.cp("act", T[8][:], ysrc, ybuf_r, ybuf_w + [bT[8]])
            self.red("dve", stt_[:, 24:32], h3(T[8][:]), [bT[8]], [b_st])
            self.ts("dve", stt_[:, 24:32], stt_[:, 24:32], 1.0 / 64, None, ALU.mult, None, [], [b_st])
            self.tt("dve", h3(T[8][:]), h3(T[8][:]), bc8(stt_[:, 24:32]), ALU.subtract, [b_st], [bT[8]])
            self.tt("pool", T[5][:], T[8][:], T[8][:], ALU.mult, [bT[8]], [bT[5]])
            self.red("dve", stt_[:, 32:40], h3(T[5][:]), [bT[5]], [b_st])
            self.act(stt_[:, 40:48], stt_[:, 32:40], AF.Sqrt, [], [b_st], scale=1.0 / 64, bias=GN_EPS)
            self.P.op("dve", lambda e: e.reciprocal(out=stt_[:, 32:40], in_=stt_[:, 40:48]), [], [b_st])
            self.tt("dve", h3(T[8][:]), h3(T[8][:]), bc8(stt_[:, 32:40]), ALU.mult, [b_st], [bT[8]])
            self.tt("pool", T[8][:], T[8][:], pb["lnx_g"][:], ALU.mult, [b_c], [bT[8]])
            self.tt("pool", T[8][:], T[8][:], pb["lnx_b"][:], ALU.add, [b_c], [bT[8]])
            self.tt("dve", h3(T[5][:]), h3(v_), bc8(stt_[:, 16:24]), ALU.mult, [b_xs[k], b_st], [bT[5]])
            self.tt("pool", T[8][:], T[8][:], T[5][:], ALU.add, [bT[5]], [bT[8]])
            self.tt("dve", rwb[k][:], T[8][:], T[6][:], ALU.mult, [bT[8], bT[6]], [b_rwb[k]])
            self.dma("sp", S["rw"].ap()[i * 128:(i + 1) * 128, :], rwb[k][:], [b_rwb[k]], [])

        load(0)
        cur = 0
        for i in range(NTP):
            k = i % 2
            load(i + 1)
            rstage(i)
            X = xs[k]
            r_, v_ = X[:, 0:512], X[:, 1024:1536]
            self.mm(Fb[3][:], tri[:], T[0][:], True, True, [b_c, bT[0]], [bF[3]])
            self.mm(Fb[4][:], ones[:], T[0][:], True, True, [b_c, bT[0]], [bF[4]])
            self.cp("act", T[9][:], Fb[4][:], [], [bF[4], bT[9]])
            self.cp("dve", T[7][:], Fb[3][:], [], [bF[3], bT[7]])
            self.act(T[8][:], T[7][:], AF.Exp, [bT[7]], [bT[8]])
            self.tt("dve", fm[:, 0, :], r_, T[8][:], ALU.mult, [b_xs[k], bT[8]], [b_fm])
            self.act(T[8][:], T[7][:], AF.Exp, [bT[7]], [bT[8]], scale=-1.0)
            self.tt("dve", fm[:, 1, :], T[3][:], T[8][:], ALU.mult, [bT[3], bT[8]], [b_fm])
            self.tt("pool", fm[:, 2, :], T[4][:], T[8][:], ALU.mult, [bT[4], bT[8]], [b_fm])
            self.tt("pool", T[10][:], T[7][:], T[0][:], ALU.subtract, [bT[7], bT[0]], [bT[10]])
            self.act(T[10][:], T[10][:], AF.Exp, [], [bT[10]])
            self.tt("dve", GR[:].rearrange("p h e -> p (h e)"), T[2][:], T[10][:], ALU.mult, [bT[2], bT[10]], [b_GR])
            self.tt("pool", T[10][:], T[9][:], T[7][:], ALU.subtract, [bT[9], bT[7]], [bT[10]])
            self.act(T[10][:], T[10][:], AF.Exp, [], [bT[10]])
            self.tt("dve", BK[:, 0, :], T[4][:], T[10][:], ALU.mult, [bT[4], bT[10]], [b_BK])
            self.tt("pool", BK[:, 1, :], T[3][:], T[10][:], ALU.mult, [bT[3], bT[10]], [b_BK])
            self.act(T[9][0:64, :], T[9][0:64, :], AF.Exp, [], [bT[9]])
            self.tt("dve", Dt[:], identf[0:64, 0:64].unsqueeze(1).to_broadcast([64, 8, 64]), T[9][0:64, :].rearrange("p (h e) -> p h e", e=64),
                    ALU.mult, [b_c, bT[9]], [b_Dt])
            for q in range(2):
                for hpi in range(2):
                    hp = 2 * q + hpi
                    srcs = [GR[:, 2 * hp:2 * hp + 2, :].rearrange("p h e -> p (h e)"), fm[:, 0, hp * 128:(hp + 1) * 128],
                            fm[:, 1, hp * 128:(hp + 1) * 128], fm[:, 2, hp * 128:(hp + 1) * 128]]
                    for w_, src in enumerate(srcs):
                        self.tr(Hb[q][:, hpi * 4 + w_, :], src, identb[:], [b_GR, b_fm, b_c], [bH[q]], last=(hpi == 1 and w_ == 3))
                self.cp("dve" if q == 0 else "act", TA[:, 2 * q:2 * q + 2, :, :].rearrange("p a b m -> p (a b) m"), Hb[q][:], [], [bH[q], b_TA])
            for hp in range(4):
                for h2 in range(2):
                    pr = slice(64 * h2, 64 * h2 + 64)
                    f1, f2 = Fb[2 * h2], Fb[2 * h2 + 1]
                    w1, w2 = bF[2 * h2], bF[2 * h2 + 1]
                    self.mm(f1[:, 0:256], TA[pr, hp, 3, :], TA[pr, hp, 0:2, :].rearrange("p a m -> p (a m)"), True, True, [b_TA], [w1], last=False)
                    self.mm(f1[:, 256:384], TA[pr, hp, 2, :], TA[pr, hp, 1, :], True, True, [b_TA], [w1], last=False)
                    self.mm(f2[:, 0:256], TA[pr, hp, 0, :], TA[pr, hp, 2:4, :].rearrange("p a m -> p (a m)"), True, True, [b_TA], [w2], last=(h2 == 1))
                for h2 in range(2):
                    h = 2 * hp + h2
                    self.tt("dve", HB[:, h, 0:3, :].rearrange("p a m -> p (a m)"), Fb[2 * h2][:, 0:384], M3[:].rearrange("p a m -> p (a m)"), ALU.mult,
                            [b_c], [bF[2 * h2], b_HB])
                    self.tt("dve", HB[:, h, 3:5, :].rearrange("p a m -> p (a m)"), Fb[2 * h2 + 1][:, 0:256], M2[:].rearrange("p a m -> p (a m)"), ALU.mult,
                            [b_c], [bF[2 * h2 + 1], b_HB])
            self.tt("pool", Tm[:], identb[:].unsqueeze(1).to_broadcast([128, 8, 128]), HB[:, :, 0, :], ALU.subtract, [b_c, b_HB], [b_Tm])
            pk_r = lambda hh: HB[:, hh, 0, :]
            pkT_r = lambda hh: HB[:, hh, 4, :]
            bk_r, bkT_r = b_HB, b_HB
            for lv in range(NLV):
                pw = lv % 2
                lastlv = (lv == NLV - 1)
                for g in range(2):
                    fa, fb, fc = Fb[3 * g], Fb[3 * g + 1], Fb[3 * g + 2]
                    wa, wb, wc = bF[3 * g], bF[3 * g + 1], bF[3 * g + 2]
                    for h4 in range(4):
                        hh = 4 * g + h4
                        if not lastlv:
                            self.mm(fa[:, h4 * 128:(h4 + 1) * 128], pkT_r(hh), pk_r(hh), True, True, [bk_r, bkT_r], [wa], last=(h4 == 3))
                    for h4 in range(4):
                        hh = 4 * g + h4
                        self.mm(fb[:, h4 * 128:(h4 + 1) * 128], pk_r(hh), pkT_r(hh), True, True, [bk_r, bkT_r], [wb], last=(h4 == 3))
                    if not lastlv:
                        self.cp("act", Pk[pw][:, 4 * g:4 * g + 4, :].rearrange("p h m -> p (h m)"), fa[:], [], [wa, b_Pk[pw]])
                    self.cp("dve", PkT[pw][:, 4 * g:4 * g + 4, :].rearrange("p h m -> p (h m)"), fb[:], [], [wb, b_PkT[pw]])
                    for h4 in range(4):
                        hh = 4 * g + h4
                        self.mm(fc[:, h4 * 128:(h4 + 1) * 128], PkT[pw][:, hh, :], Tm[:, hh, :], True, True, [b_PkT[pw], b_Tm], [wc], last=(h4 == 3))
                    self.tt("dve", Tm[:, 4 * g:4 * g + 4, :].rearrange("p h m -> p (h m)"), fc[:], Tm[:, 4 * g:4 * g + 4, :].rearrange("p h m -> p (h m)"),
                            ALU.add, [], [wc, b_Tm])
                pk_r = (lambda hh, pw=pw: Pk[pw][:, hh, :])
                pkT_r = (lambda hh, pw=pw: PkT[pw][:, hh, :])
                bk_r, bkT_r = b_Pk[pw], b_PkT[pw]
            for hq in range(4):
                f = Fb[hq % 2]
                w = bF[hq % 2]
                for h2 in range(2):
                    h = 2 * hq + h2
                    self.mm(f[:, h2 * 192:h2 * 192 + 64], Tm[:, h, :], GR[:, h, :], True, True, [b_Tm, b_GR], [w], last=False)
                    self.mm(f[:, h2 * 192 + 64:h2 * 192 + 192], Tm[:, h, :], HB[:, h, 3, :], True, True, [b_Tm, b_HB], [w], last=(h2 == 1))
                self.ts("dve", NG[:, 2 * hq:2 * hq + 2, :].rearrange("p h m -> p (h m)"), f[:, 0:384], -1.0, None, ALU.mult, None, [], [w, b_NG])
            for hq in range(4):
                fo, fl = Fb[2 + (hq % 2) * 2], Fb[3 + (hq % 2) * 2]
                wo, wl = bF[2 + (hq % 2) * 2], bF[3 + (hq % 2) * 2]
                for h2 in range(2):
                    h = 2 * hq + h2
                    pr = slice(64 * h2, 64 * h2 + 64)
                    c0 = h2 * 192
                    self.mm(fo[0:64, c0:c0 + 64], NG[:, h, 0:64], BK[:, 0, h * 64:(h + 1) * 64], True, True, [b_NG, b_BK], [wo], last=False)
                    self.mm(fo[0:64, c0 + 64:c0 + 192], NG[:, h, 0:64], HB[:, h, 1, :], True, False, [b_NG, b_HB], [wo], last=False)
                    self.mm(fo[0:64, c0 + 64:c0 + 192], identb[pr, pr], TA[pr, hq, 1, :], False, True, [b_c, b_TA], [wo], last=(h2 == 1))
                for h2 in range(2):
                    h = 2 * hq + h2
                    c0 = h2 * 192
                    self.mm(fl[:, c0:c0 + 64], NG[:, h, 64:192], BK[:, 0, h * 64:(h + 1) * 64], True, False, [b_NG, b_BK], [wl], last=False)
                    self.mm(fl[:, c0:c0 + 64], identb[:], BK[:, 1, h * 64:(h + 1) * 64], False, True, [b_c, b_BK], [wl], last=False)
                    self.mm(fl[:, c0 + 64:c0 + 192], NG[:, h, 64:192], HB[:, h, 1, :], True, False, [b_NG, b_HB], [wl], last=False)
                    self.mm(fl[:, c0 + 64:c0 + 192], identb[:], HB[:, h, 2, :], False, True, [b_c, b_HB], [wl], last=(h2 == 1))
                fo3 = fo[0:64, 0:384].rearrange("p (h m) -> p h m", m=192)
                self.tt("dve", PHI[:, 2 * hq:2 * hq + 2, :], fo3[:, :, 0:64], Dt[:, 2 * hq:2 * hq + 2, :], ALU.add, [b_Dt], [wo, b_PHI])
                self.cp("dve", OM[:, 2 * hq:2 * hq + 2, :], fo3[:, :, 64:192], [], [wo, b_OM])
                self.cp("act", PSL[:, 2 * hq:2 * hq + 2, :].rearrange("p h m -> p (h m)"), fl[:, 0:384], [], [wl, b_PSL])
            nxt = 1 - cur
            for h in range(8):
                self.mm(Fb[0][:, h * 64:(h + 1) * 64], OM[:, h, :], ST[cur][:, h, :], True, False, [b_OM, b_ST[cur]], [bF[0]], last=False)
                self.mm(Fb[0][:, h * 64:(h + 1) * 64], PSL[:, h, 64:192], v_[:, h * 64:(h + 1) * 64], False, True, [b_PSL, b_xs[k]], [bF[0]], last=(h == 7))
            for h in range(8):
                self.mm(Fb[1][0:64, h * 64:(h + 1) * 64], PHI[:, h, :], ST[cur][:, h, :], True, False, [b_PHI, b_ST[cur]], [bF[1]], last=False)
                self.mm(Fb[1][0:64, h * 64:(h + 1) * 64], PSL[:, h, 0:64], v_[:, h * 64:(h + 1) * 64], False, True, [b_PSL, b_xs[k]], [bF[1]], last=(h == 7))
            self.cp("dve", ST[nxt][:].rearrange("p h m -> p (h m)"), Fb[1][0:64, :], [], [bF[1], b_ST[nxt]])
            cur = nxt
            post(i, Fb[0][:], [], [bF[0]])
        for h in range(8):
            self.mm(Fb[2][0:64, h * 64:(h + 1) * 64], ST[cur][:, h, :], identf[0:64, 0:64], True, True, [b_ST[cur], b_c], [bF[2]], last=(h == 7))
        self.cp("dve", PHI[:].rearrange("p h m -> p (h m)"), Fb[2][0:64, :], [], [bF[2], b_PHI])
        self.dma("sp", bass.AP(O["wkvp"], 0, [[64, 64], [4096, 8], [1, 64]]), PHI[:], [b_PHI], [])
        self.rwkv_sample(st, locals())
        self.barrier()
        self.P.emit_phase()


KB.phase1b = _phase1b


def _rwkv_sample(self, st, L):
    nc, NTP = self.nc, self.NTP
    I, O, S = self.i, self.o, self.s
    sb = lambda name, shape, dt=F32: st.enter_context(nc.sbuf_tensor(name, list(shape), dt))
    T, bT, xs, b_xs, rstage, post = L["T"], L["bT"], L["xs"], L["b_xs"], L["rstage"], L["post"]
    i = NTP
    k = i % 2
    Ssb = sb("r_Ssb", [128, 4096]); tmp = sb("r_tmp", [128, 4096])
    vec = sb("r_vec", [128, 6, 8, 64]); ysb = sb("r_ysb", [128, 8, 64]); ytm = sb("r_ytm", [128, 512]); sk = sb("r_sk", [128, 64])
    b_S, b_tmp, b_vec, b_ys, b_ytm, b_sk, b_rs, b_yscr = (Buf() for _ in range(8))
    self.dma("sp", Ssb[:], I["swkv"].ap(), [], [b_S])
    rstage(i)
    self.act(T[0][:], T[0][:], AF.Exp, [], [bT[0]])
    X = xs[k]
    srcs = [(X[:, 0:512], b_xs[k]), (T[0][:], bT[0]), (T[3][:], bT[3]), (X[:, 1024:1536], b_xs[k]), (T[2][:], bT[2]), (T[4][:], bT[4])]
    for q, (ap, bb) in enumerate(srcs):
        self.dma("sp", S["rs"].ap()[:, q * 512:(q + 1) * 512], ap, [bb], [b_rs])
    for q in range(6):
        for s in range(16):
            self.dma("sp", vec[8 * s:8 * s + 8, q, :, :], bass.AP(S["rs"], s * 8 * 3072 + q * 512, [[64, 8], [3072, 8], [1, 64]]), [b_rs], [b_vec])
    S3 = Ssb[:].rearrange("p (i j) -> p i j", j=64)
    t3 = tmp[:].rearrange("p (i j) -> p i j", j=64)
    bi = lambda ap: ap.unsqueeze(1).to_broadcast([128, 64, 64])
    bj = lambda ap: ap.unsqueeze(2).to_broadcast([128, 64, 64])
    for t in range(8):
        r_t, w_t, k_t, v_t, kap_t, b_t = (vec[:, q, t, :] for q in range(6))
        self.tt("dve", t3, S3, bi(kap_t), ALU.mult, [b_S, b_vec], [b_tmp])
        self.red("dve", sk[:], t3, [b_tmp], [b_sk])
        self.tt("pool", S3, S3, bi(w_t), ALU.mult, [b_vec], [b_S])
        self.tt("dve", t3, bj(sk[:]), bi(b_t), ALU.mult, [b_sk, b_vec], [b_tmp])
        self.tt("pool", S3, S3, t3, ALU.subtract, [b_tmp], [b_S])
        self.tt("dve", t3, bj(v_t), bi(k_t), ALU.mult, [b_vec], [b_tmp])
        self.tt("pool", S3, S3, t3, ALU.add, [b_tmp], [b_S])
        self.tt("dve", t3, S3, bi(r_t), ALU.mult, [b_S, b_vec], [b_tmp])
        self.red("dve", ysb[:, t, :], t3, [b_tmp], [b_ys])
    self.dma("sp", O["wkvs"].ap(), Ssb[:], [b_S], [])
    for s in range(16):
        self.dma("sp", bass.AP(S["ys"], s * 8 * 512, [[64, 8], [512, 8], [1, 64]]), ysb[8 * s:8 * s + 8, :, :], [b_ys], [b_yscr])
    self.dma("sp", ytm[:], S["ys"].ap(), [b_yscr], [b_ytm])
    post(i, ytm[:], [b_ytm], [])


KB.rwkv_sample = _rwkv_sample


class _NS:
    pass


def _phase1b_v2(self):
    nc, P, NT, NTP, TP = self.nc, self.P, self.NT, self.NTP, self.TP
    I, O, S = self.i, self.o, self.s
    import os
    NLV = int(os.environ.get("RW_LEVELS", "6"))
    h3 = lambda ap: ap.rearrange("p (h e) -> p h e", e=64)
    bc8 = lambda ap: ap.unsqueeze(2).to_broadcast([128, 8, 64])
    with contextlib.ExitStack() as st0:
        sb0 = lambda name, shape, dt=F32: st0.enter_context(nc.sbuf_tensor(name, list(shape), dt))
        ps = lambda name, shape, dt=F32: st0.enter_context(nc.psum_tensor(name, list(shape), dt))
        identf = sb0("r_identf", [128, 128]); identb = sb0("r_identb", [128, 128], BF16)
        tri = sb0("r_tri", [128, 128]); ones = sb0("r_ones", [128, 128])
        M3 = sb0("r_M3", [128, 3, 128]); M2 = sb0("r_M2", [128, 2, 128])
        mu = sb0("r_mu", [128, NSH])
        pb = {n: sb0("r_pb_" + n, [128, 512]) for n in ("w0", "a0", "k_k", "k_a", "r_k", "lnx_g", "lnx_b")}
        wl2 = sb0("r_wl2", [64, 512], BF16); al2 = sb0("r_al2", [64, 512], BF16)
        gl2a = sb0("r_gl2a", [128, 512], BF16); gl2b = sb0("r_gl2b", [32, 512], BF16)
        ST = [sb0(f"r_ST{i}", [64, 8, 64]) for i in range(2)]
        Fb = [ps(f"r_F{i}", [128, 512]) for i in range(6)]
        Hb = [ps(f"r_H{i}", [128, 8, 128], BF16) for i in range(2)]
        bF = [Buf() for _ in range(6)]; bH = [Buf(), Buf()]
        b_c = Buf(); b_ST = [Buf(), Buf()]
        cst = I["cst"].ap()
        self.dma("sp", identf[:], cst[:, C_ID, :], [], [b_c])
        self.dma("sp", tri[:], cst[:, C_TRI, :], [], [b_c])
        self.dma("sp", ones[:], cst[:, C_ONES, :], [], [b_c])
        for j_, cc in enumerate((C_MUS, C_MUI, C_MUI)):
            self.dma("sp", M3[:, j_, :], cst[:, cc, :], [], [b_c])
        for j_ in range(2):
            self.dma("sp", M2[:, j_, :], cst[:, C_MLS, :], [], [b_c])
        self.dma("sp", mu[:], self.bcast_row(I["mu_shift"]), [], [b_c])
        for n in pb:
            self.dma("sp", pb[n][:], self.bcast_row(I[n]), [], [b_c])
        self.dma("pool", wl2[:], I["w_lora2"].ap(), [], [b_c])
        self.dma("pool", al2[:], I["a_lora2"].ap(), [], [b_c])
        self.dma("pool", gl2a[:], I["g_lora2"].ap()[0:128, :], [], [b_c])
        self.dma("pool", gl2b[:], I["g_lora2"].ap()[128:160, :], [], [b_c])
        self.cp("dve", identb[:], identf[:], [b_c], [b_c])
        self.memset("pool", ST[0][:], 0.0, [b_ST[0]])

        def mkset(stk, tag, par=0):
            sb = lambda name, shape, dt=F32: stk.enter_context(nc.sbuf_tensor(f"r{tag}_{name}", list(shape), dt))
            z = _NS()
            z.F = [Fb[3 * par + j_] for j_ in range(3)]; z.bF = [bF[3 * par + j_] for j_ in range(3)]
            z.H = Hb[par]; z.bH = bH[par]
            z.pt = sb("pt", [128, NSH]); z.xs = sb("xs", [128, NSH])
            z.lin = sb("lin", [128, 288], BF16); z.linT = sb("linT", [128, 4, 128], BF16)
            z.T = [sb(f"t{i}", [128, 512]) for i in range(11)]
            z.fm = sb("fm", [128, 3, 512], BF16); z.GR = sb("GR", [128, 8, 64], BF16); z.BK = sb("BK", [128, 2, 512], BF16)
            z.TA = sb("TA", [128, 4, 4, 128], BF16); z.HB = sb("HB", [128, 8, 5, 128], BF16)
            z.Pk = [sb(f"Pk{i}", [128, 8, 128], BF16) for i in range(2)]
            z.PkT = [sb(f"PkT{i}", [128, 8, 128], BF16) for i in range(2)]
            z.Tm = sb("Tm", [128, 8, 128], BF16); z.NG = sb("NG", [128, 8, 192], BF16)
            z.Dt = sb("Dt", [64, 8, 64]); z.PHI = sb("PHI", [64, 8, 64]); z.OM = sb("OM", [64, 8, 128])
            z.PSL = z.pt[:, 0:1536].rearrange("p (h m) -> p h m", m=192)
            z.rwb = sb("rwb", [128, 512], BF16); z.st = sb("st", [128, 64])
            z.b_pt, z.b_xs, z.b_lin, z.b_linT = Buf(), Buf(), Buf(), Buf()
            z.bT = [Buf() for _ in range(11)]
            (z.b_fm, z.b_GR, z.b_BK, z.b_TA, z.b_HB, z.b_Tm, z.b_NG, z.b_Dt, z.b_PHI, z.b_OM, z.b_PSL, z.b_st, z.b_rwb) = (Buf() for _ in range(13))
            z.b_PSL = z.b_pt
            z.b_Pk = [Buf(), Buf()]; z.b_PkT = [Buf(), Buf()]
            return z

        def load(i, z):
            r0 = i * 128
            self.dma("sp", z.pt[:], S["p"].ap()[r0:r0 + 128, :], [], [z.b_pt])
            if i == 0:
                self.memset("pool", z.xs[0:1, :], 0.0, [z.b_xs])
                self.dma("sp", z.xs[1:128, :], S["p"].ap()[0:127, :], [], [z.b_xs])
            elif i < NTP:
                self.dma("sp", z.xs[:], S["p"].ap()[r0 - 1:r0 + 127, :], [], [z.b_xs])
            else:
                self.dma("sp", z.xs[1:128, :], S["p"].ap()[r0:r0 + 127, :], [], [z.b_xs])
                z.b_xrows = [Buf() for _ in range(16)]
                for s in range(16):
                    self.dma("sp", z.xs[8 * s:8 * s + 1, :], I["sshift"].ap()[s:s + 1, :], [z.b_xs], [z.b_xrows[s]])

        def rstage(i, z):
            X, Pt, T, bT, stt_ = z.xs, z.pt, z.T, z.bT, z.st
            self.tt("dve", X[:], X[:], Pt[:], ALU.subtract, [z.b_pt] + getattr(z, "b_xrows", []), [z.b_xs])
            yield
            self.tt("dve", X[:], X[:], mu[:], ALU.mult, [b_c], [z.b_xs])
            yield
            self.tt("dve", X[:], X[:], Pt[:], ALU.add, [z.b_pt], [z.b_xs])
            self.act(z.lin[:, 0:64], X[:, 1536:1600], AF.Tanh, [z.b_xs], [z.b_lin])
            self.act(z.lin[:, 64:128], X[:, 1600:1664], AF.Copy, [z.b_xs], [z.b_lin])
            self.act(z.lin[:, 128:288], X[:, 1664:1824], AF.Sigmoid, [z.b_xs], [z.b_lin])
            yield
            self.tr(z.H[0:64, 0, :], z.lin[:, 0:64], identb[:], [z.b_lin, b_c], [z.bH], last=False)
            self.tr(z.H[0:64, 1, :], z.lin[:, 64:128], identb[:], [z.b_lin, b_c], [z.bH], last=False)
            self.tr(z.H[:, 2, :], z.lin[:, 128:256], identb[:], [z.b_lin, b_c], [z.bH], last=False)
            self.tr(z.H[0:32, 3, :], z.lin[:, 256:288], identb[:], [z.b_lin, b_c], [z.bH], last=True)
            self.cp("dve", z.linT[0:64, 0:2, :], z.H[0:64, 0:2, :], [], [z.bH, z.b_linT])
            self.cp("dve", z.linT[:, 2, :], z.H[:, 2, :], [], [z.bH, z.b_linT])
            self.cp("dve", z.linT[0:32, 3, :], z.H[0:32, 3, :], [], [z.bH, z.b_linT])
            yield
            self.mm(z.F[0][:], z.linT[0:64, 0, :], wl2[:], True, True, [z.b_linT, b_c], [z.bF[0]])
            self.mm(z.F[1][:], z.linT[0:64, 1, :], al2[:], True, True, [z.b_linT, b_c], [z.bF[1]])
            self.mm(z.F[2][:], z.linT[:, 2, :], gl2a[:], True, False, [z.b_linT, b_c], [z.bF[2]])
            self.mm(z.F[2][:], z.linT[0:32, 3, :], gl2b[:], False, True, [z.b_linT, b_c], [z.bF[2]])
            r_, k_, v_ = X[:, 0:512], X[:, 512:1024], X[:, 1024:1536]
            self.tt("dve", T[0][:], z.F[0][:], pb["w0"][:], ALU.add, [b_c], [z.bF[0], bT[0]])
            self.tt("dve", T[1][:], z.F[1][:], pb["a0"][:], ALU.add, [b_c], [z.bF[1], bT[1]])
            self.cp("act", T[6][:], z.F[2][:], [], [z.bF[2], bT[6]])
            yield
            self.act(T[0][:], T[0][:], AF.Sigmoid, [], [bT[0]])
            self.act(T[1][:], T[1][:], AF.Sigmoid, [], [bT[1]])
            self.ts("dve", T[0][:], T[0][:], -math.exp(-0.5), None, ALU.mult, None, [], [bT[0]])
            self.tt("pool", T[2][:], k_, pb["k_k"][:], ALU.mult, [z.b_xs, b_c], [bT[2]])
            yield
            self.tt("dve", T[5][:], T[2][:], T[2][:], ALU.mult, [bT[2]], [bT[5]])
            self.red("dve", stt_[:, 0:8], h3(T[5][:]), [bT[5]], [z.b_st])
            self.ts("dve", stt_[:, 0:8], stt_[:, 0:8], 1e-24, None, ALU.max, None, [], [z.b_st])
            self.act(stt_[:, 8:16], stt_[:, 0:8], AF.Sqrt, [], [z.b_st])
            yield
            self.P.op("dve", lambda e: e.reciprocal(out=stt_[:, 0:8], in_=stt_[:, 8:16]), [], [z.b_st])
            self.tt("dve", h3(T[2][:]), h3(T[2][:]), bc8(stt_[:, 0:8]), ALU.mult, [z.b_st], [bT[2]])
            self.stt("dve", T[3][:], T[1][:], -1.0, pb["k_a"][:], ALU.add, ALU.mult, [bT[1], b_c], [bT[3]])
            yield
            self.stt("dve", T[3][:], T[3][:], 1.0, k_, ALU.add, ALU.mult, [z.b_xs], [bT[3]])
            self.tt("pool", T[4][:], T[2][:], T[1][:], ALU.mult, [bT[2], bT[1]], [bT[4]])
            yield
            self.tt("dve", T[5][:], r_, T[3][:], ALU.mult, [z.b_xs, bT[3]], [bT[5]])
            self.tt("pool", T[5][:], T[5][:], pb["r_k"][:], ALU.mult, [b_c], [bT[5]])
            self.red("dve", stt_[:, 16:24], h3(T[5][:]), [bT[5]], [z.b_st])
            yield

        def post(i, z, ysrc, ybuf_r, ybuf_w):
            T, bT, stt_ = z.T, z.bT, z.st
            v_ = z.xs[:, 1024:1536]
            self.cp("act", T[8][:], ysrc, ybuf_r, ybuf_w + [bT[8]])
            self.red("dve", stt_[:, 24:32], h3(T[8][:]), [bT[8]], [z.b_st])
            self.ts("dve", stt_[:, 24:32], stt_[:, 24:32], 1.0 / 64, None, ALU.mult, None, [], [z.b_st])
            self.tt("dve", h3(T[8][:]), h3(T[8][:]), bc8(stt_[:, 24:32]), ALU.subtract, [z.b_st], [bT[8]])
            self.tt("pool", T[5][:], T[8][:], T[8][:], ALU.mult, [bT[8]], [bT[5]])
            self.red("dve", stt_[:, 32:40], h3(T[5][:]), [bT[5]], [z.b_st])
            self.act(stt_[:, 40:48], stt_[:, 32:40], AF.Sqrt, [], [z.b_st], scale=1.0 / 64, bias=GN_EPS)
            self.P.op("dve", lambda e: e.reciprocal(out=stt_[:, 32:40], in_=stt_[:, 40:48]), [], [z.b_st])
            self.tt("dve", h3(T[8][:]), h3(T[8][:]), bc8(stt_[:, 32:40]), ALU.mult, [z.b_st], [bT[8]])
            self.tt("pool", T[8][:], T[8][:], pb["lnx_g"][:], ALU.mult, [b_c], [bT[8]])
            self.tt("pool", T[8][:], T[8][:], pb["lnx_b"][:], ALU.add, [b_c], [bT[8]])
            self.tt("dve", h3(T[5][:]), h3(v_), bc8(stt_[:, 16:24]), ALU.mult, [z.b_xs, z.b_st], [bT[5]])
            self.tt("pool", T[8][:], T[8][:], T[5][:], ALU.add, [bT[5]], [bT[8]])
            self.tt("dve", z.rwb[:], T[8][:], T[6][:], ALU.mult, [bT[8], bT[6]], [z.b_rwb])
            self.dma("pool", S["rw"].ap()[i * 128:(i + 1) * 128, :], z.rwb[:], [z.b_rwb], [])

        def xstage(i, z):
            load(i, z)
            yield
            yield from rstage(i, z)
            X, T, bT = z.xs, z.T, z.bT
            r_ = X[:, 0:512]
            self.mm(z.F[0][:], tri[:], T[0][:], True, True, [b_c, bT[0]], [z.bF[0]])
            self.mm(z.F[1][:], ones[:], T[0][:], True, True, [b_c, bT[0]], [z.bF[1]])
            self.cp("act", T[9][:], z.F[1][:], [], [z.bF[1], bT[9]])
            self.cp("dve", T[7][:], z.F[0][:], [], [z.bF[0], bT[7]])
            yield
            self.act(T[8][:], T[7][:], AF.Exp, [bT[7]], [bT[8]])
            self.tt("pool", T[10][:], T[7][:], T[0][:], ALU.subtract, [bT[7], bT[0]], [bT[10]])
            yield
            self.tt("dve", z.fm[:, 0, :], r_, T[8][:], ALU.mult, [z.b_xs, bT[8]], [z.b_fm])
            self.act(T[8][:], T[7][:], AF.Exp, [bT[7]], [bT[8]], scale=-1.0)
            self.act(T[10][:], T[10][:], AF.Exp, [], [bT[10]])
            yield
            self.tt("dve", z.fm[:, 1, :], T[3][:], T[8][:], ALU.mult, [bT[3], bT[8]], [z.b_fm])
            self.tt("pool", z.fm[:, 2, :], T[4][:], T[8][:], ALU.mult, [bT[4], bT[8]], [z.b_fm])
            self.tt("dve", z.GR[:].rearrange("p h e -> p (h e)"), T[2][:], T[10][:], ALU.mult, [bT[2], bT[10]], [z.b_GR])
            yield
            self.tt("pool", T[10][:], T[9][:], T[7][:], ALU.subtract, [bT[9], bT[7]], [bT[10]])
            self.act(T[10][:], T[10][:], AF.Exp, [], [bT[10]])
            self.act(T[9][0:64, :], T[9][0:64, :], AF.Exp, [], [bT[9]])
            yield
            self.tt("dve", z.BK[:, 0, :], T[4][:], T[10][:], ALU.mult, [bT[4], bT[10]], [z.b_BK])
            self.tt("pool", z.BK[:, 1, :], T[3][:], T[10][:], ALU.mult, [bT[3], bT[10]], [z.b_BK])
            self.tt("dve", z.Dt[:], identf[0:64, 0:64].unsqueeze(1).to_broadcast([64, 8, 64]), T[9][0:64, :].rearrange("p (h e) -> p h e", e=64),
                    ALU.mult, [b_c, bT[9]], [z.b_Dt])
            yield
            for q in range(2):
                for hpi in range(2):
                    hp = 2 * q + hpi
                    srcs = [z.GR[:, 2 * hp:2 * hp + 2, :].rearrange("p h e -> p (h e)"), z.fm[:, 0, hp * 128:(hp + 1) * 128],
                            z.fm[:, 1, hp * 128:(hp + 1) * 128], z.fm[:, 2, hp * 128:(hp + 1) * 128]]
                    for w_, src in enumerate(srcs):
                        self.tr(z.H[:, hpi * 4 + w_, :], src, identb[:], [z.b_GR, z.b_fm, b_c], [z.bH], last=(hpi == 1 and w_ == 3))
                self.cp("dve" if q == 0 else "act", z.TA[:, 2 * q:2 * q + 2, :, :].rearrange("p a b m -> p (a b) m"), z.H[:], [], [z.bH, z.b_TA])
                yield
            TA, HB = z.TA, z.HB
            FX, FY, FZ = z.F
            wX, wY, wZ = z.bF
            MLS_ = M2[:, 0, :]
            for hp in range(4):
                p0, p1 = slice(0, 64), slice(64, 128)
                self.mm(FX[:, 0:256], TA[p0, hp, 3, :], TA[p0, hp, 0:2, :].rearrange("p a m -> p (a m)"), True, True, [z.b_TA], [wX], last=False)
                self.mm(FY[:, 0:256], TA[p1, hp, 3, :], TA[p1, hp, 0:2, :].rearrange("p a m -> p (a m)"), True, True, [z.b_TA], [wY], last=False)
                self.mm(FX[:, 256:384], TA[p0, hp, 2, :], TA[p0, hp, 1, :], True, True, [z.b_TA], [wX], last=False)
                self.mm(FY[:, 256:384], TA[p1, hp, 2, :], TA[p1, hp, 1, :], True, True, [z.b_TA], [wY], last=False)
                self.mm(FZ[:, 0:256], TA[p0, hp, 0, :], TA[p0, hp, 2:4, :].rearrange("p a m -> p (a m)"), True, True, [z.b_TA], [wZ], last=False)
                self.mm(FY[:, 384:512], TA[p1, hp, 0, :], TA[p1, hp, 3, :], True, True, [z.b_TA], [wY], last=False)
                self.mm(FX[:, 384:512], TA[p1, hp, 0, :], TA[p1, hp, 2, :], True, True, [z.b_TA], [wX], last=True)
                h0_, h1_ = 2 * hp, 2 * hp + 1
                m3 = M3[:].rearrange("p a m -> p (a m)")
                self.tt("dve", HB[:, h0_, 0:3, :].rearrange("p a m -> p (a m)"), FX[:, 0:384], m3, ALU.mult, [b_c], [wX, z.b_HB])
                self.tt("dve", HB[:, h0_, 3:5, :].rearrange("p a m -> p (a m)"), FZ[:, 0:256], M2[:].rearrange("p a m -> p (a m)"), ALU.mult, [b_c], [wZ, z.b_HB])
                self.tt("dve", HB[:, h1_, 0:3, :].rearrange("p a m -> p (a m)"), FY[:, 0:384], m3, ALU.mult, [b_c], [wY, z.b_HB])
                self.tt("dve", HB[:, h1_, 3, :], FX[:, 384:512], MLS_, ALU.mult, [b_c], [wX, z.b_HB])
                self.tt("dve", HB[:, h1_, 4, :], FY[:, 384:512], MLS_, ALU.mult, [b_c], [wY, z.b_HB])
                yield
            Tm, Pk, PkT = z.Tm, z.Pk, z.PkT
            self.tt("pool", Tm[:], identb[:].unsqueeze(1).to_broadcast([128, 8, 128]), HB[:, :, 0, :], ALU.subtract, [b_c, z.b_HB], [z.b_Tm])
            pk_r = lambda hh: HB[:, hh, 0, :]
            pkT_r = lambda hh: HB[:, hh, 4, :]
            bk_r, bkT_r = z.b_HB, z.b_HB
            for lv in range(NLV):
                pw = lv % 2
                lastlv = (lv == NLV - 1)
                for g in range(2):
                    fa, fb, fc = z.F
                    wa, wb, wc = z.bF
                    if not lastlv:
                        for h4 in range(4):
                            hh = 4 * g + h4
                            self.mm(fa[:, h4 * 128:(h4 + 1) * 128], pkT_r(hh), pk_r(hh), True, True, [bk_r, bkT_r], [wa], last=(h4 == 3))
                    for h4 in range(4):
                        hh = 4 * g + h4
                        self.mm(fb[:, h4 * 128:(h4 + 1) * 128], pk_r(hh), pkT_r(hh), True, True, [bk_r, bkT_r], [wb], last=(h4 == 3))
                    if not lastlv:
                        self.cp("act", Pk[pw][:, 4 * g:4 * g + 4, :].rearrange("p h m -> p (h m)"), fa[:], [], [wa, z.b_Pk[pw]])
                    self.cp("dve", PkT[pw][:, 4 * g:4 * g + 4, :].rearrange("p h m -> p (h m)"), fb[:], [], [wb, z.b_PkT[pw]])
                    yield
                    for h4 in range(4):
                        hh = 4 * g + h4
                        self.mm(fc[:, h4 * 128:(h4 + 1) * 128], PkT[pw][:, hh, :], Tm[:, hh, :], True, True, [z.b_PkT[pw], z.b_Tm], [wc], last=(h4 == 3))
                    self.tt("dve", Tm[:, 4 * g:4 * g + 4, :].rearrange("p h m -> p (h m)"), fc[:], Tm[:, 4 * g:4 * g + 4, :].rearrange("p h m -> p (h m)"),
                            ALU.add, [], [wc, z.b_Tm])
                    yield
                pk_r = (lambda hh, pw=pw: Pk[pw][:, hh, :])
                pkT_r = (lambda hh, pw=pw: PkT[pw][:, hh, :])
                bk_r, bkT_r = z.b_Pk[pw], z.b_PkT[pw]
            NG, GR, BK = z.NG, z.GR, z.BK
            for hq in range(4):
                f = z.F[hq % 2]
                w = z.bF[hq % 2]
                for h2 in range(2):
                    h = 2 * hq + h2
                    self.mm(f[:, h2 * 192:h2 * 192 + 64], Tm[:, h, :], GR[:, h, :], True, True, [z.b_Tm, z.b_GR], [w], last=False)
                    self.mm(f[:, h2 * 192 + 64:h2 * 192 + 192], Tm[:, h, :], HB[:, h, 3, :], True, True, [z.b_Tm, z.b_HB], [w], last=(h2 == 1))
                self.ts("dve", NG[:, 2 * hq:2 * hq + 2, :].rearrange("p h m -> p (h m)"), f[:, 0:384], -1.0, None, ALU.mult, None, [], [w, z.b_NG])
                yield
            for hq in range(4):
                fo, fl = z.F[(2 * hq + 2) % 3], z.F[(2 * hq + 3) % 3]
                wo, wl = z.bF[(2 * hq + 2) % 3], z.bF[(2 * hq + 3) % 3]
                for h2 in range(2):
                    h = 2 * hq + h2
                    pr = slice(64 * h2, 64 * h2 + 64)
                    c0 = h2 * 192
                    self.mm(fo[0:64, c0:c0 + 64], NG[:, h, 0:64], BK[:, 0, h * 64:(h + 1) * 64], True, True, [z.b_NG, z.b_BK], [wo], last=False)
                    self.mm(fo[0:64, c0 + 64:c0 + 192], NG[:, h, 0:64], HB[:, h, 1, :], True, False, [z.b_NG, z.b_HB], [wo], last=False)
                    self.mm(fo[0:64, c0 + 64:c0 + 192], identb[pr, pr], TA[pr, hq, 1, :], False, True, [b_c, z.b_TA], [wo], last=(h2 == 1))
                for h2 in range(2):
                    h = 2 * hq + h2
                    c0 = h2 * 192
                    self.mm(fl[:, c0:c0 + 64], NG[:, h, 64:192], BK[:, 0, h * 64:(h + 1) * 64], True, False, [z.b_NG, z.b_BK], [wl], last=False)
                    self.mm(fl[:, c0:c0 + 64], identb[:], BK[:, 1, h * 64:(h + 1) * 64], False, True, [b_c, z.b_BK], [wl], last=False)
                    self.mm(fl[:, c0 + 64:c0 + 192], NG[:, h, 64:192], HB[:, h, 1, :], True, False, [z.b_NG, z.b_HB], [wl], last=False)
                    self.mm(fl[:, c0 + 64:c0 + 192], identb[:], HB[:, h, 2, :], False, True, [b_c, z.b_HB], [wl], last=(h2 == 1))
                fo3 = fo[0:64, 0:384].rearrange("p (h m) -> p h m", m=192)
                self.tt("dve", z.PHI[:, 2 * hq:2 * hq + 2, :], fo3[:, :, 0:64], z.Dt[:, 2 * hq:2 * hq + 2, :], ALU.add, [z.b_Dt], [wo, z.b_PHI])
                self.cp("dve", z.OM[:, 2 * hq:2 * hq + 2, :], fo3[:, :, 64:192], [], [wo, z.b_OM])
                self.cp("act", z.PSL[:, 2 * hq:2 * hq + 2, :].rearrange("p h m -> p (h m)"), fl[:, 0:384], [], [wl, z.b_PSL])
                yield

        state = {"cur": 0}

        def ystage(i, z):
            cur = state["cur"]
            nxt = 1 - cur
            v_ = z.xs[:, 1024:1536]
            for h in range(8):
                self.mm(z.F[0][:, h * 64:(h + 1) * 64], z.OM[:, h, :], ST[cur][:, h, :], True, False, [z.b_OM, b_ST[cur]], [z.bF[0]], last=False)
                self.mm(z.F[0][:, h * 64:(h + 1) * 64], z.PSL[:, h, 64:192], v_[:, h * 64:(h + 1) * 64], False, True, [z.b_PSL, z.b_xs], [z.bF[0]], last=(h == 7))
            for h in range(8):
                self.mm(z.F[1][0:64, h * 64:(h + 1) * 64], z.PHI[:, h, :], ST[cur][:, h, :], True, False, [z.b_PHI, b_ST[cur]], [z.bF[1]], last=False)
                self.mm(z.F[1][0:64, h * 64:(h + 1) * 64], z.PSL[:, h, 0:64], v_[:, h * 64:(h + 1) * 64], False, True, [z.b_PSL, z.b_xs], [z.bF[1]], last=(h == 7))
            self.cp("dve", ST[nxt][:].rearrange("p h m -> p (h m)"), z.F[1][0:64, :], [], [z.bF[1], b_ST[nxt]])
            state["cur"] = nxt
            post(i, z, z.F[0][:], [], [z.bF[0]])

        with contextlib.ExitStack() as st1:
            sets = [mkset(st1, "A", 0), mkset(st1, "B", 1)]
            active = []
            nxt_i = 0
            free_sets = [sets[0], sets[1]]
            while nxt_i < NTP or active:
                while nxt_i < NTP and free_sets:
                    z = free_sets.pop(0)
                    active.append([nxt_i, xstage(nxt_i, z), z])
                    nxt_i += 1
                done_any = False
                for ent in list(active):
                    try:
                        next(ent[1])
                    except StopIteration:
                        assert ent is active[0]
                        ystage(ent[0], ent[2])
                        active.pop(0)
                        free_sets.append(ent[2])
                        done_any = True
                        break
            cur = state["cur"]
            for h in range(8):
                self.mm(sets[0].F[2][0:64, h * 64:(h + 1) * 64], ST[cur][:, h, :], identf[0:64, 0:64], True, True, [b_ST[cur], b_c], [sets[0].bF[2]], last=(h == 7))
            zz = sets[0]
            self.cp("dve", zz.PHI[:].rearrange("p h m -> p (h m)"), zz.F[2][0:64, :], [], [zz.bF[2], zz.b_PHI])
            self.dma("sp", bass.AP(O["wkvp"], 0, [[64, 64], [4096, 8], [1, 64]]), zz.PHI[:], [zz.b_PHI], [])
            self.barrier()
            self.P.emit_phase()

        with contextlib.ExitStack() as st2:
            z = mkset(st2, "S")
            sb = lambda name, shape, dt=F32: st2.enter_context(nc.sbuf_tensor(name, list(shape), dt))
            i = NTP
            Ssb = sb("r_Ssb", [128, 4096]); tmp = sb("r_tmp", [128, 4096])
            vec = sb("r_vec", [128, 6, 8, 64]); ysb = sb("r_ysb", [128, 8, 64]); ytm = sb("r_ytm", [128, 512]); sk = sb("r_sk", [128, 64])
            b_S, b_tmp, b_vec, b_ys, b_ytm, b_sk, b_rs, b_yscr = (Buf() for _ in range(8))
            self.dma("sp", Ssb[:], I["swkv"].ap(), [], [b_S])
            load(i, z)
            for _ in rstage(i, z):
                pass
            T, bT = z.T, z.bT
            self.act(T[0][:], T[0][:], AF.Exp, [], [bT[0]])
            X = z.xs
            srcs = [(X[:, 0:512], z.b_xs), (T[0][:], bT[0]), (T[3][:], bT[3]), (X[:, 1024:1536], z.b_xs), (T[2][:], bT[2]), (T[4][:], bT[4])]
            b_rsq = [Buf() for _ in range(6)]
            for q, (ap, bb) in enumerate(srcs):
                self.dma("sp", S["rs"].ap()[:, q * 512:(q + 1) * 512], ap, [bb], [b_rsq[q]])
            b_vecs = [Buf() for _ in range(96)]
            for q in range(6):
                for s in range(16):
                    self.dma("sp", vec[8 * s:8 * s + 8, q, :, :], bass.AP(S["rs"], s * 8 * 3072 + q * 512, [[64, 8], [3072, 8], [1, 64]]), [b_rsq[q]], [b_vecs[q * 16 + s]])
            self.P.op("dve", lambda e: e.memset(sk[:], 0.0), b_vecs, [b_vec, b_sk])
            S3 = Ssb[:].rearrange("p (i j) -> p i j", j=64)
            t3 = tmp[:].rearrange("p (i j) -> p i j", j=64)
            bi = lambda ap: ap.unsqueeze(1).to_broadcast([128, 64, 64])
            bj = lambda ap: ap.unsqueeze(2).to_broadcast([128, 64, 64])
            for t in range(8):
                r_t, w_t, k_t, v_t, kap_t, b_t = (vec[:, q, t, :] for q in range(6))
                self.tt("dve", t3, S3, bi(kap_t), ALU.mult, [b_S, b_vec], [b_tmp])
                self.red("dve", sk[:], t3, [b_tmp], [b_sk])
                self.tt("dve", S3, S3, bi(w_t), ALU.mult, [b_vec], [b_S])
                self.tt("dve", t3, bj(sk[:]), bi(b_t), ALU.mult, [b_sk, b_vec], [b_tmp])
                self.tt("dve", S3, S3, t3, ALU.subtract, [b_tmp], [b_S])
                self.tt("dve", t3, bj(v_t), bi(k_t), ALU.mult, [b_vec], [b_tmp])
                self.tt("dve", S3, S3, t3, ALU.add, [b_tmp], [b_S])
                self.tt("dve", t3, S3, bi(r_t), ALU.mult, [b_S, b_vec], [b_tmp])
                self.red("dve", ysb[:, t, :], t3, [b_tmp], [b_ys])
            self.dma("sp", O["wkvs"].ap(), Ssb[:], [b_S], [])
            b_yss = [Buf() for _ in range(16)]
            for s in range(16):
                self.dma("sp", bass.AP(S["ys"], s * 8 * 512, [[64, 8], [512, 8], [1, 64]]), ysb[8 * s:8 * s + 8, :, :], [b_ys], [b_yss[s]])
            self.dma("sp", ytm[:], S["ys"].ap(), b_yss, [b_ytm])
            post(i, z, ytm[:], [b_ytm], [])
            self.barrier()
            self.P.emit_phase()


KB.phase1b = _phase1b_v2
```
